# Optimizing a Trainium2 kernel written in Bass

```python
import math
import jax, jax.numpy as jnp
from jax import lax
import numpy as np

D_MODEL = 1024
BATCH = 4
SEQ = 4096
DEPTH = 2

GRID_W = 64
CTX_LEN = 256
Q_BLOCK = 128
ROPE_THETA = 10000.0
EPS = 1e-6
A_HEADS = 4
A_HEAD_DIM = 64
A_V_DIM = 2 * A_HEAD_DIM
B_HEADS = 8
B_Q_RANK = 256
B_KV_RANK = 128
B_NOPE_DIM = 64
B_ROPE_DIM = 32
B_V_DIM = 64
C_HEADS = 8
C_KV_HEADS = 2
C_HEAD_DIM = 64
D_HEADS = 8
D_HEAD_DIM = 64
NA_WIN_H = 8
NA_WIN_W = 16
FFN_HIDDEN = -(-8 * D_MODEL // (3 * 256)) * 256

AB_SPLITS = (A_HEADS * 2 * A_HEAD_DIM, A_HEADS * 2 * A_HEAD_DIM, A_HEADS * A_V_DIM, B_Q_RANK, B_KV_RANK, B_ROPE_DIM)
AB_IN = sum(AB_SPLITS)
AB_OUT = A_HEADS * A_V_DIM + B_HEADS * B_V_DIM
CD_SPLITS = (C_HEADS * C_HEAD_DIM, C_KV_HEADS * C_HEAD_DIM, C_KV_HEADS * C_HEAD_DIM, D_HEADS * D_HEAD_DIM, D_HEADS * D_HEAD_DIM, D_HEADS * D_HEAD_DIM)
CD_IN = sum(CD_SPLITS)
CD_OUT = C_HEADS * C_HEAD_DIM + D_HEADS * D_HEAD_DIM

kernel_name = 'hybrid_diffusion_backbone_diffmla_gqa_natten'


def split_points(sizes):
    return [int(s) for s in np.cumsum(sizes)[:-1]]


def rmsnorm(x, g):
    xf = x.astype(jnp.float32)
    y = xf * lax.rsqrt(jnp.mean(xf * xf, axis=-1, keepdims=True) + EPS)
    return (y * g.astype(jnp.float32)).astype(x.dtype)


def modulate(h, shift, scale):
    return h * (1 + scale[:, None, :]) + shift[:, None, :]


def ada_params(cond, w_mod, b_mod):
    return jnp.split(jax.nn.silu(cond) @ w_mod + b_mod, 6, axis=-1)


def softmax32(s):
    return jax.nn.softmax(s.astype(jnp.float32), axis=-1)


def axial_rope_tables(rot_dim, n_tokens):
    t = jnp.arange(n_tokens)
    rows = (t // GRID_W).astype(jnp.float32)
    cols = (t % GRID_W).astype(jnp.float32)
    axis_dim = rot_dim // 2
    inv = ROPE_THETA ** (-jnp.arange(0, axis_dim, 2, dtype=jnp.float32) / axis_dim)
    ang = jnp.concatenate([rows[:, None] * inv, cols[:, None] * inv], axis=-1)
    return jnp.cos(ang), jnp.sin(ang)


def apply_rope(x, cos, sin):
    extra = x.ndim - 3
    cos = cos.reshape(cos.shape[0], *([1] * extra), cos.shape[1])
    sin = sin.reshape(sin.shape[0], *([1] * extra), sin.shape[1])
    half = x.shape[-1] // 2
    x1 = x[..., :half].astype(jnp.float32)
    x2 = x[..., half:].astype(jnp.float32)
    return jnp.concatenate([x1 * cos - x2 * sin, x1 * sin + x2 * cos], axis=-1).astype(x.dtype)


def to_blocks(a):
    b, t = a.shape[:2]
    return jnp.moveaxis(a.reshape(b, t // Q_BLOCK, Q_BLOCK, *a.shape[2:]), 1, 0)


def from_blocks(a):
    nb, b, qb = a.shape[:3]
    return jnp.moveaxis(a, 0, 1).reshape(b, nb * qb, *a.shape[3:])


def sweep_query_blocks(fn, *qs):
    out = lax.map(lambda blk: fn(*blk), tuple(to_blocks(q) for q in qs))
    return from_blocks(out)


def diff_core(q1, q2, k1, k2, v, lam):
    scale = A_HEAD_DIM ** -0.5
    p1 = softmax32(jnp.einsum('bqhd,bkhd->bhqk', q1, k1) * scale)
    p2 = softmax32(jnp.einsum('bqhd,bkhd->bhqk', q2, k2) * scale)
    p = (p1 - lam * p2).astype(v.dtype)
    return jnp.einsum('bhqk,bkhd->bqhd', p, v)


def mla_core(qn, qr, kn, kr, v):
    scale = (B_NOPE_DIM + B_ROPE_DIM) ** -0.5
    s = jnp.einsum('bqhd,bkhd->bhqk', qn, kn) + jnp.einsum('bqhr,bkr->bhqk', qr, kr)
    p = softmax32(s * scale).astype(v.dtype)
    return jnp.einsum('bhqk,bkhd->bqhd', p, v)


def gqa_core(q, k, v):
    b, nq, h, d = q.shape
    g = k.shape[2]
    qg = q.reshape(b, nq, g, h // g, d)
    s = jnp.einsum('bqgrd,bkgd->bgrqk', qg, k) * d ** -0.5
    p = softmax32(s).astype(v.dtype)
    return jnp.einsum('bgrqk,bkgd->bqgrd', p, v).reshape(b, nq, h, d)


def neighbourhood_tables(n_tokens):
    rows_n = n_tokens // GRID_W
    wh = min(NA_WIN_H, rows_n)
    ww = NA_WIN_W
    t = jnp.arange(n_tokens)
    r = t // GRID_W
    col = t % GRID_W
    rs = jnp.clip(r - wh // 2, 0, rows_n - wh)
    cs = jnp.clip(col - ww // 2, 0, GRID_W - ww)
    krow = rs[:, None, None] + jnp.arange(wh)[None, :, None]
    kcol = cs[:, None, None] + jnp.arange(ww)[None, None, :]
    shape = (n_tokens, wh, ww)
    idx = jnp.broadcast_to(krow * GRID_W + kcol, shape).reshape(n_tokens, wh * ww)
    rel_r = jnp.broadcast_to(krow - r[:, None, None] + NA_WIN_H - 1, shape).reshape(n_tokens, wh * ww)
    rel_c = jnp.broadcast_to(kcol - col[:, None, None] + NA_WIN_W - 1, shape).reshape(n_tokens, wh * ww)
    return idx, rel_r, rel_c


def neighbourhood_attn(q, k, v, k_ctx, v_ctx, rpb):
    n_tokens = q.shape[1]
    idx, rel_r, rel_c = neighbourhood_tables(n_tokens)
    win = idx.shape[-1]
    nb = n_tokens // Q_BLOCK
    scale = D_HEAD_DIM ** -0.5

    def blk(args):
        qb, idxb, rrb, rcb = args
        kb = jnp.take(k, idxb, axis=1)
        vb = jnp.take(v, idxb, axis=1)
        bias = rpb[:, rrb, rcb]
        s_loc = jnp.einsum('bqhd,bqwhd->bhqw', qb, kb) * scale + bias[None]
        s_ctx = jnp.einsum('bqhd,bkhd->bhqk', qb, k_ctx) * scale
        p = softmax32(jnp.concatenate([s_loc, s_ctx], axis=-1)).astype(v.dtype)
        return (jnp.einsum('bhqw,bqwhd->bqhd', p[..., :win], vb)
                + jnp.einsum('bhqk,bkhd->bqhd', p[..., win:], v_ctx))

    out = lax.map(blk, (to_blocks(q), idx.reshape(nb, Q_BLOCK, win),
                        rel_r.reshape(nb, Q_BLOCK, win), rel_c.reshape(nb, Q_BLOCK, win)))
    return from_blocks(out)


def ffn_sublayer(x, g_norm, shift, scale, gate, w1, w3, w2):
    h = modulate(rmsnorm(x, g_norm), shift, scale)
    return x + gate[:, None, :] * ((jax.nn.silu(h @ w1) * (h @ w3)) @ w2)


def ab_layer(x, xc, c, c_ctx, w_mod, b_mod, g_attn, w_in, lam_q1, lam_k1, lam_q2, lam_k2, g_subln,
             g_cq, w_uq, g_ckv, w_ukv, w_out, g_ffn, w1, w3, w2, layer_idx, last):
    n_tokens = x.shape[1]
    cos_a, sin_a = axial_rope_tables(A_HEAD_DIM, n_tokens)
    cos_b, sin_b = axial_rope_tables(B_ROPE_DIM, n_tokens)
    sh_a, sc_a, gt_a, sh_f, sc_f, gt_f = ada_params(c, w_mod, b_mod)
    csh_a, csc_a, cgt_a, csh_f, csc_f, cgt_f = ada_params(c_ctx[None], w_mod, b_mod)

    def project(h, rotary):
        bp, tp, _ = h.shape
        qa, ka, va, cq, ckv, kr = jnp.split(h @ w_in, split_points(AB_SPLITS), axis=-1)
        qa = qa.reshape(bp, tp, A_HEADS, 2, A_HEAD_DIM)
        ka = ka.reshape(bp, tp, A_HEADS, 2, A_HEAD_DIM)
        va = va.reshape(bp, tp, A_HEADS, A_V_DIM)
        q = (rmsnorm(cq, g_cq) @ w_uq).reshape(bp, tp, B_HEADS, B_NOPE_DIM + B_ROPE_DIM)
        kv = (rmsnorm(ckv, g_ckv) @ w_ukv).reshape(bp, tp, B_HEADS, B_NOPE_DIM + B_V_DIM)
        qn, qr = q[..., :B_NOPE_DIM], q[..., B_NOPE_DIM:]
        kn, vb = kv[..., :B_NOPE_DIM], kv[..., B_NOPE_DIM:]
        if rotary:
            qa = apply_rope(qa, cos_a, sin_a)
            ka = apply_rope(ka, cos_a, sin_a)
            qr = apply_rope(qr, cos_b, sin_b)
            kr = apply_rope(kr, cos_b, sin_b)
        return qa, ka, va, qn, qr, kn, kr, vb

    qa, ka, va, qn, qr, kn, kr, vb = project(modulate(rmsnorm(x, g_attn), sh_a, sc_a), True)
    qac, kac, vac, qnc, qrc, knc, krc, vbc = project(modulate(rmsnorm(xc, g_attn), csh_a, csc_a), False)

    lambda_init = 0.8 - 0.6 * math.exp(-0.3 * layer_idx)
    lam = (jnp.exp(jnp.sum(lam_q1.astype(jnp.float32) * lam_k1.astype(jnp.float32)))
           - jnp.exp(jnp.sum(lam_q2.astype(jnp.float32) * lam_k2.astype(jnp.float32))) + lambda_init)

    k1_all = jnp.concatenate([kac[..., 0, :], ka[..., 0, :]], axis=1)
    k2_all = jnp.concatenate([kac[..., 1, :], ka[..., 1, :]], axis=1)
    va_all = jnp.concatenate([vac, va], axis=1)
    kn_all = jnp.concatenate([knc, kn], axis=1)
    kr_all = jnp.concatenate([krc, kr], axis=1)
    vb_all = jnp.concatenate([vbc, vb], axis=1)

    def merge(oa, ob):
        bo, to = oa.shape[:2]
        oa = rmsnorm(oa, g_subln) * (1.0 - lambda_init)
        return jnp.concatenate([oa.reshape(bo, to, -1), ob.reshape(bo, to, -1)], axis=-1) @ w_out

    o_a = sweep_query_blocks(lambda q: diff_core(q[..., 0, :], q[..., 1, :], k1_all, k2_all, va_all, lam), qa)
    o_b = sweep_query_blocks(lambda q_n, q_r: mla_core(q_n, q_r, kn_all, kr_all, vb_all), qn, qr)
    x = x + gt_a[:, None, :] * merge(o_a, o_b)
    x = ffn_sublayer(x, g_ffn, sh_f, sc_f, gt_f, w1, w3, w2)
    if last:
        return x, None
    o_ac = diff_core(qac[..., 0, :], qac[..., 1, :], kac[..., 0, :], kac[..., 1, :], vac, lam)
    o_bc = mla_core(qnc, qrc, knc, krc, vbc)
    xc = xc + cgt_a[:, None, :] * merge(o_ac, o_bc)
    xc = ffn_sublayer(xc, g_ffn, csh_f, csc_f, cgt_f, w1, w3, w2)
    return x, xc


def cd_layer(x, xc, c, c_ctx, w_mod, b_mod, g_attn, w_in, g_qc, g_kc, rpb, w_out, g_ffn, w1, w3, w2, last):
    n_tokens = x.shape[1]
    cos_c, sin_c = axial_rope_tables(C_HEAD_DIM, n_tokens)
    sh_a, sc_a, gt_a, sh_f, sc_f, gt_f = ada_params(c, w_mod, b_mod)
    csh_a, csc_a, cgt_a, csh_f, csc_f, cgt_f = ada_params(c_ctx[None], w_mod, b_mod)

    def project(h, rotary):
        bp, tp, _ = h.shape
        qc, kc, vc, qd, kd, vd = jnp.split(h @ w_in, split_points(CD_SPLITS), axis=-1)
        qc = rmsnorm(qc.reshape(bp, tp, C_HEADS, C_HEAD_DIM), g_qc)
        kc = rmsnorm(kc.reshape(bp, tp, C_KV_HEADS, C_HEAD_DIM), g_kc)
        vc = vc.reshape(bp, tp, C_KV_HEADS, C_HEAD_DIM)
        qd = qd.reshape(bp, tp, D_HEADS, D_HEAD_DIM)
        kd = kd.reshape(bp, tp, D_HEADS, D_HEAD_DIM)
        vd = vd.reshape(bp, tp, D_HEADS, D_HEAD_DIM)
        if rotary:
            qc = apply_rope(qc, cos_c, sin_c)
            kc = apply_rope(kc, cos_c, sin_c)
        return qc, kc, vc, qd, kd, vd

    qc, kc, vc, qd, kd, vd = project(modulate(rmsnorm(x, g_attn), sh_a, sc_a), True)
    qcc, kcc, vcc, qdc, kdc, vdc = project(modulate(rmsnorm(xc, g_attn), csh_a, csc_a), False)

    kc_all = jnp.concatenate([kcc, kc], axis=1)
    vc_all = jnp.concatenate([vcc, vc], axis=1)

    def merge(oc, od):
        bo, to = oc.shape[:2]
        return jnp.concatenate([oc.reshape(bo, to, -1), od.reshape(bo, to, -1)], axis=-1) @ w_out

    o_c = sweep_query_blocks(lambda q: gqa_core(q, kc_all, vc_all), qc)
    o_d = neighbourhood_attn(qd, kd, vd, kdc, vdc, rpb)
    x = x + gt_a[:, None, :] * merge(o_c, o_d)
    x = ffn_sublayer(x, g_ffn, sh_f, sc_f, gt_f, w1, w3, w2)
    if last:
        return x, None
    o_cc = gqa_core(qcc, kcc, vcc)
    o_dc = gqa_core(qdc, kdc, vdc)
    xc = xc + cgt_a[:, None, :] * merge(o_cc, o_dc)
    xc = ffn_sublayer(xc, g_ffn, csh_f, csc_f, cgt_f, w1, w3, w2)
    return x, xc


def setup_inputs(seed: int = 0) -> dict:
    key = jax.random.key(seed)
    ks = iter(jax.random.split(key, 40))

    def nrm(shape, scale=1.0):
        return scale * jax.random.normal(next(ks), shape, jnp.float32)

    def gain(n):
        return 1.0 + 0.1 * nrm((n,))

    d = D_MODEL
    mod_scale = 0.5 * d ** -0.5
    inp = {}
    inp['x'] = nrm((BATCH, SEQ, d))
    inp['c'] = nrm((BATCH, d))
    inp['ctx'] = nrm((BATCH, CTX_LEN, d))
    inp['c_ctx'] = nrm((d,))
    inp['l0_w_mod'] = nrm((d, 6 * d), mod_scale)
    inp['l0_b_mod'] = nrm((6 * d,), 0.02)
    inp['l0_g_attn'] = gain(d)
    inp['l0_w_in'] = nrm((d, AB_IN), d ** -0.5)
    inp['l0_lam_q1'] = nrm((A_HEAD_DIM,), 0.1)
    inp['l0_lam_k1'] = nrm((A_HEAD_DIM,), 0.1)
    inp['l0_lam_q2'] = nrm((A_HEAD_DIM,), 0.1)
    inp['l0_lam_k2'] = nrm((A_HEAD_DIM,), 0.1)
    inp['l0_g_subln'] = gain(A_V_DIM)
    inp['l0_g_cq'] = gain(B_Q_RANK)
    inp['l0_w_uq'] = nrm((B_Q_RANK, B_HEADS * (B_NOPE_DIM + B_ROPE_DIM)), B_Q_RANK ** -0.5)
    inp['l0_g_ckv'] = gain(B_KV_RANK)
    inp['l0_w_ukv'] = nrm((B_KV_RANK, B_HEADS * (B_NOPE_DIM + B_V_DIM)), B_KV_RANK ** -0.5)
    inp['l0_w_out'] = nrm((AB_OUT, d), AB_OUT ** -0.5)
    inp['l0_g_ffn'] = gain(d)
    inp['l0_w1'] = nrm((d, FFN_HIDDEN), d ** -0.5)
    inp['l0_w3'] = nrm((d, FFN_HIDDEN), d ** -0.5)
    inp['l0_w2'] = nrm((FFN_HIDDEN, d), FFN_HIDDEN ** -0.5)
    inp['l1_w_mod'] = nrm((d, 6 * d), mod_scale)
    inp['l1_b_mod'] = nrm((6 * d,), 0.02)
    inp['l1_g_attn'] = gain(d)
    inp['l1_w_in'] = nrm((d, CD_IN), d ** -0.5)
    inp['l1_g_qc'] = gain(C_HEAD_DIM)
    inp['l1_g_kc'] = gain(C_HEAD_DIM)
    inp['l1_rpb'] = nrm((D_HEADS, 2 * NA_WIN_H - 1, 2 * NA_WIN_W - 1), 0.1)
    inp['l1_w_out'] = nrm((CD_OUT, d), CD_OUT ** -0.5)
    inp['l1_g_ffn'] = gain(d)
    inp['l1_w1'] = nrm((d, FFN_HIDDEN), d ** -0.5)
    inp['l1_w3'] = nrm((d, FFN_HIDDEN), d ** -0.5)
    inp['l1_w2'] = nrm((FFN_HIDDEN, d), FFN_HIDDEN ** -0.5)
    inp['g_final'] = gain(d)
    return inp


def reference(x, c, ctx, c_ctx,
              l0_w_mod, l0_b_mod, l0_g_attn, l0_w_in, l0_lam_q1, l0_lam_k1, l0_lam_q2, l0_lam_k2,
              l0_g_subln, l0_g_cq, l0_w_uq, l0_g_ckv, l0_w_ukv, l0_w_out, l0_g_ffn, l0_w1, l0_w3, l0_w2,
              l1_w_mod, l1_b_mod, l1_g_attn, l1_w_in, l1_g_qc, l1_g_kc, l1_rpb, l1_w_out, l1_g_ffn,
              l1_w1, l1_w3, l1_w2, g_final):
    layers = (
        (l0_w_mod, l0_b_mod, l0_g_attn, l0_w_in, l0_lam_q1, l0_lam_k1, l0_lam_q2, l0_lam_k2, l0_g_subln,
         l0_g_cq, l0_w_uq, l0_g_ckv, l0_w_ukv, l0_w_out, l0_g_ffn, l0_w1, l0_w3, l0_w2),
        (l1_w_mod, l1_b_mod, l1_g_attn, l1_w_in, l1_g_qc, l1_g_kc, l1_rpb, l1_w_out, l1_g_ffn,
         l1_w1, l1_w3, l1_w2),
    )
    xc = ctx
    for i in range(DEPTH):
        last = i == DEPTH - 1
        if i % 2 == 0:
            x, xc = ab_layer(x, xc, c, c_ctx, *layers[i], layer_idx=i, last=last)
        else:
            x, xc = cd_layer(x, xc, c, c_ctx, *layers[i], last=last)
    return rmsnorm(x, g_final)
```

```python
import math
from contextlib import ExitStack
import numpy as np
import concourse.bass as bass
import concourse.mybir as mybir
from concourse.bass_utils import run_bass_kernel_spmd

F32 = mybir.dt.float32
BF16 = mybir.dt.bfloat16
AF = mybir.ActivationFunctionType
ALU = mybir.AluOpType

ENGS = ("pe", "act", "dve", "pool", "sp")


class Res:
    __slots__ = ("name", "w", "r")

    def __init__(self, name):
        self.name = name
        self.w = None
        self.r = {}


class Rec:
    __slots__ = ("eng", "fn", "deps", "inc", "val", "dma", "semkey", "idx")

    def __init__(self, eng, fn, dma=False, semkey=None):
        self.eng = eng
        self.fn = fn
        self.deps = []
        self.inc = False
        self.val = None
        self.dma = dma
        self.semkey = semkey
        self.idx = None


class Sched:
    def __init__(self, nc):
        self.nc = nc
        self.q = {e: [] for e in ENGS}
        self.n = 0
        self.pending = {e: [] for e in ENGS}
        self.last = {e: None for e in ENGS}
        self.open_dmas = []

    def res(self, name):
        return Res(name)

    def barrier(self):
        toks = [r for r in self.last.values() if r is not None] + list(self.open_dmas)
        self.open_dmas = []
        for e in ENGS:
            self.pending[e] = list(toks)

    def _add(self, rec, reads, writes):
        eng = rec.eng
        deps = []
        for r in reads:
            if r.w is not None:
                deps.append(r.w)
        for w in writes:
            if w.w is not None:
                deps.append(w.w)
            for e2, rr in w.r.items():
                if e2 == eng and not rr.dma:
                    continue
                deps.append(rr)
        if self.pending[eng]:
            deps = deps + [d for d in self.pending[eng] if d.dma or d.eng != eng]
            self.pending[eng] = []
        seen = set()
        for d in deps:
            if d is rec or id(d) in seen:
                continue
            if d.eng == eng == "pe" and not d.dma:
                continue
            if rec.dma and d.dma and d.semkey == rec.semkey:
                continue
            seen.add(id(d))
            rec.deps.append(d)
            d.inc = True
        for r in reads:
            r.r[("dma" + str(self.n)) if rec.dma else eng] = rec
        for w in writes:
            w.w = rec
            w.r = {}
        rec.idx = self.n
        self.n += 1
        self.q[eng].append(rec)
        if rec.dma:
            self.open_dmas.append(rec)
        else:
            self.last[eng] = rec
        return rec

    def op(self, eng, fn, reads=(), writes=()):
        return self._add(Rec(eng, fn), reads, writes)

    def dma(self, eng, out, in_, reads=(), writes=(), semkey=None, **kw):
        assert semkey is not None

        def fn(e, out=out, in_=in_, kw=kw):
            return e.dma_start(out=out, in_=in_, **kw)
        rec = Rec(eng, fn, dma=True, semkey=semkey)
        rec.inc = True
        return self._add(rec, reads, writes)

    def finalize(self, final_waits=()):
        import bisect
        nc = self.nc
        dmakeys = []
        for e in ENGS:
            c = 0
            for rec in self.q[e]:
                if rec.dma:
                    if rec.semkey not in dmakeys:
                        dmakeys.append(rec.semkey)
                elif rec.inc:
                    c += 1
                    rec.val = c
        dcount = {k: 0 for k in dmakeys}
        keyq = {}
        allrecs = sorted([r for e in ENGS for r in self.q[e] if r.dma], key=lambda r: r.idx)
        klist = {}
        for rec in allrecs:
            assert keyq.setdefault(rec.semkey, rec.eng) == rec.eng
            dcount[rec.semkey] += 16
            rec.val = dcount[rec.semkey]
            klist.setdefault(rec.semkey, []).append(rec)
        kidx = {k: [r.idx for r in v] for k, v in klist.items()}
        self.nsem = len(dmakeys) + len(ENGS)
        with ExitStack() as st:
            esem = {e: st.enter_context(nc.semaphore("s_" + e)) for e in ENGS}
            dsem = {k: st.enter_context(nc.semaphore("d_%d" % i)) for i, k in enumerate(dmakeys)}
            block = st.enter_context(nc.Block())

            def semof(rec):
                return dsem[rec.semkey] if rec.dma else esem[rec.eng]

            def valof(d, consumer_idx):
                if not d.dma:
                    return d.val
                i = bisect.bisect_left(kidx[d.semkey], consumer_idx) - 1
                return max(d.val, klist[d.semkey][i].val if i >= 0 else 0)

            def replay(eng_name, e):
                seen = {}
                for rec in self.q[eng_name]:
                    need = {}
                    for d in rec.deps:
                        s = semof(d)
                        k = id(s)
                        v = valof(d, rec.idx)
                        if seen.get(k, 0) >= v:
                            continue
                        if k not in need or need[k][1] < v:
                            need[k] = (s, v)
                    for k, (s, v) in need.items():
                        e.wait_ge(s, v)
                        seen[k] = v
                    ins = rec.fn(e)
                    if rec.dma:
                        ins.then_inc(dsem[rec.semkey], 16)
                    elif rec.inc:
                        ins.then_inc(esem[eng_name], 1)
                if eng_name == "sp":
                    for d in final_waits:
                        e.wait_ge(semof(d), valof(d, 1 << 60))

            @block.tensor
            def _(e):
                replay("pe", e)

            @block.scalar
            def _(e):
                replay("act", e)

            @block.vector
            def _(e):
                replay("dve", e)

            @block.gpsimd
            def _(e):
                replay("pool", e)

            @block.sync
            def _(e):
                replay("sp", e)


D = 1024
SEQ = 4096
NCTX = 256
TOWN = 2048
NOWN = TOWN + NCTX
NKEY = SEQ + NCTX
NKB = NKEY // 128
GRID_W = 64
EPS = 1e-6
FFN = 2816
NJ = FFN // 128
LAMBDA_INIT0 = 0.8 - 0.6 * math.exp(-0.3 * 0)

CL = {}
_off = 0
for _n, _w in [("cfm", 16), ("bmod0", 48), ("bmod1", 48), ("gattn0", 8), ("gffn0", 8), ("gattn1", 8),
               ("gffn1", 8), ("gfinal", 8), ("gsub", 1), ("gcq", 2), ("gckv", 1), ("lamv", 256),
               ("gqc", 1), ("gqcs", 1), ("gkc", 1), ("gkcs", 1)]:
    CL[_n] = (_off, _w)
    _off += _w
NCONST = _off

OWN_CHUNKS = [(0, 512, 0), (512, 512, 0), (1024, 512, 0), (1536, 512, 0), (2048, 256, 1)]


def _swap_halves(cols, group):
    c = np.asarray(cols).reshape(-1, 2, group // 2)
    return c[:, ::-1, :].reshape(-1)


class Prog:
    def __init__(self, mode):
        self.mode = mode
        self.nc = bass.Bass("TRN2", target_bir_lowering=False)
        self.S = Sched(self.nc)
        self.bank_rr = 0
        self.tmp_rr = 0

    def din(self, name, shape, dt=F32):
        return self.nc.dram_tensor(name, list(shape), dt, kind="ExternalInput").ap()

    def dout(self, name, shape, dt=F32):
        return self.nc.dram_tensor(name, list(shape), dt, kind="ExternalOutput").ap()

    def dscr(self, name, shape, dt=BF16):
        return self.nc.dram_tensor(name, list(shape), dt).ap()

    def sb(self, st, name, shape, dt):
        if st is not None and st is self.g:
            return st.enter_context(self.nc.sbuf_tensor(name, list(shape), dt))
        return self.aalloc(name, shape, dt)

    def areset(self, lo=0, hi=None):
        self.a_lo = lo
        self.a_hi = self.a_words if hi is None else hi

    def aalloc(self, name, shape, dt, top=False):
        n = 1
        for d_ in shape[1:]:
            n *= d_
        w = n if dt == F32 else (n + 1) // 2
        w = (w + 7) // 8 * 8
        if top:
            self.a_hi -= w
            off = self.a_hi
        else:
            off = self.a_lo
            self.a_lo += w
        assert self.a_lo <= self.a_hi, "arena overflow %s: lo=%d hi=%d" % (name, self.a_lo, self.a_hi)
        v = self.arena[:, off:off + w]
        if dt != F32:
            v = v.bitcast(dt)
        v = v[:, 0:n]
        if len(shape) == 3:
            v = v.rearrange("p (a b) -> p a b", a=shape[1])
        if shape[0] != 128:
            v = v[0:shape[0]]
        return v

    def bank(self):
        b = self.bank_rr
        self.bank_rr = (self.bank_rr + 1) % 5
        return b

    def mm(self, out, lhsT, rhs, start, stop, reads, writes):
        self.S.op("pe", lambda e: e.matmul(out, lhsT=lhsT, rhs=rhs, start=start, stop=stop), reads, writes)

    def act(self, out, in_, func, reads, writes, scale=None, bias=None):
        kw = {}
        if scale is not None:
            kw["scale"] = scale
        if bias is not None:
            kw["bias"] = bias
        self.S.op("act", lambda e: e.activation(out=out, in_=in_, func=func, **kw), reads, writes)

    def tt(self, out, in0, in1, op, reads, writes, eng="dve"):
        self.S.op(eng, lambda e: e.tensor_tensor(out=out, in0=in0, in1=in1, op=op), reads, writes)

    def stt(self, out, in0, scalar, in1, op0, op1, reads, writes):
        self.S.op("dve", lambda e: e.scalar_tensor_tensor(out=out, in0=in0, scalar=scalar, in1=in1, op0=op0, op1=op1),
                  reads, writes)

    def ts(self, out, in0, s1, s2, op0, op1, reads, writes, eng="dve"):
        if op1 is None:
            self.S.op(eng, lambda e: e.tensor_scalar(out=out, in0=in0, scalar1=s1, scalar2=None, op0=op0), reads, writes)
        else:
            self.S.op(eng, lambda e: e.tensor_scalar(out=out, in0=in0, scalar1=s1, scalar2=s2, op0=op0, op1=op1),
                      reads, writes)

    def copy(self, out, in_, reads, writes, eng="dve"):
        if eng == "act":
            self.S.op("act", lambda e: e.copy(out=out, in_=in_), reads, writes)
        else:
            self.S.op(eng, lambda e: e.tensor_copy(out=out, in_=in_), reads, writes)

    def memset(self, ap, val, writes, eng="dve"):
        self.S.op(eng, lambda e: e.memset(ap, val), (), writes)

    def build(self):
        nc, S, mode = self.nc, self.S, self.mode
        do0 = mode in ("l0", "fused")
        do1 = mode in ("l1", "fused")
        self.xT_own = self.din("xT_own", [D, TOWN])
        self.xT_all = self.din("xT_all", [D, SEQ])
        self.ctxT = self.din("ctxT", [D, NCTX])
        self.consts_d = self.din("consts", [128, NCONST])
        I = {}
        if do0:
            for n, shp in [("wmod0", [D, 6 * D]), ("wq0", [D, 1280]), ("wk0", [D, 1344]), ("wv0", [D, 512]),
                           ("wuq", [256, 1536]), ("wkn", [128, 512]), ("wvb", [128, 512]), ("wout0", [D, D]),
                           ("w1_0", [D, FFN]), ("w3_0", [D, FFN]), ("w2_0", [FFN, D]),
                           ("c64_own", [128, NOWN]), ("s64_own", [128, NOWN]), ("c64_all", [128, NKEY]),
                           ("s64_all", [128, NKEY]), ("c96_own", [96, NOWN]), ("s96_own", [96, NOWN]),
                           ("c96_all", [96, NKEY]), ("s96_all", [96, NKEY])]:
                I[n] = self.din(n, shp)
        if do1:
            for n, shp in [("wmod1", [D, 6 * D]), ("wq1", [D, 1536]), ("wk1", [D, 768]), ("wv1", [D, 640]),
                           ("wout1", [D, D]), ("w1_1", [D, FFN]), ("w3_1", [D, FFN]), ("w2_1", [FFN, D]),
                           ("nab", [5, 128, 8 * 6 * 128]), ("c64_own", [128, NOWN]), ("s64_own", [128, NOWN]),
                           ("c64_all", [128, NKEY]), ("s64_all", [128, NKEY])]:
                if n not in I:
                    I[n] = self.din(n, shp)
        self.I = I
        if mode == "l0":
            self.y_x = self.dout("y_x", [D, TOWN])
            self.y_c = self.dout("y_c", [D, NCTX])
        else:
            self.y_x = self.dout("y_x", [D, TOWN])
        self.qT_a = self.dscr("qT_a", [4, 128, NOWN])
        self.kT_a = self.dscr("kT_a", [4, 128, NKEY])
        self.v_a = self.dscr("v_a", [4, 128, NKB, 128])
        self.qT_b = self.dscr("qT_b", [8, 128, NOWN])
        self.kT_b = self.dscr("kT_b", [8, 128, NKEY])
        self.v_b = self.dscr("v_b", [8, 128, NKB, 64])
        self.qT_c = self.dscr("qT_c", [4, 128, TOWN])
        self.kT_c = self.dscr("kT_c", [128, NKEY])
        self.v_c = self.dscr("v_c", [2, 128, NKB, 64])
        self.qT_d = self.dscr("qT_d", [4, 128, TOWN])
        self.kT_d = self.dscr("kT_d", [4, 128, 22 * 128])
        self.v_d = self.dscr("v_d", [8, 128, 22, 64])

        with ExitStack() as g:
            self.g = g
            self.a_words = 32000
            self.arena = g.enter_context(nc.sbuf_tensor("arena", [128, self.a_words], F32))
            self.areset()
            self.xT = self.sb(g, "xT", [128, 8, NOWN], F32)
            self.r_x = [S.res("x%d" % i) for i in range(5)]
            self.cst = self.sb(g, "cst", [128, NCONST], F32)
            self.r_cst = S.res("cst")
            self.ones_bf = self.sb(g, "ones_bf", [128, 128], BF16)
            self.eps_t = self.sb(g, "eps_t", [128, 1], F32)
            self.csil = self.sb(g, "csil", [128, 8, 2], BF16)
            self.ada = [self.sb(g, "ada%d" % l, [128, 48, 2], F32) for l in range(2)]
            self.A1 = [self.sb(g, "A1_%d" % l, [128, 8, 2], F32) for l in range(2)]
            self.A2 = [self.sb(g, "A2_%d" % l, [128, 8, 2], F32) for l in range(2)]
            self.small = self.sb(g, "small", [128, 16], F32)
            self.r_small = S.res("small")
            self.r_ada = S.res("ada")
            self.ps = g.enter_context(nc.psum_tensor("ps", [128, 8, 512], F32))
            self.r_ps = [S.res("ps%d" % i) for i in range(8)]

            for k in range(8):
                S.dma("sp", self.xT[:, k, 0:TOWN], self.xT_own[k * 128:(k + 1) * 128, :], writes=self.r_x[0:4], semkey="xin")
                S.dma("sp", self.xT[:, k, TOWN:NOWN], self.ctxT[k * 128:(k + 1) * 128, :], writes=[self.r_x[4]], semkey="xin")
            S.dma("sp", self.cst[:], self.consts_d, writes=[self.r_cst], semkey="cst")
            self.memset(self.ones_bf[:], 1.0, [self.r_small])
            self.memset(self.eps_t[:], EPS, [self.r_small])
            c0 = CL["cfm"][0]
            self.act(self.csil[:].rearrange("p k v -> p (k v)"), self.cst[:, c0:c0 + 16], AF.Silu, [self.r_cst], [self.r_small])

            if do0:
                self.ada_params(0)
                self.layer0()
            if do1:
                self.ada_params(1)
                self.layer1()
            fins = []
            if mode == "l0":
                S.barrier()
                for k in range(8):
                    fins.append(S.dma("sp", self.y_x[k * 128:(k + 1) * 128, :], self.xT[:, k, 0:TOWN], reads=self.r_x, semkey="yout"))
                    fins.append(S.dma("sp", self.y_c[k * 128:(k + 1) * 128, :], self.xT[:, k, TOWN:NOWN], reads=self.r_x, semkey="yout"))
            else:
                fins = self.final_norm_out()
            S.finalize(final_waits=fins)
        return nc

    def cs(self, name, j=None):
        o, w = CL[name]
        if j is None:
            return self.cst[:, o:o + w]
        return self.cst[:, o + j:o + j + 1]

    def ada_params(self, l):
        nc, S = self.nc, self.S
        wmod = self.I["wmod%d" % l].rearrange("(k p) c -> p k c", p=128)
        S.barrier()
        self.areset()
        with ExitStack() as st:
            wt = [self.sb(st, "wm%d" % i, [128, 8, 512], BF16) for i in range(2)]
            r_wt = [S.res("wm%d" % i) for i in range(2)]
            pb = 7
            for cc in range(12):
                b = cc % 2
                S.dma("pool", wt[b][:], wmod[:, :, cc * 512:(cc + 1) * 512], writes=[r_wt[b]], semkey="wm%d" % b)
                for jj in range(4):
                    j = cc * 4 + jj
                    for k in range(8):
                        self.mm(self.ps[:, pb, j * 2:j * 2 + 2], wt[b][:, k, jj * 128:(jj + 1) * 128], self.csil[:, k, :],
                                k == 0, k == 7, [r_wt[b], self.r_small], [self.r_ps[pb]])
            bm = self.cs("bmod%d" % l)
            pv = self.ps[:, pb, 0:96].rearrange("p (j v) -> p j v", v=2)
            for v in range(2):
                self.tt(self.ada[l][:, :, v], pv[:, :, v], bm, ALU.add, [self.r_ps[pb], self.r_cst], [self.r_ada])
            for v in range(2):
                self.stt(self.A1[l][:, :, v], self.ada[l][:, 8:16, v], 1.0, self.cs("gattn%d" % l), ALU.add, ALU.mult,
                         [self.r_ada, self.r_cst], [self.r_ada])
                self.stt(self.A2[l][:, :, v], self.ada[l][:, 32:40, v], 1.0, self.cs("gffn%d" % l), ALU.add, ALU.mult,
                         [self.r_ada, self.r_cst], [self.r_ada])

    def norm_chunk(self, src, r_src, T, A, sh, v, hT, r_hT, W):
        S = self.S
        pb = self.bank()
        for k in range(8):
            sq = W["sq"][k % 2]
            self.act(sq[:, :T], src(k), AF.Square, [r_src], [W["r_sq"][k % 2]])
            self.mm(self.ps[:, pb, :T], self.ones_bf[:], sq[:, :T], k == 0, k == 7, [W["r_sq"][k % 2], self.r_small], [self.r_ps[pb]])
        rstd = W["rstd"]
        self.act(rstd[:, :T], self.ps[:, pb, :T], AF.Ln, [self.r_ps[pb], self.r_small], [W["r_rstd"]], scale=1.0 / D, bias=self.eps_t[:])
        self.act(rstd[:, :T], rstd[:, :T], AF.Exp, [W["r_rstd"]], [W["r_rstd"]], scale=-0.5)
        sh_t, sh_base = sh
        for k in range(8):
            t = W["nt"][k % 2]
            self.stt(t[:, :T], src(k), A[:, k, v:v + 1], rstd[:, :T], ALU.mult, ALU.mult, [r_src, self.r_ada, W["r_rstd"]], [W["r_nt"][k % 2]])
            self.act(hT(k), t[:, :T], AF.Identity, [W["r_nt"][k % 2], self.r_ada], [r_hT], bias=sh_t[:, sh_base + k, v:v + 1])

    def alloc_work(self, st):
        S = self.S
        W = {}
        W["sq"] = [self.sb(st, "sq%d" % i, [128, 512], BF16) for i in range(2)]
        W["r_sq"] = [S.res("sq%d" % i) for i in range(2)]
        W["rstd"] = self.sb(st, "rstd", [128, 512], F32)
        W["r_rstd"] = S.res("rstd")
        W["nt"] = [self.sb(st, "nt%d" % i, [128, 512], F32) for i in range(2)]
        W["r_nt"] = [S.res("nt%d" % i) for i in range(2)]
        return W

    def proj(self, pb, M, T, w_of_k, h_of_k, nk, reads):
        for k in range(nk):
            self.mm(self.ps[0:M, pb, :T], w_of_k(k), h_of_k(k), k == 0, k == nk - 1, reads, [self.r_ps[pb]])

    def layer0(self):
        nc, S, I = self.nc, self.S, self.I
        l = 0
        ada, A1, A2 = self.ada[0], self.A1[0], self.A2[0]
        sm = self.small
        lo = CL["lamv"][0]
        self.areset()
        with ExitStack() as st:
            tmp = self.sb(st, "lamtmp", [128, 64], F32)
            r_t = S.res("lamtmp")
            for i in range(2):
                self.tt(tmp[:], self.cst[:, lo + i * 128:lo + i * 128 + 64], self.cst[:, lo + i * 128 + 64:lo + i * 128 + 128],
                        ALU.mult, [self.r_cst], [r_t])
                S.op("dve", lambda e, i=i: e.tensor_reduce(out=sm[:, i:i + 1], in_=tmp[:], axis=mybir.AxisListType.X, op=ALU.add),
                     [r_t], [self.r_small])
            self.act(sm[:, 0:2], sm[:, 0:2], AF.Exp, [self.r_small], [self.r_small])
            self.tt(sm[:, 2:3], sm[:, 1:2], sm[:, 0:1], ALU.subtract, [self.r_small], [self.r_small])
            self.ts(sm[:, 2:3], sm[:, 2:3], -LAMBDA_INIT0, None, ALU.add, None, [self.r_small], [self.r_small])
            self.ts(sm[:, 3:4], self.cs("gsub"), 1.0 - LAMBDA_INIT0, None, ALU.mult, None, [self.r_cst], [self.r_small])
            S.barrier()
        neglam = sm[:, 2:3]
        gsub8 = sm[:, 3:4]

        S.barrier()
        self.areset()
        with ExitStack() as st:
            W = self.alloc_work(st)
            wq = self.sb(st, "wq0", [128, 8, 1280], BF16)
            wk = self.sb(st, "wk0", [128, 8, 1344], BF16)
            wv = self.sb(st, "wv0", [128, 8, 512], BF16)
            wuq = self.sb(st, "wuq", [128, 2, 1536], BF16)
            wkn = self.sb(st, "wkn", [128, 512], BF16)
            wvb = self.sb(st, "wvb", [128, 512], BF16)
            r_w = S.res("wA")
            r_wq = [S.res("wq%d" % k) for k in range(8)]
            r_wk = [S.res("wk%d" % k) for k in range(8)]
            for k in range(8):
                S.dma("pool", wq[:, k, :], I["wq0"][k * 128:(k + 1) * 128, :], writes=[r_wq[k]], semkey="wq%d" % k)
            for k in range(2):
                S.dma("pool", wuq[:, k, :], I["wuq"][k * 128:(k + 1) * 128, :], writes=[r_w], semkey="wsm")
            for k in range(8):
                S.dma("pool", wk[:, k, :], I["wk0"][k * 128:(k + 1) * 128, :], writes=[r_wk[k]], semkey="wk%d" % k)
                S.dma("pool", wv[:, k, :], I["wv0"][k * 128:(k + 1) * 128, :], writes=[r_wk[k]], semkey="wk%d" % k)
            S.dma("pool", wkn[:], I["wkn"], writes=[r_w], semkey="wsm")
            S.dma("pool", wvb[:], I["wvb"], writes=[r_w], semkey="wsm")
            xs = self.sb(st, "xstage", [128, 8, 512], F32)
            r_xs = S.res("xs")
            hT = [self.sb(st, "hT%d" % i, [128, 8, 512], BF16) for i in range(2)]
            r_hT = [S.res("hT%d" % i) for i in range(2)]
            c64 = self.sb(st, "c64", [128, 512], F32)
            s64 = self.sb(st, "s64", [128, 512], F32)
            c96 = self.sb(st, "c96", [128, 512], F32)
            s96 = self.sb(st, "s96", [128, 512], F32)
            r_tab = S.res("tab")
            t1 = [self.sb(st, "t1_%d" % i, [128, 512], F32) for i in range(2)]
            t2 = [self.sb(st, "t2_%d" % i, [128, 512], F32) for i in range(2)]
            r_t1 = [S.res("t1_%d" % i) for i in range(2)]
            r_t2 = [S.res("t2_%d" % i) for i in range(2)]
            ost = [self.sb(st, "ost%d" % i, [128, 512], BF16) for i in range(4)]
            r_ost = [S.res("ost%d" % i) for i in range(4)]
            cqf = self.sb(st, "cqf", [128, 2, 512], F32)
            cqn = self.sb(st, "cqn", [128, 2, 512], BF16)
            r_cq = S.res("cq")
            vst = [self.sb(st, "vst%d" % i, [128, 512], BF16) for i in range(2)]
            r_vst = [S.res("vst%d" % i) for i in range(2)]
            cnt = {"o": 0, "t": 0, "v": 0, "h": 0}
            allw = r_wq + r_wk + [r_w]

            def rope_evac(pb1, pb2, M, T, ctab, stab, dst_aps, p0=0):
                i = cnt["t"] % 2
                cnt["t"] += 1
                o = cnt["o"] % 4
                cnt["o"] += 1
                self.tt(t1[i][p0:M, :T], self.ps[p0:M, pb1, :T], ctab[p0:M, :T], ALU.mult, [self.r_ps[pb1], r_tab], [r_t1[i]])
                self.tt(t2[i][p0:M, :T], self.ps[p0:M, pb2, :T], stab[p0:M, :T], ALU.mult, [self.r_ps[pb2], r_tab], [r_t2[i]])
                self.tt(ost[o][p0:M, :T], t1[i][p0:M, :T], t2[i][p0:M, :T], ALU.add, [r_t1[i], r_t2[i]], [r_ost[o]], eng="pool")
                for (dap, a, b) in dst_aps:
                    S.dma("sp", dap, ost[o][a:b, :T], reads=[r_ost[o]], semkey="ost%d" % o)

            def small_norm(pbs, nk, T, dim, gname, outf, outn):
                pb = self.bank()
                for j in range(nk):
                    self.copy(outf[:, j, :T], self.ps[:, pbs[j], :T], [self.r_ps[pbs[j]]], [r_cq], eng="act")
                    sq = W["sq"][j % 2]
                    self.act(sq[:, :T], self.ps[:, pbs[j], :T], AF.Square, [self.r_ps[pbs[j]]], [W["r_sq"][j % 2]])
                    self.mm(self.ps[:, pb, :T], self.ones_bf[:], sq[:, :T], j == 0, j == nk - 1, [W["r_sq"][j % 2], self.r_small], [self.r_ps[pb]])
                rstd = W["rstd"]
                self.act(rstd[:, :T], self.ps[:, pb, :T], AF.Ln, [self.r_ps[pb], self.r_small], [W["r_rstd"]], scale=1.0 / dim, bias=self.eps_t[:])
                self.act(rstd[:, :T], rstd[:, :T], AF.Exp, [W["r_rstd"]], [W["r_rstd"]], scale=-0.5)
                for j in range(nk):
                    self.stt(outn[:, j, :T], outf[:, j, :T], self.cs(gname, j), rstd[:, :T], ALU.mult, ALU.mult,
                             [r_cq, self.r_cst, W["r_rstd"]], [r_cq])

            def q_side(h, T, t0):
                hk = lambda k: h[:, k, :T]
                S.dma("sp", c64[:, :T], I["c64_own"][:, t0:t0 + T], writes=[r_tab], semkey="tab")
                S.dma("sp", s64[:, :T], I["s64_own"][:, t0:t0 + T], writes=[r_tab], semkey="tab")
                S.dma("sp", c96[0:96, :T], I["c96_own"][:, t0:t0 + T], writes=[r_tab], semkey="tab")
                S.dma("sp", s96[0:96, :T], I["s96_own"][:, t0:t0 + T], writes=[r_tab], semkey="tab")
                for hd in range(4):
                    pb1, pb2 = self.bank(), self.bank()
                    self.proj(pb1, 128, T, lambda k: wq[:, k, hd * 128:(hd + 1) * 128], hk, 8, allw + r_hT)
                    self.proj(pb2, 128, T, lambda k: wq[:, k, 512 + hd * 128:512 + (hd + 1) * 128], hk, 8, allw + r_hT)
                    rope_evac(pb1, pb2, 128, T, c64, s64, [(self.qT_a[hd, :, t0:t0 + T], 0, 128)])
                pbs = [self.bank(), self.bank()]
                for j in range(2):
                    self.proj(pbs[j], 128, T, lambda k: wq[:, k, 1024 + j * 128:1024 + (j + 1) * 128], hk, 8, allw + r_hT)
                small_norm(pbs, 2, T, 256, "gcq", cqf, cqn)
                for hd in range(8):
                    pb1, pb2 = self.bank(), self.bank()
                    self.proj(pb1, 96, T, lambda k: wuq[:, k, hd * 96:(hd + 1) * 96], lambda k: cqn[:, k, :T], 2, allw + [r_cq])
                    self.proj(pb2, 96, T, lambda k: wuq[:, k, 768 + hd * 96:768 + (hd + 1) * 96], lambda k: cqn[:, k, :T], 2, allw + [r_cq])
                    rope_evac(pb1, pb2, 96, T, c96, s96, [(self.qT_b[hd, 0:96, t0:t0 + T], 0, 96)])

            def k_side(h, T, k0):
                hk = lambda k: h[:, k, :T]
                S.dma("sp", c64[:, :T], I["c64_all"][:, k0:k0 + T], writes=[r_tab], semkey="tab")
                S.dma("sp", s64[:, :T], I["s64_all"][:, k0:k0 + T], writes=[r_tab], semkey="tab")
                S.dma("sp", c96[0:96, :T], I["c96_all"][:, k0:k0 + T], writes=[r_tab], semkey="tab")
                S.dma("sp", s96[0:96, :T], I["s96_all"][:, k0:k0 + T], writes=[r_tab], semkey="tab")
                for hd in range(4):
                    pb1, pb2 = self.bank(), self.bank()
                    self.proj(pb1, 128, T, lambda k: wk[:, k, hd * 128:(hd + 1) * 128], hk, 8, allw + r_hT)
                    self.proj(pb2, 128, T, lambda k: wk[:, k, 512 + hd * 128:512 + (hd + 1) * 128], hk, 8, allw + r_hT)
                    rope_evac(pb1, pb2, 128, T, c64, s64, [(self.kT_a[hd, :, k0:k0 + T], 0, 128)])
                pb1, pb2 = self.bank(), self.bank()
                self.proj(pb1, 96, T, lambda k: wk[:, k, 1152:1248], hk, 8, allw + r_hT)
                self.proj(pb2, 96, T, lambda k: wk[:, k, 1248:1344], hk, 8, allw + r_hT)
                rope_evac(pb1, pb2, 96, T, c96, s96, [(self.kT_b[hd, 64:96, k0:k0 + T], 64, 96) for hd in range(8)], p0=64)
                pbc = self.bank()
                self.proj(pbc, 128, T, lambda k: wk[:, k, 1024:1152], hk, 8, allw + r_hT)
                small_norm([pbc], 1, T, 128, "gckv", cqf, cqn)
                for hp in range(4):
                    pb = self.bank()
                    self.proj(pb, 128, T, lambda k: wkn[:, hp * 128:(hp + 1) * 128], lambda k: cqn[:, 0, :T], 1, allw + [r_cq])
                    o = cnt["o"] % 4
                    cnt["o"] += 1
                    self.copy(ost[o][:, :T], self.ps[:, pb, :T], [self.r_ps[pb]], [r_ost[o]])
                    S.dma("sp", self.kT_b[2 * hp, 0:64, k0:k0 + T], ost[o][0:64, :T], reads=[r_ost[o]], semkey="ost%d" % o)
                    S.dma("sp", self.kT_b[2 * hp + 1, 0:64, k0:k0 + T], ost[o][64:128, :T], reads=[r_ost[o]], semkey="ost%d" % o)
                for tb in range(T // 128):
                    kb = k0 // 128 + tb
                    pb = self.bank()
                    for k in range(8):
                        self.mm(self.ps[:, pb, :], h[:, k, tb * 128:(tb + 1) * 128], wv[:, k, :], k == 0, k == 7, allw + r_hT, [self.r_ps[pb]])
                    vi = cnt["v"] % 2
                    cnt["v"] += 1
                    self.copy(vst[vi][:], self.ps[:, pb, :], [self.r_ps[pb]], [r_vst[vi]], eng="act")
                    S.dma("sp", self.v_a[:, :, kb, :].rearrange("h p d -> p h d"), vst[vi][:].rearrange("p (h d) -> p h d", h=4),
                          reads=[r_vst[vi]], semkey="vst%d" % vi)
                    pb = self.bank()
                    self.mm(self.ps[:, pb, :], cqn[:, 0, tb * 128:(tb + 1) * 128], wvb[:], True, True, allw + [r_cq], [self.r_ps[pb]])
                    vi = cnt["v"] % 2
                    cnt["v"] += 1
                    self.copy(vst[vi][:], self.ps[:, pb, :], [self.r_ps[pb]], [r_vst[vi]], eng="act")
                    S.dma("sp", self.v_b[:, :, kb, :].rearrange("h p d -> p h d"), vst[vi][:].rearrange("p (h d) -> p h d", h=8),
                          reads=[r_vst[vi]], semkey="vst%d" % vi)

            for ci, (t0, T, v) in enumerate(OWN_CHUNKS):
                hi = cnt["h"] % 2
                cnt["h"] += 1
                self.norm_chunk(lambda k: self.xT[:, k, t0:t0 + T], self.r_x[ci], T, A1, (ada, 0), v,
                                lambda k: hT[hi][:, k, :T], r_hT[hi], W)
                q_side(hT[hi], T, t0)
                if v == 1:
                    k_side(hT[hi], T, SEQ)
            xall = self.xT_all.rearrange("(k p) t -> p k t", p=128)
            for c in range(8):
                S.dma("sp", xs[:], xall[:, :, c * 512:(c + 1) * 512], writes=[r_xs], semkey="xs")
                hi = cnt["h"] % 2
                cnt["h"] += 1
                self.norm_chunk(lambda k: xs[:, k, :], r_xs, 512, A1, (ada, 0), 0, lambda k: hT[hi][:, k, :], r_hT[hi], W)
                k_side(hT[hi], 512, c * 512)

        S.barrier()
        self.areset()
        with ExitStack() as st:
            attnT = self.sb(st, "attnT", [128, 8, NOWN], BF16)
            a_after_attn = self.a_lo
            r_attn = S.res("attnT")
            self.attention_l0(st, attnT, r_attn, neglam, gsub8)
            S.barrier()
            self.areset(lo=a_after_attn)
            hF = self.aalloc("hF", [128, 8, NOWN], BF16, top=True)
            self.hF_off = self.a_hi
            r_hF = [S.res("hF%d" % i) for i in range(5)]
            with ExitStack() as st2:
                W = self.alloc_work(st2)
                wo = self.sb(st2, "wout", [128, 8, D], BF16)
                r_wo = S.res("wout")
                for k in range(8):
                    S.dma("pool", wo[:, k, :], I["wout0"][k * 128:(k + 1) * 128, :], writes=[r_wo], semkey="wout")
                for ci, (t0, T, v) in enumerate(OWN_CHUNKS):
                    for m in range(8):
                        pb = self.bank()
                        self.proj(pb, 128, T, lambda k: wo[:, k, m * 128:(m + 1) * 128], lambda k: attnT[:, k, t0:t0 + T], 8, [r_wo, r_attn])
                        self.stt(self.xT[:, m, t0:t0 + T], self.ps[:, pb, :T], ada[:, 16 + m, v:v + 1], self.xT[:, m, t0:t0 + T],
                                 ALU.mult, ALU.add, [self.r_ps[pb], self.r_ada, self.r_x[ci]], [self.r_x[ci]])
                    self.norm_chunk(lambda k: self.xT[:, k, t0:t0 + T], self.r_x[ci], T, A2, (ada, 24), v,
                                    lambda k: hF[:, k, t0:t0 + T], r_hF[ci], W)
            self.ffn(st, hF, r_hF, I["w1_0"], I["w3_0"], I["w2_0"], ada, OWN_CHUNKS)

    def ffn(self, st_outer, hF, r_hF, w1d, w3d, w2d, ada, chunks):
        S = self.S
        S.barrier()
        self.areset(hi=self.hF_off)
        passes = [(0, 6), (6, 6), (12, 5), (17, 5)]
        w1v = w1d.rearrange("(k p) c -> p k c", p=128)
        w3v = w3d.rearrange("(k p) c -> p k c", p=128)
        w2v = w2d.rearrange("(j p) c -> p j c", p=128)
        with ExitStack() as st:
            w1 = [self.sb(st, "w1_%d" % i, [128, 8, 768], BF16) for i in range(2)]
            w3 = [self.sb(st, "w3_%d" % i, [128, 8, 768], BF16) for i in range(2)]
            w2 = [self.sb(st, "w2_%d" % i, [128, 6, D], BF16) for i in range(2)]
            r_wf = [S.res("wf%d" % i) for i in range(2)]
            gT = [self.sb(st, "gT%d" % i, [128, 6, 512], BF16) for i in range(2)]
            r_g = [S.res("gT%d" % i) for i in range(2)]
            sil = [self.sb(st, "sil%d" % i, [128, 512], BF16) for i in range(2)]
            r_sil = [S.res("sil%d" % i) for i in range(2)]
            gi = 0
            si = 0
            for pi, (j0, nj) in enumerate(passes):
                b = pi % 2
                for k in range(8):
                    S.dma("pool", w1[b][:, k, 0:nj * 128], w1v[:, k, j0 * 128:(j0 + nj) * 128], writes=[r_wf[b]], semkey="wf%d" % b)
                    S.dma("pool", w3[b][:, k, 0:nj * 128], w3v[:, k, j0 * 128:(j0 + nj) * 128], writes=[r_wf[b]], semkey="wf%d" % b)
                for jj in range(nj):
                    S.dma("pool", w2[b][:, jj, :], w2v[:, j0 + jj, :], writes=[r_wf[b]], semkey="wf%d" % b)
                for ci, (t0, T, v) in enumerate(chunks):
                    g = gT[gi % 2]
                    rg = r_g[gi % 2]
                    gi += 1
                    for jj in range(nj):
                        pb1, pb3 = self.bank(), self.bank()
                        self.proj(pb1, 128, T, lambda k: w1[b][:, k, jj * 128:(jj + 1) * 128], lambda k: hF[:, k, t0:t0 + T], 8, [r_wf[b], r_hF[ci]])
                        self.proj(pb3, 128, T, lambda k: w3[b][:, k, jj * 128:(jj + 1) * 128], lambda k: hF[:, k, t0:t0 + T], 8, [r_wf[b], r_hF[ci]])
                        s_ = sil[si % 2]
                        rs = r_sil[si % 2]
                        si += 1
                        self.act(s_[:, :T], self.ps[:, pb1, :T], AF.Silu, [self.r_ps[pb1]], [rs])
                        self.tt(g[:, jj, :T], self.ps[:, pb3, :T], s_[:, :T], ALU.mult, [self.r_ps[pb3], rs], [rg])
                    for m in range(8):
                        pb = self.bank()
                        self.proj(pb, 128, T, lambda jj: w2[b][:, jj, m * 128:(m + 1) * 128], lambda jj: g[:, jj, :T], nj, [r_wf[b], rg])
                        self.stt(self.xT[:, m, t0:t0 + T], self.ps[:, pb, :T], ada[:, 40 + m, v:v + 1], self.xT[:, m, t0:t0 + T],
                                 ALU.mult, ALU.add, [self.r_ps[pb], self.r_ada, self.r_x[ci]], [self.r_x[ci]])

    def attn_stream(self, steps, Pt, r_P, st_state):
        S = self.S
        n = len(steps)
        LOOK = 2

        def emit_s(i):
            s = steps[i]
            sb_ = i % 3
            s["sb"] = sb_
            self.mm(self.ps[:, sb_, :s["T"]], s["kT"], s["qT"], True, True, s["reads"], [self.r_ps[sb_]])
        for i in range(min(LOOK, n)):
            emit_s(i)
        for i in range(n):
            s = steps[i]
            T = s["T"]
            p = Pt[i % len(Pt)]
            rp = r_P[i % len(Pt)]
            self.act(p[:, :T], self.ps[:, s["sb"], :T], AF.Exp, [self.r_ps[s["sb"]]], [rp], scale=s["scale"])
            if i + LOOK < n:
                emit_s(i + LOOK)
            self.mm(self.ps[0:s["M"], s["ob"], :T], s["v"], p[:, :T], s["first"], s["last"], s["reads"] + [rp], [self.r_ps[s["ob"]]])
            if s["zb"] is not None:
                self.mm(self.ps[:, s["zb"], :T], self.ones_bf[:], p[:, :T], s["first"], s["last"], [rp, self.r_small], [self.r_ps[s["zb"]]])
            if s["last"] and s["fin"] is not None:
                s["fin"]()

    def attention_l0(self, st, attnT, r_attn, neglam, gsub8):
        S = self.S
        Qb = [self.sb(st, "Qb%d" % i, [128, NOWN], BF16) for i in range(2)]
        Kb = [self.sb(st, "Kb%d" % i, [128, NKEY], BF16) for i in range(2)]
        Vb = [self.sb(st, "Vb%d" % i, [128, NKB, 192], BF16) for i in range(2)]
        r_set = [S.res("set%d" % i) for i in range(2)]
        Pt = [self.sb(st, "Pt%d" % i, [128, 512], BF16) for i in range(4)]
        r_P = [S.res("Pt%d" % i) for i in range(4)]
        W = self.alloc_work(st)
        o1n = self.sb(st, "o1n", [128, 512], F32)
        comb = self.sb(st, "comb", [128, 512], F32)
        rz = [self.sb(st, "rz%d" % i, [128, 512], F32) for i in range(2)]
        osb = [self.sb(st, "osb%d" % i, [128, 512], F32) for i in range(2)]
        r_fin = [S.res("fin%d" % i) for i in range(2)]
        r_o1n = S.res("o1n")
        r_comb = S.res("comb")
        heads = [("a", h) for h in range(4)] + [("b", h) for h in range(8)]

        def load(idx):
            kind, h = heads[idx]
            b = idx % 2
            if kind == "a":
                S.dma("sp", Qb[b][:], self.qT_a[h], writes=[r_set[b]], semkey="set%d" % b)
                S.dma("sp", Kb[b][:], self.kT_a[h], writes=[r_set[b]], semkey="set%d" % b)
                S.dma("sp", Vb[b][:, :, 0:128], self.v_a[h], writes=[r_set[b]], semkey="set%d" % b)
            else:
                if h < 2:
                    self.memset(Vb[b][:, :, 0:64], 1.0, [r_set[b]])
                    self.memset(Vb[b][:, :, 128:192], 1.0, [r_set[b]])
                S.dma("sp", Qb[b][0:96, :], self.qT_b[h, 0:96, :], writes=[r_set[b]], semkey="set%d" % b)
                S.dma("sp", Kb[b][0:96, :], self.kT_b[h, 0:96, :], writes=[r_set[b]], semkey="set%d" % b)
                S.dma("sp", Vb[b][:, :, 64:128], self.v_b[h], writes=[r_set[b]], semkey="set%d" % b)

        accs = [(3, 5), (4, 6)]
        state = {"acc": 0, "fin": 0}
        load(0)
        for idx, (kind, h) in enumerate(heads):
            b = idx % 2
            if idx + 1 < len(heads):
                load(idx + 1)
            steps = []
            if kind == "a":
                for (t0, T, v) in OWN_CHUNKS:
                    kbs = list(range(NKB)) if v == 0 else [32, 33]
                    for s_ in range(2):
                        ob, zb = accs[state["acc"] % 2]
                        state["acc"] += 1
                        for n_, kb in enumerate(kbs):
                            step = dict(kT=Kb[b][s_ * 64:(s_ + 1) * 64, kb * 128:(kb + 1) * 128], qT=Qb[b][s_ * 64:(s_ + 1) * 64, t0:t0 + T],
                                        T=T, scale=0.125, v=Vb[b][:, kb, 0:128], M=128, ob=ob, zb=zb, first=(n_ == 0),
                                        last=(n_ == len(kbs) - 1), reads=[r_set[b]], fin=None)
                            steps.append(step)

                        def fin(ob=ob, zb=zb, s_=s_, t0=t0, T=T, h=h):
                            fi = state["fin"] % 2
                            state["fin"] += 1
                            S.op("dve", lambda e: e.reciprocal(out=rz[fi][:, :T], in_=self.ps[:, zb, :T]), [self.r_ps[zb]], [r_fin[fi]])
                            if s_ == 0:
                                self.tt(o1n[:, :T], self.ps[:, ob, :T], rz[fi][:, :T], ALU.mult, [self.r_ps[ob], r_fin[fi]], [r_o1n])
                            else:
                                self.tt(osb[fi][:, :T], self.ps[:, ob, :T], rz[fi][:, :T], ALU.mult, [self.r_ps[ob], r_fin[fi]], [r_fin[fi]])
                                self.stt(comb[:, :T], osb[fi][:, :T], neglam, o1n[:, :T], ALU.mult, ALU.add,
                                         [r_fin[fi], r_o1n, self.r_small], [r_comb])
                                sq = W["sq"][0]
                                self.tt(sq[:, :T], comb[:, :T], comb[:, :T], ALU.mult, [r_comb], [W["r_sq"][0]])
                                self.mm(self.ps[:, 7, :T], self.ones_bf[:], sq[:, :T], True, True, [W["r_sq"][0], self.r_small], [self.r_ps[7]])
                                rstd = W["rstd"]
                                self.act(rstd[:, :T], self.ps[:, 7, :T], AF.Ln, [self.r_ps[7], self.r_small], [W["r_rstd"]], scale=1.0 / 128, bias=self.eps_t[:])
                                self.act(rstd[:, :T], rstd[:, :T], AF.Exp, [W["r_rstd"]], [W["r_rstd"]], scale=-0.5)
                                self.stt(attnT[:, h, t0:t0 + T], comb[:, :T], gsub8, rstd[:, :T], ALU.mult, ALU.mult,
                                         [r_comb, W["r_rstd"], self.r_small], [r_attn])
                        steps[-1]["fin"] = fin
            else:
                po, pz = (0, 64) if h % 2 == 0 else (64, 0)
                vsl = (64, 192) if h % 2 == 0 else (0, 128)
                for (t0, T, v) in OWN_CHUNKS:
                    kbs = list(range(NKB)) if v == 0 else [32, 33]
                    ob, _ = accs[state["acc"] % 2]
                    state["acc"] += 1
                    for n_, kb in enumerate(kbs):
                        step = dict(kT=Kb[b][0:96, kb * 128:(kb + 1) * 128], qT=Qb[b][0:96, t0:t0 + T], T=T, scale=96 ** -0.5,
                                    v=Vb[b][:, kb, vsl[0]:vsl[1]], M=128, ob=ob, zb=None, first=(n_ == 0), last=(n_ == len(kbs) - 1),
                                    reads=[r_set[b]], fin=None)
                        steps.append(step)

                    def fin(ob=ob, t0=t0, T=T, h=h, po=po, pz=pz):
                        fi = state["fin"] % 2
                        state["fin"] += 1
                        self.copy(osb[fi][po:po + 64, :T], self.ps[po:po + 64, ob, :T], [self.r_ps[ob]], [r_fin[fi]])
                        S.op("dve", lambda e: e.reciprocal(out=self.ps[pz:pz + 64, ob, :T], in_=self.ps[pz:pz + 64, ob, :T]),
                             [self.r_ps[ob]], [self.r_ps[ob]])
                        self.tt(attnT[po:po + 64, 4 + h // 2, t0:t0 + T], self.ps[pz:pz + 64, ob, :T], osb[fi][po:po + 64, :T], ALU.mult,
                                [self.r_ps[ob], r_fin[fi]], [r_attn])
                    steps[-1]["fin"] = fin
            self.attn_stream(steps, Pt, r_P, state)

    def layer1(self):
        nc, S, I = self.nc, self.S, self.I
        ada, A1, A2 = self.ada[1], self.A1[1], self.A2[1]
        fused = self.mode == "fused"
        LAT = OWN_CHUNKS[:4]
        S.barrier()
        self.areset()
        with ExitStack() as st:
            W = self.alloc_work(st)
            bd = self.sb(st, "bd_ones", [128, 128], BF16)
            r_bd = S.res("bd")
            self.memset(bd[:], 0.0, [r_bd])
            self.memset(bd[0:64, 0:64], 1.0, [r_bd])
            self.memset(bd[64:128, 64:128], 1.0, [r_bd])
            wq = self.sb(st, "wq1", [128, 8, 1536], BF16)
            wk = self.sb(st, "wk1", [128, 8, 768], BF16)
            wv = self.sb(st, "wv1", [128, 8, 640], BF16)
            r_w = S.res("wA1")
            for k in range(8):
                S.dma("pool", wq[:, k, :], I["wq1"][k * 128:(k + 1) * 128, :], writes=[r_w], semkey="w1a")
                S.dma("pool", wk[:, k, :], I["wk1"][k * 128:(k + 1) * 128, :], writes=[r_w], semkey="w1a")
                S.dma("pool", wv[:, k, :], I["wv1"][k * 128:(k + 1) * 128, :], writes=[r_w], semkey="w1a")
            xs = self.sb(st, "xstage", [128, 8, 512], F32)
            r_xs = S.res("xs")
            hT = [self.sb(st, "hT%d" % i, [128, 8, 512], BF16) for i in range(2)]
            r_hT = [S.res("hT%d" % i) for i in range(2)]
            c64 = self.sb(st, "c64", [128, 512], F32)
            s64 = self.sb(st, "s64", [128, 512], F32)
            r_tab = S.res("tab")
            t1 = [self.sb(st, "t1_%d" % i, [128, 512], F32) for i in range(2)]
            t2 = [self.sb(st, "t2_%d" % i, [128, 512], F32) for i in range(2)]
            r_t1 = [S.res("t1_%d" % i) for i in range(2)]
            r_t2 = [S.res("t2_%d" % i) for i in range(2)]
            ost = [self.sb(st, "ost%d" % i, [128, 512], BF16) for i in range(4)]
            r_ost = [S.res("ost%d" % i) for i in range(4)]
            vst = [self.sb(st, "vst%d" % i, [128, 640], BF16) for i in range(2)]
            r_vst = [S.res("vst%d" % i) for i in range(2)]
            cnt = {"o": 0, "t": 0, "v": 0, "h": 0}
            allw = [r_w]

            def load_tabs(T, which, t0):
                S.dma("sp", c64[:, :T], I["c64_" + which][:, t0:t0 + T], writes=[r_tab], semkey="tab")
                S.dma("sp", s64[:, :T], I["s64_" + which][:, t0:t0 + T], writes=[r_tab], semkey="tab")

            def normrope(pb1, pb2, T, gname, dsts):
                i = cnt["t"] % 2
                cnt["t"] += 1
                o = cnt["o"] % 4
                cnt["o"] += 1
                pbs = self.bank()
                sq = W["sq"][0]
                self.act(sq[:, :T], self.ps[:, pb1, :T], AF.Square, [self.r_ps[pb1]], [W["r_sq"][0]])
                self.mm(self.ps[:, pbs, :T], bd[:], sq[:, :T], True, True, [W["r_sq"][0], r_bd], [self.r_ps[pbs]])
                rstd = W["rstd"]
                self.act(rstd[:, :T], self.ps[:, pbs, :T], AF.Ln, [self.r_ps[pbs], self.r_small], [W["r_rstd"]], scale=1.0 / 64, bias=self.eps_t[:])
                self.act(rstd[:, :T], rstd[:, :T], AF.Exp, [W["r_rstd"]], [W["r_rstd"]], scale=-0.5)
                self.stt(t1[i][:, :T], self.ps[:, pb1, :T], self.cs(gname), rstd[:, :T], ALU.mult, ALU.mult,
                         [self.r_ps[pb1], self.r_cst, W["r_rstd"]], [r_t1[i]])
                self.stt(t2[i][:, :T], self.ps[:, pb2, :T], self.cs(gname + "s"), rstd[:, :T], ALU.mult, ALU.mult,
                         [self.r_ps[pb2], self.r_cst, W["r_rstd"]], [r_t2[i]])
                self.tt(t1[i][:, :T], t1[i][:, :T], c64[:, :T], ALU.mult, [r_t1[i], r_tab], [r_t1[i]])
                self.tt(t2[i][:, :T], t2[i][:, :T], s64[:, :T], ALU.mult, [r_t2[i], r_tab], [r_t2[i]], eng="pool")
                self.tt(ost[o][:, :T], t1[i][:, :T], t2[i][:, :T], ALU.add, [r_t1[i], r_t2[i]], [r_ost[o]])
                for dap in dsts:
                    S.dma("sp", dap, ost[o][:, :T], reads=[r_ost[o]], semkey="ost%d" % o)

            def plain_evac(pb, T, dsts, c0=0):
                o = cnt["o"] % 4
                cnt["o"] += 1
                self.copy(ost[o][:, :T], self.ps[:, pb, c0:c0 + T], [self.r_ps[pb]], [r_ost[o]], eng="act")
                for dap in dsts:
                    S.dma("sp", dap, ost[o][:, :T], reads=[r_ost[o]], semkey="ost%d" % o)

            def kd_vd(h, c0, T, L0):
                for j in range(4):
                    pb = self.bank()
                    self.proj(pb, 128, T, lambda k: wk[:, k, 256 + j * 128:256 + (j + 1) * 128], lambda k: h[:, k, c0:c0 + T], 8, allw + r_hT)
                    plain_evac(pb, T, [self.kT_d[j, :, L0 * 128:L0 * 128 + T]])
                for tb in range(T // 128):
                    pb = self.bank()
                    for k in range(8):
                        self.mm(self.ps[:, pb, :], h[:, k, c0 + tb * 128:c0 + (tb + 1) * 128], wv[:, k, 128:640], k == 0, k == 7, allw + r_hT, [self.r_ps[pb]])
                    vi = cnt["v"] % 2
                    cnt["v"] += 1
                    self.copy(vst[vi][:, 0:512], self.ps[:, pb, :], [self.r_ps[pb]], [r_vst[vi]], eng="act")
                    S.dma("sp", self.v_d[:, :, L0 + tb, :].rearrange("h p d -> p h d"), vst[vi][:, 0:512].rearrange("p (h d) -> p h d", h=8),
                          reads=[r_vst[vi]], semkey="vst%d" % vi)

            def kc_vc(h, T, k0):
                hk = lambda k: h[:, k, :T]
                load_tabs(T, "all", k0)
                pb1, pb2 = self.bank(), self.bank()
                self.proj(pb1, 128, T, lambda k: wk[:, k, 0:128], hk, 8, allw + r_hT)
                self.proj(pb2, 128, T, lambda k: wk[:, k, 128:256], hk, 8, allw + r_hT)
                normrope(pb1, pb2, T, "gkc", [self.kT_c[:, k0:k0 + T]])
                for tb in range(T // 128):
                    pb = self.bank()
                    for k in range(8):
                        self.mm(self.ps[:, pb, 0:128], h[:, k, tb * 128:(tb + 1) * 128], wv[:, k, 0:128], k == 0, k == 7, allw + r_hT, [self.r_ps[pb]])
                    vi = cnt["v"] % 2
                    cnt["v"] += 1
                    self.copy(vst[vi][:, 512:640], self.ps[:, pb, 0:128], [self.r_ps[pb]], [r_vst[vi]], eng="act")
                    S.dma("sp", self.v_c[:, :, k0 // 128 + tb, :].rearrange("g p d -> p g d"), vst[vi][:, 512:640].rearrange("p (g d) -> p g d", g=2),
                          reads=[r_vst[vi]], semkey="vst%d" % vi)

            def q_side(h, T, t0):
                hk = lambda k: h[:, k, :T]
                load_tabs(T, "own", t0)
                for j in range(4):
                    pb1, pb2 = self.bank(), self.bank()
                    self.proj(pb1, 128, T, lambda k: wq[:, k, j * 128:(j + 1) * 128], hk, 8, allw + r_hT)
                    self.proj(pb2, 128, T, lambda k: wq[:, k, 512 + j * 128:512 + (j + 1) * 128], hk, 8, allw + r_hT)
                    normrope(pb1, pb2, T, "gqc", [self.qT_c[j, :, t0:t0 + T]])
                for j in range(4):
                    pb = self.bank()
                    self.proj(pb, 128, T, lambda k: wq[:, k, 1024 + j * 128:1024 + (j + 1) * 128], hk, 8, allw + r_hT)
                    plain_evac(pb, T, [self.qT_d[j, :, t0:t0 + T]])

            for ci, (t0, T, v) in enumerate(OWN_CHUNKS):
                hi = cnt["h"] % 2
                cnt["h"] += 1
                self.norm_chunk(lambda k: self.xT[:, k, t0:t0 + T], self.r_x[ci], T, A1, (ada, 0), v,
                                lambda k: hT[hi][:, k, :T], r_hT[hi], W)
                if v == 0:
                    q_side(hT[hi], T, t0)
                    kd_vd(hT[hi], 0, T, 2 + t0 // 128)
                else:
                    kc_vc(hT[hi], T, SEQ)
                    kd_vd(hT[hi], 0, T, 20)
            xall = self.xT_all.rearrange("(k p) t -> p k t", p=128)
            for c in range(8):
                hi = cnt["h"] % 2
                cnt["h"] += 1
                S.dma("sp", xs[:], xall[:, :, c * 512:(c + 1) * 512], writes=[r_xs], semkey="xs")
                self.norm_chunk(lambda k: xs[:, k, :], r_xs, 512, A1, (ada, 0), 0, lambda k: hT[hi][:, k, :], r_hT[hi], W)
                kc_vc(hT[hi], 512, c * 512)
                if c == 3:
                    kd_vd(hT[hi], 256, 256, 0)
                if c == 4:
                    kd_vd(hT[hi], 0, 256, 18)

        S.barrier()
        self.areset()
        attnT = self.aalloc("attnT1", [128, 8, TOWN], BF16)
        a_after_attn = self.a_lo
        r_attn = S.res("attnT1")
        self.attention_gqa(attnT, r_attn)
        S.barrier()
        self.areset(lo=a_after_attn)
        self.attention_na(attnT, r_attn)
        S.barrier()
        self.areset(lo=a_after_attn)
        hF = self.aalloc("hF1", [128, 8, TOWN], BF16, top=True)
        self.hF_off = self.a_hi
        r_hF = [S.res("hF%d" % i) for i in range(4)]
        W = self.alloc_work(None)
        wo = self.aalloc("wout1", [128, 8, D], BF16)
        r_wo = S.res("wout1")
        for k in range(8):
            S.dma("pool", wo[:, k, :], I["wout1"][k * 128:(k + 1) * 128, :], writes=[r_wo], semkey="wout")
        for ci, (t0, T, v) in enumerate(LAT):
            for m in range(8):
                pb = self.bank()
                self.proj(pb, 128, T, lambda k: wo[:, k, m * 128:(m + 1) * 128], lambda k: attnT[:, k, t0:t0 + T], 8, [r_wo, r_attn])
                self.stt(self.xT[:, m, t0:t0 + T], self.ps[:, pb, :T], ada[:, 16 + m, v:v + 1], self.xT[:, m, t0:t0 + T],
                         ALU.mult, ALU.add, [self.r_ps[pb], self.r_ada, self.r_x[ci]], [self.r_x[ci]])
            self.norm_chunk(lambda k: self.xT[:, k, t0:t0 + T], self.r_x[ci], T, A2, (ada, 24), v,
                            lambda k: hF[:, k, t0:t0 + T], r_hF[ci], W)
        self.ffn(None, hF, r_hF, I["w1_1"], I["w3_1"], I["w2_1"], ada, LAT)

    def attention_gqa(self, attnT, r_attn):
        S = self.S
        Qc = [self.aalloc("Qc%d" % j, [128, TOWN], BF16) for j in range(4)]
        Kc = self.aalloc("Kc", [128, NKEY], BF16)
        Vc = [self.aalloc("Vc%d" % g, [128, NKB, 192], BF16) for g in range(2)]
        r_in = S.res("gqa_in")
        Pt = [self.aalloc("Pt%d" % i, [128, 512], BF16) for i in range(4)]
        r_P = [S.res("Pt%d" % i) for i in range(4)]
        osb = [self.aalloc("osb%d" % i, [128, 512], F32) for i in range(2)]
        r_fin = [S.res("fin%d" % i) for i in range(2)]
        for g in range(2):
            self.memset(Vc[g][:, :, 0:64], 1.0, [r_in])
            self.memset(Vc[g][:, :, 128:192], 1.0, [r_in])
        S.dma("sp", Kc[:], self.kT_c, writes=[r_in], semkey="gin")
        for j in range(4):
            S.dma("sp", Qc[j][:], self.qT_c[j], writes=[r_in], semkey="gin")
        for g in range(2):
            S.dma("sp", Vc[g][:, :, 64:128], self.v_c[g], writes=[r_in], semkey="gin")
        accs = [3, 4]
        state = {"acc": 0, "fin": 0}
        steps = []
        for g in range(2):
            for j in range(4):
                h = g * 4 + j
                po, pz = (0, 64) if h % 2 == 0 else (64, 0)
                vsl = (64, 192) if h % 2 == 0 else (0, 128)
                for (t0, T, v) in OWN_CHUNKS[:4]:
                    ob = accs[state["acc"] % 2]
                    state["acc"] += 1
                    for kb in range(NKB):
                        steps.append(dict(kT=Kc[g * 64:(g + 1) * 64, kb * 128:(kb + 1) * 128], qT=Qc[j][g * 64:(g + 1) * 64, t0:t0 + T],
                                          T=T, scale=0.125, v=Vc[g][:, kb, vsl[0]:vsl[1]], M=128, ob=ob, zb=None, first=(kb == 0),
                                          last=(kb == NKB - 1), reads=[r_in], fin=None))

                    def fin(ob=ob, t0=t0, T=T, h=h, po=po, pz=pz):
                        fi = state["fin"] % 2
                        state["fin"] += 1
                        self.copy(osb[fi][po:po + 64, :T], self.ps[po:po + 64, ob, :T], [self.r_ps[ob]], [r_fin[fi]])
                        S.op("dve", lambda e: e.reciprocal(out=self.ps[pz:pz + 64, ob, :T], in_=self.ps[pz:pz + 64, ob, :T]),
                             [self.r_ps[ob]], [self.r_ps[ob]])
                        self.tt(attnT[po:po + 64, h // 2, t0:t0 + T], self.ps[pz:pz + 64, ob, :T], osb[fi][po:po + 64, :T], ALU.mult,
                                [self.r_ps[ob], r_fin[fi]], [r_attn])
                    steps[-1]["fin"] = fin
        self.attn_stream(steps, Pt, r_P, state)

    def attention_na(self, attnT, r_attn):
        S = self.S
        Qd = [self.aalloc("Qd%d" % i, [128, TOWN], BF16) for i in range(2)]
        Kd = [self.aalloc("Kd%d" % i, [128, 22 * 128], BF16) for i in range(2)]
        Vd = [self.aalloc("Vd%d" % i, [128, 22, 192], BF16) for i in range(2)]
        tab = [self.aalloc("natab%d" % i, [128, 5, 768], F32) for i in range(2)]
        r_pair = [S.res("napair%d" % i) for i in range(2)]
        r_head = [S.res("nahead%d" % i) for i in range(2)]
        tmp = [self.aalloc("natmp%d" % i, [128, 768], F32) for i in range(2)]
        r_tmp = [S.res("natmp%d" % i) for i in range(2)]
        P = [self.aalloc("naP%d" % i, [128, 1024], BF16) for i in range(2)]
        r_P = [S.res("naP%d" % i) for i in range(2)]
        osb = [self.aalloc("osb%d" % i, [128, 512], F32) for i in range(2)]
        r_fin = [S.res("fin%d" % i) for i in range(2)]
        for b in range(2):
            self.memset(Vd[b][:, :, 0:64], 1.0, [r_head[b]])
            self.memset(Vd[b][:, :, 128:192], 1.0, [r_head[b]])
        nab = self.I["nab"]

        def load_pair(j):
            b = j % 2
            S.dma("sp", Qd[b][:], self.qT_d[j], writes=[r_pair[b]], semkey="napair%d" % b)
            S.dma("sp", Kd[b][:], self.kT_d[j], writes=[r_pair[b]], semkey="napair%d" % b)

        def load_head(h):
            b = h % 2
            S.dma("sp", Vd[b][:, :, 64:128], self.v_d[h], writes=[r_head[b]], semkey="nahead%d" % b)
            S.dma("sp", tab[b][:], nab[:, :, h * 768:(h + 1) * 768].rearrange("c p f -> p c f"), writes=[r_head[b]], semkey="nahead%d" % b)

        units = [(h, i) for h in range(8) for i in range(16)]
        CLS = {0: 1, 1: 2, 14: 3, 15: 4}
        sbanks = [(0, 1), (2, 3)]
        obanks = [4, 5]
        state = {"fin": 0}

        def emit_s(u):
            h, i = units[u]
            j, e = h // 2, h % 2
            bp = j % 2
            A, B = sbanks[u % 2]
            L0 = min(i, 14)
            q = Qd[bp][e * 64:(e + 1) * 64, i * 128:(i + 1) * 128]
            for s_ in range(8):
                L = L0 + s_ if s_ < 6 else 20 + (s_ - 6)
                bk = A if s_ < 4 else B
                cc = (s_ % 4) * 128
                self.mm(self.ps[:, bk, cc:cc + 128], Kd[bp][e * 64:(e + 1) * 64, L * 128:(L + 1) * 128], q, True, True,
                        [r_pair[bp]], [self.r_ps[bk]])

        load_pair(0)
        load_head(0)
        emit_s(0)
        for u, (h, i) in enumerate(units):
            j, e = h // 2, h % 2
            bp, bh = j % 2, h % 2
            if i == 0:
                if h + 1 < 8:
                    load_head(h + 1)
                    if e == 1:
                        load_pair(j + 1)
            A, B = sbanks[u % 2]
            cls = CLS.get(i, 0)
            L0 = min(i, 14)
            t = tmp[u % 2]
            p = P[u % 2]
            self.stt(t[:, 0:512], self.ps[:, A, :], 0.125, tab[bh][:, cls, 0:512], ALU.mult, ALU.add,
                     [self.r_ps[A], r_head[bh]], [r_tmp[u % 2]])
            self.stt(t[:, 512:768], self.ps[:, B, 0:256], 0.125, tab[bh][:, cls, 512:768], ALU.mult, ALU.add,
                     [self.r_ps[B], r_head[bh]], [r_tmp[u % 2]])
            self.act(p[:, 0:768], t[:, 0:768], AF.Exp, [r_tmp[u % 2]], [r_P[u % 2]])
            self.act(p[:, 768:1024], self.ps[:, B, 256:512], AF.Exp, [self.r_ps[B]], [r_P[u % 2]], scale=0.125)
            if u + 1 < len(units):
                emit_s(u + 1)
            ob = obanks[(u // 4) % 2]
            oc = (i % 4) * 128
            vsl = (64, 192) if e == 0 else (0, 128)
            for s_ in range(8):
                L = L0 + s_ if s_ < 6 else 20 + (s_ - 6)
                self.mm(self.ps[:, ob, oc:oc + 128], Vd[bh][:, L, vsl[0]:vsl[1]], p[:, s_ * 128:(s_ + 1) * 128], s_ == 0, s_ == 7,
                        [r_head[bh], r_P[u % 2]], [self.r_ps[ob]])
            if i % 4 == 3:
                po, pz = (0, 64) if e == 0 else (64, 0)
                t0 = (i - 3) * 128
                fi = state["fin"] % 2
                state["fin"] += 1
                self.copy(osb[fi][po:po + 64, :], self.ps[po:po + 64, ob, :], [self.r_ps[ob]], [r_fin[fi]])
                S.op("dve", lambda e_, ob=ob, pz=pz: e_.reciprocal(out=self.ps[pz:pz + 64, ob, :], in_=self.ps[pz:pz + 64, ob, :]),
                     [self.r_ps[ob]], [self.r_ps[ob]])
                self.tt(attnT[po:po + 64, 4 + h // 2, t0:t0 + 512], self.ps[pz:pz + 64, ob, :], osb[fi][po:po + 64, :], ALU.mult,
                        [self.r_ps[ob], r_fin[fi]], [r_attn])

    def final_norm_out(self):
        S = self.S
        S.barrier()
        self.areset()
        W = self.alloc_work(None)
        stage = [self.aalloc("ostage%d" % i, [128, 8, 512], F32) for i in range(2)]
        r_stage = [S.res("ostage%d" % i) for i in range(2)]
        fins = []
        gf = CL["gfinal"][0]
        for ci, (t0, T, v) in enumerate(OWN_CHUNKS[:4]):
            pb = self.bank()
            for k in range(8):
                sq = W["sq"][k % 2]
                self.act(sq[:, :T], self.xT[:, k, t0:t0 + T], AF.Square, [self.r_x[ci]], [W["r_sq"][k % 2]])
                self.mm(self.ps[:, pb, :T], self.ones_bf[:], sq[:, :T], k == 0, k == 7, [W["r_sq"][k % 2], self.r_small], [self.r_ps[pb]])
            rstd = W["rstd"]
            self.act(rstd[:, :T], self.ps[:, pb, :T], AF.Ln, [self.r_ps[pb], self.r_small], [W["r_rstd"]], scale=1.0 / D, bias=self.eps_t[:])
            self.act(rstd[:, :T], rstd[:, :T], AF.Exp, [W["r_rstd"]], [W["r_rstd"]], scale=-0.5)
            sg = stage[ci % 2]
            for k in range(8):
                self.stt(sg[:, k, :T], self.xT[:, k, t0:t0 + T], self.cst[:, gf + k:gf + k + 1], rstd[:, :T], ALU.mult, ALU.mult,
                         [self.r_x[ci], self.r_cst, W["r_rstd"]], [r_stage[ci % 2]])
            for k in range(8):
                fins.append(S.dma("sp", self.y_x[k * 128:(k + 1) * 128, t0:t0 + T], sg[:, k, :T], reads=[r_stage[ci % 2]],
                                  semkey="ystage%d" % (ci % 2)))
        return fins


def _rope_tables(rot_dim, pos):
    rows = (pos // GRID_W).astype(np.float32)
    cols = (pos % GRID_W).astype(np.float32)
    axis_dim = rot_dim // 2
    inv = (10000.0 ** (-np.arange(0, axis_dim, 2, dtype=np.float32) / axis_dim)).astype(np.float32)
    ang = np.concatenate([rows[:, None] * inv, cols[:, None] * inv], axis=-1).astype(np.float32)
    return np.cos(ang).astype(np.float32), np.sin(ang).astype(np.float32)


def _tables64(pos, nctx):
    c, s = _rope_tables(64, pos)
    C = np.concatenate([c, c, c, c], axis=1).T
    Sg = np.concatenate([-s, s, -s, s], axis=1).T
    C = np.concatenate([C, np.ones((128, nctx), np.float32)], axis=1)
    Sg = np.concatenate([Sg, np.zeros((128, nctx), np.float32)], axis=1)
    return np.ascontiguousarray(C, np.float32), np.ascontiguousarray(Sg, np.float32)


def _tables96(pos, nctx):
    c, s = _rope_tables(32, pos)
    T = len(pos)
    C = np.concatenate([np.ones((T, 64), np.float32), c, c], axis=1).T
    Sg = np.concatenate([np.zeros((T, 64), np.float32), -s, s], axis=1).T
    C = np.concatenate([C, np.ones((96, nctx), np.float32)], axis=1)
    Sg = np.concatenate([Sg, np.zeros((96, nctx), np.float32)], axis=1)
    return np.ascontiguousarray(C, np.float32), np.ascontiguousarray(Sg, np.float32)


def _fm(vec, k):
    return np.ascontiguousarray(np.asarray(vec, np.float32).reshape(k, 128).T)


def _consts(inp, b):
    cst = np.zeros((128, NCONST), np.float32)

    def put(name, arr):
        o, w = CL[name]
        cst[:, o:o + w] = np.asarray(arr, np.float32).reshape(128, w)
    cfm = np.stack([_fm(inp["c"][b], 8), _fm(inp["c_ctx"], 8)], axis=-1)
    put("cfm", cfm.reshape(128, 16))
    put("bmod0", _fm(inp["l0_b_mod"], 48))
    put("bmod1", _fm(inp["l1_b_mod"], 48))
    put("gattn0", _fm(inp["l0_g_attn"], 8))
    put("gffn0", _fm(inp["l0_g_ffn"], 8))
    put("gattn1", _fm(inp["l1_g_attn"], 8))
    put("gffn1", _fm(inp["l1_g_ffn"], 8))
    put("gfinal", _fm(inp["g_final"], 8))
    put("gsub", _fm(inp["l0_g_subln"], 1))
    put("gcq", _fm(inp["l0_g_cq"], 2))
    put("gckv", _fm(inp["l0_g_ckv"], 1))
    lam = np.concatenate([inp["l0_lam_q1"], inp["l0_lam_k1"], inp["l0_lam_q2"], inp["l0_lam_k2"]]).astype(np.float32)
    put("lamv", np.broadcast_to(lam[None, :], (128, 256)))
    gq = np.asarray(inp["l1_g_qc"], np.float32)
    gk = np.asarray(inp["l1_g_kc"], np.float32)
    sw = _swap_halves(np.arange(64), 64)
    put("gqc", np.tile(gq, 2)[:, None])
    put("gqcs", np.tile(gq[sw], 2)[:, None])
    put("gkc", np.tile(gk, 2)[:, None])
    put("gkcs", np.tile(gk[sw], 2)[:, None])
    return cst


def _l0_weights(inp):
    w_in = np.asarray(inp["l0_w_in"], np.float32)
    qa = np.arange(0, 512)
    ka = np.arange(512, 1024)
    va = np.arange(1024, 1536)
    cq = np.arange(1536, 1792)
    ckv = np.arange(1792, 1920)
    kr = np.arange(1920, 1952)
    wq_cols = np.concatenate([qa, _swap_halves(qa, 64), cq])
    wk_cols = np.concatenate([ka, _swap_halves(ka, 64), ckv, ckv[:64], kr, ckv[:64], _swap_halves(kr, 32)])
    w_uq = np.asarray(inp["l0_w_uq"], np.float32)
    uq = np.arange(768).reshape(8, 96)
    uqs = uq.copy()
    for h in range(8):
        uqs[h, 64:96] = _swap_halves(uq[h, 64:96], 32)
    w_ukv = np.asarray(inp["l0_w_ukv"], np.float32)
    kv = np.arange(1024).reshape(8, 128)
    return {
        "wq0": np.ascontiguousarray(w_in[:, wq_cols]),
        "wk0": np.ascontiguousarray(w_in[:, wk_cols]),
        "wv0": np.ascontiguousarray(w_in[:, va]),
        "wuq": np.ascontiguousarray(w_uq[:, np.concatenate([uq.reshape(-1), uqs.reshape(-1)])]),
        "wkn": np.ascontiguousarray(w_ukv[:, kv[:, :64].reshape(-1)]),
        "wvb": np.ascontiguousarray(w_ukv[:, kv[:, 64:].reshape(-1)]),
    }


def _core_inputs_l0(inp, core, shared):
    b, half = core // 2, core % 2
    x = np.asarray(inp["x"], np.float32)
    m = {}
    xb = x[b]
    m["xT_own"] = np.ascontiguousarray(xb[half * TOWN:(half + 1) * TOWN].T)
    m["xT_all"] = shared.setdefault(("xT_all", b), np.ascontiguousarray(xb.T))
    m["ctxT"] = shared.setdefault(("ctxT", b), np.ascontiguousarray(np.asarray(inp["ctx"], np.float32)[b].T))
    m["consts"] = _consts(inp, b)
    return m


_CACHE = {}


def _get_prog(mode):
    if mode not in _CACHE:
        p = Prog(mode)
        p.build()
        _CACHE[mode] = p
    return _CACHE[mode]


def run_l0(inp):
    prog = _get_prog("l0")
    shared = {}
    W = _l0_weights(inp)
    pos_all = np.arange(SEQ)
    c64a, s64a = _tables64(pos_all, NCTX)
    c96a, s96a = _tables96(pos_all, NCTX)
    maps = []
    for core in range(8):
        half = core % 2
        m = _core_inputs_l0(inp, core, shared)
        m.update(W)
        m["wmod0"] = np.asarray(inp["l0_w_mod"], np.float32)
        m["wout0"] = np.asarray(inp["l0_w_out"], np.float32)
        m["w1_0"] = np.asarray(inp["l0_w1"], np.float32)
        m["w3_0"] = np.asarray(inp["l0_w3"], np.float32)
        m["w2_0"] = np.asarray(inp["l0_w2"], np.float32)
        pos_own = np.arange(half * TOWN, (half + 1) * TOWN)
        m["c64_own"], m["s64_own"] = shared.setdefault(("t64", half), _tables64(pos_own, NCTX))
        m["c96_own"], m["s96_own"] = shared.setdefault(("t96", half), _tables96(pos_own, NCTX))
        m["c64_all"], m["s64_all"], m["c96_all"], m["s96_all"] = c64a, s64a, c96a, s96a
        maps.append(m)
    res = run_bass_kernel_spmd(prog.nc, maps, core_ids=list(range(8)))
    x1 = np.zeros((4, SEQ, D), np.float32)
    xc1 = np.zeros((4, NCTX, D), np.float32)
    for core in range(8):
        b, half = core // 2, core % 2
        x1[b, half * TOWN:(half + 1) * TOWN] = res.results[core]["y_x"].T
        xc1[b] = res.results[core]["y_c"].T
    return x1, xc1


def _l1_weights(inp):
    w_in = np.asarray(inp["l1_w_in"], np.float32)
    qc = np.arange(0, 512).reshape(8, 64)
    kc = np.arange(512, 640)
    vc = np.arange(640, 768)
    qd = np.arange(768, 1280)
    kd = np.arange(1280, 1792)
    vd = np.arange(1792, 2304)
    qct = np.concatenate([np.concatenate([qc[j], qc[j + 4]]) for j in range(4)])
    wq_cols = np.concatenate([qct, _swap_halves(qct, 64), qd])
    wk_cols = np.concatenate([kc, _swap_halves(kc, 64), kd])
    wv_cols = np.concatenate([vc, vd])
    return {"wq1": np.ascontiguousarray(w_in[:, wq_cols]), "wk1": np.ascontiguousarray(w_in[:, wk_cols]),
            "wv1": np.ascontiguousarray(w_in[:, wv_cols])}


def _na_tables(rpb, half):
    rpb = np.asarray(rpb, np.float32)
    NEG = -30000.0
    out = np.full((5, 128, 8, 6, 128), NEG, np.float32)
    for cls, i in [(0, 5), (1, 0), (2, 1), (3, 14), (4, 15)]:
        gi = 16 * half + i
        qpos = gi * 128 + np.arange(128)
        r, c = qpos // GRID_W, qpos % GRID_W
        rs = np.clip(r - 4, 0, 56)
        cs = np.clip(c - 8, 0, 48)
        L0 = min(i, 14)
        for s_ in range(6):
            L = L0 + s_
            if L < 2:
                gb = 14 + L
            elif L < 18:
                gb = 16 * half + L - 2
            else:
                gb = 16 + L - 18
            kpos = gb * 128 + np.arange(128)
            kr, kc = kpos // GRID_W, kpos % GRID_W
            inwin = ((kr[:, None] >= rs[None, :]) & (kr[:, None] < rs[None, :] + 8)
                     & (kc[:, None] >= cs[None, :]) & (kc[:, None] < cs[None, :] + 16))
            rel_r = np.clip(kr[:, None] - r[None, :] + 7, 0, 14)
            rel_c = np.clip(kc[:, None] - c[None, :] + 15, 0, 30)
            for h in range(8):
                out[cls, :, h, s_, :] = np.where(inwin, rpb[h][rel_r, rel_c], NEG)
    return np.ascontiguousarray(out.reshape(5, 128, 8 * 6 * 128))


def run_l1(inp, x1, xc1):
    prog = _get_prog("l1")
    shared = {}
    W = _l1_weights(inp)
    c64a, s64a = _tables64(np.arange(SEQ), NCTX)
    maps = []
    inp2 = dict(inp)
    inp2["x"] = x1
    inp2["ctx"] = xc1
    for core in range(8):
        half = core % 2
        m = _core_inputs_l0(inp2, core, shared)
        m.update(W)
        m["wmod1"] = np.asarray(inp["l1_w_mod"], np.float32)
        m["wout1"] = np.asarray(inp["l1_w_out"], np.float32)
        m["w1_1"] = np.asarray(inp["l1_w1"], np.float32)
        m["w3_1"] = np.asarray(inp["l1_w3"], np.float32)
        m["w2_1"] = np.asarray(inp["l1_w2"], np.float32)
        pos_own = np.arange(half * TOWN, (half + 1) * TOWN)
        m["c64_own"], m["s64_own"] = shared.setdefault(("t64", half), _tables64(pos_own, NCTX))
        m["c64_all"], m["s64_all"] = c64a, s64a
        m["nab"] = shared.setdefault(("nab", half), _na_tables(inp["l1_rpb"], half))
        maps.append(m)
    res = run_bass_kernel_spmd(prog.nc, maps, core_ids=list(range(8)))
    out = np.zeros((4, SEQ, D), np.float32)
    for core in range(8):
        b, half = core // 2, core % 2
        out[b, half * TOWN:(half + 1) * TOWN] = res.results[core]["y_x"].T
    return out


def kernel(**inputs):
    inp = {k: np.asarray(v) for k, v in inputs.items()}
    x1, xc1 = run_l0(inp)
    return run_l1(inp, x1, xc1)
```

```python
import math
from contextlib import ExitStack
import numpy as np
import concourse.bass as bass
import concourse.mybir as mybir
from concourse.bass_utils import run_bass_kernel_spmd

F32 = mybir.dt.float32
BF16 = mybir.dt.bfloat16
AF = mybir.ActivationFunctionType
ALU = mybir.AluOpType

ENGS = ("pe", "act", "dve", "pool", "sp")


class Res:
    __slots__ = ("name", "w", "r")

    def __init__(self, name):
        self.name = name
        self.w = None
        self.r = {}


class Rec:
    __slots__ = ("eng", "fn", "deps", "inc", "val", "dma", "semkey", "idx")

    def __init__(self, eng, fn, dma=False, semkey=None):
        self.eng = eng
        self.fn = fn
        self.deps = []
        self.inc = False
        self.val = None
        self.dma = dma
        self.semkey = semkey
        self.idx = None


class Sched:
    def __init__(self, nc):
        self.nc = nc
        self.q = {e: [] for e in ENGS}
        self.n = 0
        self.pending = {e: [] for e in ENGS}
        self.last = {e: None for e in ENGS}
        self.open_dmas = []

    def res(self, name):
        return Res(name)

    def barrier(self):
        toks = [r for r in self.last.values() if r is not None] + list(self.open_dmas)
        self.open_dmas = []
        for e in ENGS:
            self.pending[e] = list(toks)

    def _add(self, rec, reads, writes):
        eng = rec.eng
        deps = []
        for r in reads:
            if r.w is not None:
                deps.append(r.w)
        for w in writes:
            if w.w is not None:
                deps.append(w.w)
            for e2, rr in w.r.items():
                if e2 == eng and not rr.dma:
                    continue
                deps.append(rr)
        if self.pending[eng]:
            deps = deps + [d for d in self.pending[eng] if d.dma or d.eng != eng]
            self.pending[eng] = []
        seen = set()
        for d in deps:
            if d is rec or id(d) in seen:
                continue
            if d.eng == eng == "pe" and not d.dma:
                continue
            if rec.dma and d.dma and d.semkey == rec.semkey:
                continue
            seen.add(id(d))
            rec.deps.append(d)
            d.inc = True
        for r in reads:
            r.r[("dma" + str(self.n)) if rec.dma else eng] = rec
        for w in writes:
            w.w = rec
            w.r = {}
        rec.idx = self.n
        self.n += 1
        self.q[eng].append(rec)
        if rec.dma:
            self.open_dmas.append(rec)
        else:
            self.last[eng] = rec
        return rec

    def op(self, eng, fn, reads=(), writes=()):
        return self._add(Rec(eng, fn), reads, writes)

    def dma(self, eng, out, in_, reads=(), writes=(), semkey=None, **kw):
        assert semkey is not None

        def fn(e, out=out, in_=in_, kw=kw):
            return e.dma_start(out=out, in_=in_, **kw)
        rec = Rec(eng, fn, dma=True, semkey=semkey)
        rec.inc = True
        return self._add(rec, reads, writes)

    def finalize(self, final_waits=()):
        import bisect
        nc = self.nc
        dmakeys = []
        for e in ENGS:
            c = 0
            for rec in self.q[e]:
                if rec.dma:
                    if rec.semkey not in dmakeys:
                        dmakeys.append(rec.semkey)
                elif rec.inc:
                    c += 1
                    rec.val = c
        dcount = {k: 0 for k in dmakeys}
        keyq = {}
        allrecs = sorted([r for e in ENGS for r in self.q[e] if r.dma], key=lambda r: r.idx)
        klist = {}
        for rec in allrecs:
            assert keyq.setdefault(rec.semkey, rec.eng) == rec.eng
            dcount[rec.semkey] += 16
            rec.val = dcount[rec.semkey]
            klist.setdefault(rec.semkey, []).append(rec)
        kidx = {k: [r.idx for r in v] for k, v in klist.items()}
        self.nsem = len(dmakeys) + len(ENGS)
        with ExitStack() as st:
            esem = {e: st.enter_context(nc.semaphore("s_" + e)) for e in ENGS}
            dsem = {k: st.enter_context(nc.semaphore("d_%d" % i)) for i, k in enumerate(dmakeys)}
            block = st.enter_context(nc.Block())

            def semof(rec):
                return dsem[rec.semkey] if rec.dma else esem[rec.eng]

            def valof(d, consumer_idx):
                if not d.dma:
                    return d.val
                i = bisect.bisect_left(kidx[d.semkey], consumer_idx) - 1
                return max(d.val, klist[d.semkey][i].val if i >= 0 else 0)

            def replay(eng_name, e):
                seen = {}
                for rec in self.q[eng_name]:
                    need = {}
                    for d in rec.deps:
                        s = semof(d)
                        k = id(s)
                        v = valof(d, rec.idx)
                        if seen.get(k, 0) >= v:
                            continue
                        if k not in need or need[k][1] < v:
                            need[k] = (s, v)
                    for k, (s, v) in need.items():
                        e.wait_ge(s, v)
                        seen[k] = v
                    ins = rec.fn(e)
                    if rec.dma:
                        ins.then_inc(dsem[rec.semkey], 16)
                    elif rec.inc:
                        ins.then_inc(esem[eng_name], 1)
                if eng_name == "sp":
                    for d in final_waits:
                        e.wait_ge(semof(d), valof(d, 1 << 60))

            @block.tensor
            def _(e):
                replay("pe", e)

            @block.scalar
            def _(e):
                replay("act", e)

            @block.vector
            def _(e):
                replay("dve", e)

            @block.gpsimd
            def _(e):
                replay("pool", e)

            @block.sync
            def _(e):
                replay("sp", e)


D = 1024
SEQ = 4096
NCTX = 256
TOWN = 2048
NOWN = TOWN + NCTX
NKEY = SEQ + NCTX
NKB = NKEY // 128
GRID_W = 64
EPS = 1e-6
FFN = 2816
NJ = FFN // 128
LAMBDA_INIT0 = 0.8 - 0.6 * math.exp(-0.3 * 0)

CL = {}
_off = 0
for _n, _w in [("cfm", 16), ("bmod0", 48), ("bmod1", 48), ("gattn0", 8), ("gffn0", 8), ("gattn1", 8),
               ("gffn1", 8), ("gfinal", 8), ("gsub", 1), ("gcq", 2), ("gckv", 1), ("lamv", 256),
               ("gqc", 1), ("gqcs", 1), ("gkc", 1), ("gkcs", 1)]:
    CL[_n] = (_off, _w)
    _off += _w
NCONST = _off

OWN_CHUNKS = [(0, 512, 0), (512, 512, 0), (1024, 512, 0), (1536, 512, 0), (2048, 256, 1)]


def _swap_halves(cols, group):
    c = np.asarray(cols).reshape(-1, 2, group // 2)
    return c[:, ::-1, :].reshape(-1)


class Prog:
    def __init__(self, mode):
        self.mode = mode
        self.nc = bass.Bass("TRN2", target_bir_lowering=False)
        self.S = Sched(self.nc)
        self.bank_rr = 0
        self.tmp_rr = 0

    def din(self, name, shape, dt=F32):
        return self.nc.dram_tensor(name, list(shape), dt, kind="ExternalInput").ap()

    def dout(self, name, shape, dt=F32):
        return self.nc.dram_tensor(name, list(shape), dt, kind="ExternalOutput").ap()

    def dscr(self, name, shape, dt=BF16):
        return self.nc.dram_tensor(name, list(shape), dt).ap()

    def sb(self, st, name, shape, dt):
        if st is not None and st is self.g:
            return st.enter_context(self.nc.sbuf_tensor(name, list(shape), dt))
        return self.aalloc(name, shape, dt)

    def areset(self, lo=0, hi=None):
        self.a_lo = lo
        self.a_hi = self.a_words if hi is None else hi

    def aalloc(self, name, shape, dt, top=False):
        n = 1
        for d_ in shape[1:]:
            n *= d_
        w = n if dt == F32 else (n + 1) // 2
        w = (w + 7) // 8 * 8
        if top:
            self.a_hi -= w
            off = self.a_hi
        else:
            off = self.a_lo
            self.a_lo += w
        assert self.a_lo <= self.a_hi, "arena overflow %s: lo=%d hi=%d" % (name, self.a_lo, self.a_hi)
        v = self.arena[:, off:off + w]
        if dt != F32:
            v = v.bitcast(dt)
        v = v[:, 0:n]
        if len(shape) == 3:
            v = v.rearrange("p (a b) -> p a b", a=shape[1])
        if shape[0] != 128:
            v = v[0:shape[0]]
        return v

    def bank(self):
        b = self.bank_rr
        self.bank_rr = (self.bank_rr + 1) % 5
        return b

    def mm(self, out, lhsT, rhs, start, stop, reads, writes):
        self.S.op("pe", lambda e: e.matmul(out, lhsT=lhsT, rhs=rhs, start=start, stop=stop), reads, writes)

    def act(self, out, in_, func, reads, writes, scale=None, bias=None):
        kw = {}
        if scale is not None:
            kw["scale"] = scale
        if bias is not None:
            kw["bias"] = bias
        self.S.op("act", lambda e: e.activation(out=out, in_=in_, func=func, **kw), reads, writes)

    def tt(self, out, in0, in1, op, reads, writes, eng="dve"):
        self.S.op(eng, lambda e: e.tensor_tensor(out=out, in0=in0, in1=in1, op=op), reads, writes)

    def stt(self, out, in0, scalar, in1, op0, op1, reads, writes):
        self.S.op("dve", lambda e: e.scalar_tensor_tensor(out=out, in0=in0, scalar=scalar, in1=in1, op0=op0, op1=op1),
                  reads, writes)

    def ts(self, out, in0, s1, s2, op0, op1, reads, writes, eng="dve"):
        if op1 is None:
            self.S.op(eng, lambda e: e.tensor_scalar(out=out, in0=in0, scalar1=s1, scalar2=None, op0=op0), reads, writes)
        else:
            self.S.op(eng, lambda e: e.tensor_scalar(out=out, in0=in0, scalar1=s1, scalar2=s2, op0=op0, op1=op1),
                      reads, writes)

    def copy(self, out, in_, reads, writes, eng="dve"):
        if eng == "act":
            self.S.op("act", lambda e: e.copy(out=out, in_=in_), reads, writes)
        else:
            self.S.op(eng, lambda e: e.tensor_copy(out=out, in_=in_), reads, writes)

    def memset(self, ap, val, writes, eng="dve"):
        self.S.op(eng, lambda e: e.memset(ap, val), (), writes)

    def build(self):
        nc, S, mode = self.nc, self.S, self.mode
        do0 = mode in ("l0", "fused")
        do1 = mode in ("l1", "fused")
        self.xT_own = self.din("xT_own", [D, TOWN])
        self.xT_all = self.din("xT_all", [D, SEQ])
        self.ctxT = self.din("ctxT", [D, NCTX])
        self.consts_d = self.din("consts", [128, NCONST])
        I = {}
        if do0:
            for n, shp in [("wmod0", [D, 6 * D]), ("wq0", [D, 1280]), ("wk0", [D, 1344]), ("wv0", [D, 512]),
                           ("wuq", [256, 1536]), ("wkn", [128, 512]), ("wvb", [128, 512]), ("wout0", [D, D]),
                           ("w1_0", [D, FFN]), ("w3_0", [D, FFN]), ("w2_0", [FFN, D]),
                           ("c64_own", [128, NOWN]), ("s64_own", [128, NOWN]), ("c64_all", [128, NKEY]),
                           ("s64_all", [128, NKEY]), ("c96_own", [96, NOWN]), ("s96_own", [96, NOWN]),
                           ("c96_all", [96, NKEY]), ("s96_all", [96, NKEY])]:
                I[n] = self.din(n, shp)
        if do1:
            for n, shp in [("wmod1", [D, 6 * D]), ("wq1", [D, 1536]), ("wk1", [D, 768]), ("wv1", [D, 640]),
                           ("wout1", [D, D]), ("w1_1", [D, FFN]), ("w3_1", [D, FFN]), ("w2_1", [FFN, D]),
                           ("nab", [5, 128, 8 * 6 * 128]), ("c64_own", [128, NOWN]), ("s64_own", [128, NOWN]),
                           ("c64_all", [128, NKEY]), ("s64_all", [128, NKEY])]:
                if n not in I:
                    I[n] = self.din(n, shp)
        self.I = I
        if mode == "l0":
            self.y_x = self.dout("y_x", [D, TOWN])
            self.y_c = self.dout("y_c", [D, NCTX])
        else:
            self.y_x = self.dout("y_x", [D, TOWN])
        self.qT_a = self.dscr("qT_a", [4, 128, NOWN])
        self.kT_a = self.dscr("kT_a", [4, 128, NKEY])
        self.v_a = self.dscr("v_a", [4, 128, NKB, 128])
        self.qT_b = self.dscr("qT_b", [8, 128, NOWN])
        self.kT_b = self.dscr("kT_b", [8, 128, NKEY])
        self.v_b = self.dscr("v_b", [8, 128, NKB, 64])
        self.qT_c = self.dscr("qT_c", [4, 128, TOWN])
        self.kT_c = self.dscr("kT_c", [128, NKEY])
        self.v_c = self.dscr("v_c", [2, 128, NKB, 64])
        self.qT_d = self.dscr("qT_d", [4, 128, TOWN])
        self.kT_d = self.dscr("kT_d", [4, 128, 22 * 128])
        self.v_d = self.dscr("v_d", [8, 128, 22, 64])
        if mode == "fused":
            self.gidx_d = self.nc.dram_tensor("gidx", [128, 2], mybir.dt.uint32, kind="ExternalInput").ap()
            self.xsend = self.nc.dram_tensor("xsend", [512, 2048], BF16, kind="Internal", addr_space="Local").ap()
            self.xgath = self.nc.dram_tensor("xgath", [8 * 512, 2048], BF16, kind="Internal", addr_space="Local").ap()

        with ExitStack() as g:
            self.g = g
            self.a_words = 31936
            self.arena = g.enter_context(nc.sbuf_tensor("arena", [128, self.a_words], F32))
            self.areset()
            self.xT = self.sb(g, "xT", [128, 8, NOWN], F32)
            self.r_x = [S.res("x%d" % i) for i in range(5)]
            self.cst = self.sb(g, "cst", [128, NCONST], F32)
            self.r_cst = S.res("cst")
            self.ones_bf = self.sb(g, "ones_bf", [128, 128], BF16)
            self.eps_t = self.sb(g, "eps_t", [128, 1], F32)
            self.csil = self.sb(g, "csil", [128, 8, 2], BF16)
            self.ada = [self.sb(g, "ada%d" % l, [128, 48, 2], F32) for l in range(2)]
            self.A1 = [self.sb(g, "A1_%d" % l, [128, 8, 2], F32) for l in range(2)]
            self.A2 = [self.sb(g, "A2_%d" % l, [128, 8, 2], F32) for l in range(2)]
            self.small = self.sb(g, "small", [128, 16], F32)
            self.r_small = S.res("small")
            self.r_ada = S.res("ada")
            self.ps = g.enter_context(nc.psum_tensor("ps", [128, 8, 512], F32))
            self.r_ps = [S.res("ps%d" % i) for i in range(8)]
            if mode == "fused":
                self.xstg = [self.sb(g, "xstg%d" % i, [128, 2048], BF16) for i in range(2)]
                self.gidx = self.sb(g, "gidx_sb", [128, 2], mybir.dt.uint32)
                self.r_gidx = S.res("gidx")
                S.dma("sp", self.gidx[:], self.gidx_d, writes=[self.r_gidx], semkey="gidx")

            for k in range(8):
                S.dma("sp", self.xT[:, k, 0:TOWN], self.xT_own[k * 128:(k + 1) * 128, :], writes=self.r_x[0:4], semkey="xin")
                S.dma("sp", self.xT[:, k, TOWN:NOWN], self.ctxT[k * 128:(k + 1) * 128, :], writes=[self.r_x[4]], semkey="xin")
            S.dma("sp", self.cst[:], self.consts_d, writes=[self.r_cst], semkey="cst")
            self.memset(self.ones_bf[:], 1.0, [self.r_small])
            self.memset(self.eps_t[:], EPS, [self.r_small])
            c0 = CL["cfm"][0]
            self.act(self.csil[:].rearrange("p k v -> p (k v)"), self.cst[:, c0:c0 + 16], AF.Silu, [self.r_cst], [self.r_small])

            if do0:
                self.ada_params(0)
                self.layer0()
            if do1:
                self.ada_params(1)
                self.layer1()
            fins = []
            if mode == "l0":
                S.barrier()
                for k in range(8):
                    fins.append(S.dma("sp", self.y_x[k * 128:(k + 1) * 128, :], self.xT[:, k, 0:TOWN], reads=self.r_x, semkey="yout"))
                    fins.append(S.dma("sp", self.y_c[k * 128:(k + 1) * 128, :], self.xT[:, k, TOWN:NOWN], reads=self.r_x, semkey="yout"))
            else:
                fins = self.final_norm_out()
            S.finalize(final_waits=fins)
        return nc

    def cs(self, name, j=None):
        o, w = CL[name]
        if j is None:
            return self.cst[:, o:o + w]
        return self.cst[:, o + j:o + j + 1]

    def ada_params(self, l):
        nc, S = self.nc, self.S
        wmod = self.I["wmod%d" % l].rearrange("(k p) c -> p k c", p=128)
        S.barrier()
        self.areset()
        with ExitStack() as st:
            wt = [self.sb(st, "wm%d" % i, [128, 8, 512], BF16) for i in range(2)]
            r_wt = [S.res("wm%d" % i) for i in range(2)]
            pb = 7
            for cc in range(12):
                b = cc % 2
                S.dma("pool", wt[b][:], wmod[:, :, cc * 512:(cc + 1) * 512], writes=[r_wt[b]], semkey="wm%d" % b)
                for jj in range(4):
                    j = cc * 4 + jj
                    for k in range(8):
                        self.mm(self.ps[:, pb, j * 2:j * 2 + 2], wt[b][:, k, jj * 128:(jj + 1) * 128], self.csil[:, k, :],
                                k == 0, k == 7, [r_wt[b], self.r_small], [self.r_ps[pb]])
            bm = self.cs("bmod%d" % l)
            pv = self.ps[:, pb, 0:96].rearrange("p (j v) -> p j v", v=2)
            for v in range(2):
                self.tt(self.ada[l][:, :, v], pv[:, :, v], bm, ALU.add, [self.r_ps[pb], self.r_cst], [self.r_ada])
            for v in range(2):
                self.stt(self.A1[l][:, :, v], self.ada[l][:, 8:16, v], 1.0, self.cs("gattn%d" % l), ALU.add, ALU.mult,
                         [self.r_ada, self.r_cst], [self.r_ada])
                self.stt(self.A2[l][:, :, v], self.ada[l][:, 32:40, v], 1.0, self.cs("gffn%d" % l), ALU.add, ALU.mult,
                         [self.r_ada, self.r_cst], [self.r_ada])

    def norm_chunk(self, src, r_src, T, A, sh, v, hT, r_hT, W):
        S = self.S
        pb = self.bank()
        for k in range(8):
            sq = W["sq"][k % 2]
            self.act(sq[:, :T], src(k), AF.Square, [r_src], [W["r_sq"][k % 2]])
            self.mm(self.ps[:, pb, :T], self.ones_bf[:], sq[:, :T], k == 0, k == 7, [W["r_sq"][k % 2], self.r_small], [self.r_ps[pb]])
        rstd = W["rstd"]
        self.act(rstd[:, :T], self.ps[:, pb, :T], AF.Ln, [self.r_ps[pb], self.r_small], [W["r_rstd"]], scale=1.0 / D, bias=self.eps_t[:])
        self.act(rstd[:, :T], rstd[:, :T], AF.Exp, [W["r_rstd"]], [W["r_rstd"]], scale=-0.5)
        sh_t, sh_base = sh
        for k in range(8):
            t = W["nt"][k % 2]
            self.stt(t[:, :T], src(k), A[:, k, v:v + 1], rstd[:, :T], ALU.mult, ALU.mult, [r_src, self.r_ada, W["r_rstd"]], [W["r_nt"][k % 2]])
            self.act(hT(k), t[:, :T], AF.Identity, [W["r_nt"][k % 2], self.r_ada], [r_hT], bias=sh_t[:, sh_base + k, v:v + 1])

    def alloc_work(self, st):
        S = self.S
        W = {}
        W["sq"] = [self.sb(st, "sq%d" % i, [128, 512], BF16) for i in range(2)]
        W["r_sq"] = [S.res("sq%d" % i) for i in range(2)]
        W["rstd"] = self.sb(st, "rstd", [128, 512], F32)
        W["r_rstd"] = S.res("rstd")
        W["nt"] = [self.sb(st, "nt%d" % i, [128, 512], F32) for i in range(2)]
        W["r_nt"] = [S.res("nt%d" % i) for i in range(2)]
        return W

    def proj(self, pb, M, T, w_of_k, h_of_k, nk, reads):
        for k in range(nk):
            self.mm(self.ps[0:M, pb, :T], w_of_k(k), h_of_k(k), k == 0, k == nk - 1, reads, [self.r_ps[pb]])

    def layer0(self):
        nc, S, I = self.nc, self.S, self.I
        l = 0
        ada, A1, A2 = self.ada[0], self.A1[0], self.A2[0]
        sm = self.small
        lo = CL["lamv"][0]
        self.areset()
        with ExitStack() as st:
            tmp = self.sb(st, "lamtmp", [128, 64], F32)
            r_t = S.res("lamtmp")
            for i in range(2):
                self.tt(tmp[:], self.cst[:, lo + i * 128:lo + i * 128 + 64], self.cst[:, lo + i * 128 + 64:lo + i * 128 + 128],
                        ALU.mult, [self.r_cst], [r_t])
                S.op("dve", lambda e, i=i: e.tensor_reduce(out=sm[:, i:i + 1], in_=tmp[:], axis=mybir.AxisListType.X, op=ALU.add),
                     [r_t], [self.r_small])
            self.act(sm[:, 0:2], sm[:, 0:2], AF.Exp, [self.r_small], [self.r_small])
            self.tt(sm[:, 2:3], sm[:, 1:2], sm[:, 0:1], ALU.subtract, [self.r_small], [self.r_small])
            self.ts(sm[:, 2:3], sm[:, 2:3], -LAMBDA_INIT0, None, ALU.add, None, [self.r_small], [self.r_small])
            self.ts(sm[:, 3:4], self.cs("gsub"), 1.0 - LAMBDA_INIT0, None, ALU.mult, None, [self.r_cst], [self.r_small])
            S.barrier()
        neglam = sm[:, 2:3]
        gsub8 = sm[:, 3:4]

        S.barrier()
        self.areset()
        with ExitStack() as st:
            W = self.alloc_work(st)
            wq = self.sb(st, "wq0", [128, 8, 1280], BF16)
            wk = self.sb(st, "wk0", [128, 8, 1344], BF16)
            wv = self.sb(st, "wv0", [128, 8, 512], BF16)
            wuq = self.sb(st, "wuq", [128, 2, 1536], BF16)
            wkn = self.sb(st, "wkn", [128, 512], BF16)
            wvb = self.sb(st, "wvb", [128, 512], BF16)
            r_w = S.res("wA")
            r_wq = [S.res("wq%d" % k) for k in range(8)]
            r_wk = [S.res("wk%d" % k) for k in range(8)]
            for k in range(8):
                S.dma("pool", wq[:, k, :], I["wq0"][k * 128:(k + 1) * 128, :], writes=[r_wq[k]], semkey="wq%d" % k)
            for k in range(2):
                S.dma("pool", wuq[:, k, :], I["wuq"][k * 128:(k + 1) * 128, :], writes=[r_w], semkey="wsm")
            for k in range(8):
                S.dma("pool", wk[:, k, :], I["wk0"][k * 128:(k + 1) * 128, :], writes=[r_wk[k]], semkey="wk%d" % k)
                S.dma("pool", wv[:, k, :], I["wv0"][k * 128:(k + 1) * 128, :], writes=[r_wk[k]], semkey="wk%d" % k)
            S.dma("pool", wkn[:], I["wkn"], writes=[r_w], semkey="wsm")
            S.dma("pool", wvb[:], I["wvb"], writes=[r_w], semkey="wsm")
            xs = self.sb(st, "xstage", [128, 8, 512], F32)
            r_xs = S.res("xs")
            hT = [self.sb(st, "hT%d" % i, [128, 8, 512], BF16) for i in range(2)]
            r_hT = [S.res("hT%d" % i) for i in range(2)]
            c64 = self.sb(st, "c64", [128, 512], F32)
            s64 = self.sb(st, "s64", [128, 512], F32)
            c96 = self.sb(st, "c96", [128, 512], F32)
            s96 = self.sb(st, "s96", [128, 512], F32)
            r_tab = S.res("tab")
            t1 = [self.sb(st, "t1_%d" % i, [128, 512], F32) for i in range(2)]
            t2 = [self.sb(st, "t2_%d" % i, [128, 512], F32) for i in range(2)]
            r_t1 = [S.res("t1_%d" % i) for i in range(2)]
            r_t2 = [S.res("t2_%d" % i) for i in range(2)]
            ost = [self.sb(st, "ost%d" % i, [128, 512], BF16) for i in range(3)]
            r_ost = [S.res("ost%d" % i) for i in range(3)]
            cqf = self.sb(st, "cqf", [128, 2, 512], F32)
            cqn = self.sb(st, "cqn", [128, 2, 512], BF16)
            r_cq = S.res("cq")
            vst = [self.sb(st, "vst%d" % i, [128, 512], BF16) for i in range(2)]
            r_vst = [S.res("vst%d" % i) for i in range(2)]
            cnt = {"o": 0, "t": 0, "v": 0, "h": 0}
            allw = r_wq + r_wk + [r_w]

            def rope_evac(pb1, pb2, M, T, ctab, stab, dst_aps, p0=0):
                i = cnt["t"] % 2
                cnt["t"] += 1
                o = cnt["o"] % 3
                cnt["o"] += 1
                self.tt(t1[i][p0:M, :T], self.ps[p0:M, pb1, :T], ctab[p0:M, :T], ALU.mult, [self.r_ps[pb1], r_tab], [r_t1[i]])
                self.tt(t2[i][p0:M, :T], self.ps[p0:M, pb2, :T], stab[p0:M, :T], ALU.mult, [self.r_ps[pb2], r_tab], [r_t2[i]])
                self.tt(ost[o][p0:M, :T], t1[i][p0:M, :T], t2[i][p0:M, :T], ALU.add, [r_t1[i], r_t2[i]], [r_ost[o]], eng="pool")
                for (dap, a, b) in dst_aps:
                    S.dma("sp", dap, ost[o][a:b, :T], reads=[r_ost[o]], semkey="ost%d" % o)

            def small_norm(pbs, nk, T, dim, gname, outf, outn):
                pb = self.bank()
                for j in range(nk):
                    self.copy(outf[:, j, :T], self.ps[:, pbs[j], :T], [self.r_ps[pbs[j]]], [r_cq], eng="act")
                    sq = W["sq"][j % 2]
                    self.act(sq[:, :T], self.ps[:, pbs[j], :T], AF.Square, [self.r_ps[pbs[j]]], [W["r_sq"][j % 2]])
                    self.mm(self.ps[:, pb, :T], self.ones_bf[:], sq[:, :T], j == 0, j == nk - 1, [W["r_sq"][j % 2], self.r_small], [self.r_ps[pb]])
                rstd = W["rstd"]
                self.act(rstd[:, :T], self.ps[:, pb, :T], AF.Ln, [self.r_ps[pb], self.r_small], [W["r_rstd"]], scale=1.0 / dim, bias=self.eps_t[:])
                self.act(rstd[:, :T], rstd[:, :T], AF.Exp, [W["r_rstd"]], [W["r_rstd"]], scale=-0.5)
                for j in range(nk):
                    self.stt(outn[:, j, :T], outf[:, j, :T], self.cs(gname, j), rstd[:, :T], ALU.mult, ALU.mult,
                             [r_cq, self.r_cst, W["r_rstd"]], [r_cq])

            def q_side(h, T, t0):
                hk = lambda k: h[:, k, :T]
                S.dma("sp", c64[:, :T], I["c64_own"][:, t0:t0 + T], writes=[r_tab], semkey="tab")
                S.dma("sp", s64[:, :T], I["s64_own"][:, t0:t0 + T], writes=[r_tab], semkey="tab")
                S.dma("sp", c96[0:96, :T], I["c96_own"][:, t0:t0 + T], writes=[r_tab], semkey="tab")
                S.dma("sp", s96[0:96, :T], I["s96_own"][:, t0:t0 + T], writes=[r_tab], semkey="tab")
                for hd in range(4):
                    pb1, pb2 = self.bank(), self.bank()
                    self.proj(pb1, 128, T, lambda k: wq[:, k, hd * 128:(hd + 1) * 128], hk, 8, allw + r_hT)
                    self.proj(pb2, 128, T, lambda k: wq[:, k, 512 + hd * 128:512 + (hd + 1) * 128], hk, 8, allw + r_hT)
                    rope_evac(pb1, pb2, 128, T, c64, s64, [(self.qT_a[hd, :, t0:t0 + T], 0, 128)])
                pbs = [self.bank(), self.bank()]
                for j in range(2):
                    self.proj(pbs[j], 128, T, lambda k: wq[:, k, 1024 + j * 128:1024 + (j + 1) * 128], hk, 8, allw + r_hT)
                small_norm(pbs, 2, T, 256, "gcq", cqf, cqn)
                for hd in range(8):
                    pb1, pb2 = self.bank(), self.bank()
                    self.proj(pb1, 96, T, lambda k: wuq[:, k, hd * 96:(hd + 1) * 96], lambda k: cqn[:, k, :T], 2, allw + [r_cq])
                    self.proj(pb2, 96, T, lambda k: wuq[:, k, 768 + hd * 96:768 + (hd + 1) * 96], lambda k: cqn[:, k, :T], 2, allw + [r_cq])
                    rope_evac(pb1, pb2, 96, T, c96, s96, [(self.qT_b[hd, 0:96, t0:t0 + T], 0, 96)])

            def k_side(h, T, k0):
                hk = lambda k: h[:, k, :T]
                S.dma("sp", c64[:, :T], I["c64_all"][:, k0:k0 + T], writes=[r_tab], semkey="tab")
                S.dma("sp", s64[:, :T], I["s64_all"][:, k0:k0 + T], writes=[r_tab], semkey="tab")
                S.dma("sp", c96[0:96, :T], I["c96_all"][:, k0:k0 + T], writes=[r_tab], semkey="tab")
                S.dma("sp", s96[0:96, :T], I["s96_all"][:, k0:k0 + T], writes=[r_tab], semkey="tab")
                for hd in range(4):
                    pb1, pb2 = self.bank(), self.bank()
                    self.proj(pb1, 128, T, lambda k: wk[:, k, hd * 128:(hd + 1) * 128], hk, 8, allw + r_hT)
                    self.proj(pb2, 128, T, lambda k: wk[:, k, 512 + hd * 128:512 + (hd + 1) * 128], hk, 8, allw + r_hT)
                    rope_evac(pb1, pb2, 128, T, c64, s64, [(self.kT_a[hd, :, k0:k0 + T], 0, 128)])
                pb1, pb2 = self.bank(), self.bank()
                self.proj(pb1, 96, T, lambda k: wk[:, k, 1152:1248], hk, 8, allw + r_hT)
                self.proj(pb2, 96, T, lambda k: wk[:, k, 1248:1344], hk, 8, allw + r_hT)
                rope_evac(pb1, pb2, 96, T, c96, s96, [(self.kT_b[hd, 64:96, k0:k0 + T], 64, 96) for hd in range(8)], p0=64)
                pbc = self.bank()
                self.proj(pbc, 128, T, lambda k: wk[:, k, 1024:1152], hk, 8, allw + r_hT)
                small_norm([pbc], 1, T, 128, "gckv", cqf, cqn)
                for hp in range(4):
                    pb = self.bank()
                    self.proj(pb, 128, T, lambda k: wkn[:, hp * 128:(hp + 1) * 128], lambda k: cqn[:, 0, :T], 1, allw + [r_cq])
                    o = cnt["o"] % 3
                    cnt["o"] += 1
                    self.copy(ost[o][:, :T], self.ps[:, pb, :T], [self.r_ps[pb]], [r_ost[o]])
                    S.dma("sp", self.kT_b[2 * hp, 0:64, k0:k0 + T], ost[o][0:64, :T], reads=[r_ost[o]], semkey="ost%d" % o)
                    S.dma("sp", self.kT_b[2 * hp + 1, 0:64, k0:k0 + T], ost[o][64:128, :T], reads=[r_ost[o]], semkey="ost%d" % o)
                for tb in range(T // 128):
                    kb = k0 // 128 + tb
                    pb = self.bank()
                    for k in range(8):
                        self.mm(self.ps[:, pb, :], h[:, k, tb * 128:(tb + 1) * 128], wv[:, k, :], k == 0, k == 7, allw + r_hT, [self.r_ps[pb]])
                    vi = cnt["v"] % 2
                    cnt["v"] += 1
                    self.copy(vst[vi][:], self.ps[:, pb, :], [self.r_ps[pb]], [r_vst[vi]], eng="act")
                    S.dma("sp", self.v_a[:, :, kb, :].rearrange("h p d -> p h d"), vst[vi][:].rearrange("p (h d) -> p h d", h=4),
                          reads=[r_vst[vi]], semkey="vst%d" % vi)
                    pb = self.bank()
                    self.mm(self.ps[:, pb, :], cqn[:, 0, tb * 128:(tb + 1) * 128], wvb[:], True, True, allw + [r_cq], [self.r_ps[pb]])
                    vi = cnt["v"] % 2
                    cnt["v"] += 1
                    self.copy(vst[vi][:], self.ps[:, pb, :], [self.r_ps[pb]], [r_vst[vi]], eng="act")
                    S.dma("sp", self.v_b[:, :, kb, :].rearrange("h p d -> p h d"), vst[vi][:].rearrange("p (h d) -> p h d", h=8),
                          reads=[r_vst[vi]], semkey="vst%d" % vi)

            for ci, (t0, T, v) in enumerate(OWN_CHUNKS):
                hi = cnt["h"] % 2
                cnt["h"] += 1
                self.norm_chunk(lambda k: self.xT[:, k, t0:t0 + T], self.r_x[ci], T, A1, (ada, 0), v,
                                lambda k: hT[hi][:, k, :T], r_hT[hi], W)
                q_side(hT[hi], T, t0)
                if v == 1:
                    k_side(hT[hi], T, SEQ)
            xall = self.xT_all.rearrange("(k p) t -> p k t", p=128)
            for c in range(8):
                S.dma("sp", xs[:], xall[:, :, c * 512:(c + 1) * 512], writes=[r_xs], semkey="xs")
                hi = cnt["h"] % 2
                cnt["h"] += 1
                self.norm_chunk(lambda k: xs[:, k, :], r_xs, 512, A1, (ada, 0), 0, lambda k: hT[hi][:, k, :], r_hT[hi], W)
                k_side(hT[hi], 512, c * 512)

        S.barrier()
        self.areset()
        with ExitStack() as st:
            attnT = self.sb(st, "attnT", [128, 8, NOWN], BF16)
            a_after_attn = self.a_lo
            r_attn = S.res("attnT")
            self.attention_l0(st, attnT, r_attn, neglam, gsub8)
            S.barrier()
            self.areset(lo=a_after_attn)
            hF = self.aalloc("hF", [128, 8, NOWN], BF16, top=True)
            self.hF_off = self.a_hi
            r_hF = [S.res("hF%d" % i) for i in range(5)]
            with ExitStack() as st2:
                W = self.alloc_work(st2)
                wo = self.sb(st2, "wout", [128, 8, D], BF16)
                r_wo = S.res("wout")
                for k in range(8):
                    S.dma("pool", wo[:, k, :], I["wout0"][k * 128:(k + 1) * 128, :], writes=[r_wo], semkey="wout")
                for ci, (t0, T, v) in enumerate(OWN_CHUNKS):
                    for m in range(8):
                        pb = self.bank()
                        self.proj(pb, 128, T, lambda k: wo[:, k, m * 128:(m + 1) * 128], lambda k: attnT[:, k, t0:t0 + T], 8, [r_wo, r_attn])
                        self.stt(self.xT[:, m, t0:t0 + T], self.ps[:, pb, :T], ada[:, 16 + m, v:v + 1], self.xT[:, m, t0:t0 + T],
                                 ALU.mult, ALU.add, [self.r_ps[pb], self.r_ada, self.r_x[ci]], [self.r_x[ci]])
                    self.norm_chunk(lambda k: self.xT[:, k, t0:t0 + T], self.r_x[ci], T, A2, (ada, 24), v,
                                    lambda k: hF[:, k, t0:t0 + T], r_hF[ci], W)
            self.ffn(st, hF, r_hF, I["w1_0"], I["w3_0"], I["w2_0"], ada, OWN_CHUNKS)

    def ffn(self, st_outer, hF, r_hF, w1d, w3d, w2d, ada, chunks):
        S = self.S
        S.barrier()
        self.areset(hi=self.hF_off)
        passes = [(0, 6), (6, 6), (12, 5), (17, 5)]
        w1v = w1d.rearrange("(k p) c -> p k c", p=128)
        w3v = w3d.rearrange("(k p) c -> p k c", p=128)
        w2v = w2d.rearrange("(j p) c -> p j c", p=128)
        with ExitStack() as st:
            w1 = [self.sb(st, "w1_%d" % i, [128, 8, 768], BF16) for i in range(2)]
            w3 = [self.sb(st, "w3_%d" % i, [128, 8, 768], BF16) for i in range(2)]
            w2 = [self.sb(st, "w2_%d" % i, [128, 6, D], BF16) for i in range(2)]
            r_wf = [S.res("wf%d" % i) for i in range(2)]
            gT = [self.sb(st, "gT%d" % i, [128, 6, 512], BF16) for i in range(2)]
            r_g = [S.res("gT%d" % i) for i in range(2)]
            sil = [self.sb(st, "sil%d" % i, [128, 512], BF16) for i in range(2)]
            r_sil = [S.res("sil%d" % i) for i in range(2)]
            gi = 0
            si = 0
            for pi, (j0, nj) in enumerate(passes):
                b = pi % 2
                for k in range(8):
                    S.dma("pool", w1[b][:, k, 0:nj * 128], w1v[:, k, j0 * 128:(j0 + nj) * 128], writes=[r_wf[b]], semkey="wf%d" % b)
                    S.dma("pool", w3[b][:, k, 0:nj * 128], w3v[:, k, j0 * 128:(j0 + nj) * 128], writes=[r_wf[b]], semkey="wf%d" % b)
                for jj in range(nj):
                    S.dma("pool", w2[b][:, jj, :], w2v[:, j0 + jj, :], writes=[r_wf[b]], semkey="wf%d" % b)
                for ci, (t0, T, v) in enumerate(chunks):
                    g = gT[gi % 2]
                    rg = r_g[gi % 2]
                    gi += 1
                    for jj in range(nj):
                        pb1, pb3 = self.bank(), self.bank()
                        self.proj(pb1, 128, T, lambda k: w1[b][:, k, jj * 128:(jj + 1) * 128], lambda k: hF[:, k, t0:t0 + T], 8, [r_wf[b], r_hF[ci]])
                        self.proj(pb3, 128, T, lambda k: w3[b][:, k, jj * 128:(jj + 1) * 128], lambda k: hF[:, k, t0:t0 + T], 8, [r_wf[b], r_hF[ci]])
                        s_ = sil[si % 2]
                        rs = r_sil[si % 2]
                        si += 1
                        self.act(s_[:, :T], self.ps[:, pb1, :T], AF.Silu, [self.r_ps[pb1]], [rs])
                        self.tt(g[:, jj, :T], self.ps[:, pb3, :T], s_[:, :T], ALU.mult, [self.r_ps[pb3], rs], [rg])
                    for m in range(8):
                        pb = self.bank()
                        self.proj(pb, 128, T, lambda jj: w2[b][:, jj, m * 128:(m + 1) * 128], lambda jj: g[:, jj, :T], nj, [r_wf[b], rg])
                        self.stt(self.xT[:, m, t0:t0 + T], self.ps[:, pb, :T], ada[:, 40 + m, v:v + 1], self.xT[:, m, t0:t0 + T],
                                 ALU.mult, ALU.add, [self.r_ps[pb], self.r_ada, self.r_x[ci]], [self.r_x[ci]])

    def attn_stream(self, steps, Pt, r_P, st_state):
        S = self.S
        groups = []
        i = 0
        while i < len(steps):
            if i + 1 < len(steps) and not steps[i]["last"]:
                groups.append([steps[i], steps[i + 1]])
                i += 2
            else:
                groups.append([steps[i]])
                i += 1
        n = len(groups)
        sbanks = [(0, 1), (2, 3)]

        def emit_s(gi):
            bk = sbanks[gi % 2]
            for j, s in enumerate(groups[gi]):
                s["sb"] = bk[j]
                self.mm(self.ps[:, bk[j], :s["T"]], s["kT"], s["qT"], True, True, s["reads"], [self.r_ps[bk[j]]])
        emit_s(0)
        for gi in range(n):
            g = groups[gi]
            T = g[0]["T"]
            ng = len(g)
            p = Pt[gi % len(Pt)]
            rp = r_P[gi % len(Pt)]
            b0 = g[0]["sb"]
            self.act(p[:, 0:ng, :T], self.ps[:, b0:b0 + ng, :T], AF.Exp, [self.r_ps[b0 + j] for j in range(ng)], [rp], scale=g[0]["scale"])
            if gi + 1 < n:
                emit_s(gi + 1)
            for j, s in enumerate(g):
                self.mm(self.ps[0:s["M"], s["ob"], :T], s["v"], p[:, j, :T], s["first"], s["last"], s["reads"] + [rp], [self.r_ps[s["ob"]]])
            for j, s in enumerate(g):
                if s["zb"] is not None:
                    self.mm(self.ps[:, s["zb"], :T], self.ones_bf[:], p[:, j, :T], s["first"], s["last"], [rp, self.r_small], [self.r_ps[s["zb"]]])
            if g[-1]["last"] and g[-1]["fin"] is not None:
                g[-1]["fin"]()

    def attention_l0(self, st, attnT, r_attn, neglam, gsub8):
        S = self.S
        Qz = [[self.sb(st, "Qz%d_%d" % (i, s_), [128, NOWN], BF16) for s_ in range(2)] for i in range(2)]
        Kb = [self.sb(st, "Kb%d" % i, [128, NKEY], BF16) for i in range(2)]
        Vb = [self.sb(st, "Vb%d" % i, [128, NKB, 192], BF16) for i in range(2)]
        r_set = [S.res("set%d" % i) for i in range(2)]
        Pt = [self.sb(st, "Pt%d" % i, [128, 2, 512], BF16) for i in range(3)]
        r_P = [S.res("Pt%d" % i) for i in range(3)]
        for i in range(2):
            for s_ in range(2):
                self.memset(Qz[i][s_][:], 0.0, [r_set[i]])
        W = self.alloc_work(st)
        o1n = self.sb(st, "o1n", [128, 512], F32)
        comb = self.sb(st, "comb", [128, 512], F32)
        rz = [self.sb(st, "rz%d" % i, [128, 512], F32) for i in range(2)]
        osb = [self.sb(st, "osb%d" % i, [128, 512], F32) for i in range(2)]
        r_fin = [S.res("fin%d" % i) for i in range(2)]
        r_o1n = S.res("o1n")
        r_comb = S.res("comb")
        heads = [("a", h) for h in range(4)] + [("b", h) for h in range(8)]

        def load(idx):
            kind, h = heads[idx]
            b = idx % 2
            if kind == "a":
                S.dma("sp", Qz[b][0][0:64, :], self.qT_a[h, 0:64, :], writes=[r_set[b]], semkey="set%d" % b)
                S.dma("sp", Qz[b][1][64:128, :], self.qT_a[h, 64:128, :], writes=[r_set[b]], semkey="set%d" % b)
                S.dma("sp", Kb[b][:], self.kT_a[h], writes=[r_set[b]], semkey="set%d" % b)
                S.dma("sp", Vb[b][:, :, 0:128], self.v_a[h], writes=[r_set[b]], semkey="set%d" % b)
            else:
                if h < 2:
                    self.memset(Vb[b][:, :, 0:64], 1.0, [r_set[b]])
                    self.memset(Vb[b][:, :, 128:192], 1.0, [r_set[b]])
                S.dma("sp", Qz[b][0][0:96, :], self.qT_b[h, 0:96, :], writes=[r_set[b]], semkey="set%d" % b)
                S.dma("sp", Kb[b][0:96, :], self.kT_b[h, 0:96, :], writes=[r_set[b]], semkey="set%d" % b)
                S.dma("sp", Vb[b][:, :, 64:128], self.v_b[h], writes=[r_set[b]], semkey="set%d" % b)

        accs = [(4, 6), (5, 7)]
        state = {"acc": 0, "fin": 0}
        load(0)
        for idx, (kind, h) in enumerate(heads):
            b = idx % 2
            if idx + 1 < len(heads):
                load(idx + 1)
            steps = []
            if kind == "a":
                for (t0, T, v) in OWN_CHUNKS:
                    kbs = list(range(NKB)) if v == 0 else [32, 33]
                    for s_ in range(2):
                        ob, zb = accs[state["acc"] % 2]
                        state["acc"] += 1
                        for n_, kb in enumerate(kbs):
                            step = dict(kT=Kb[b][:, kb * 128:(kb + 1) * 128], qT=Qz[b][s_][:, t0:t0 + T],
                                        T=T, scale=0.125, v=Vb[b][:, kb, 0:128], M=128, ob=ob, zb=zb, first=(n_ == 0),
                                        last=(n_ == len(kbs) - 1), reads=[r_set[b]], fin=None)
                            steps.append(step)

                        def fin(ob=ob, zb=zb, s_=s_, t0=t0, T=T, h=h):
                            fi = state["fin"] % 2
                            state["fin"] += 1
                            S.op("dve", lambda e: e.reciprocal(out=rz[fi][:, :T], in_=self.ps[:, zb, :T]), [self.r_ps[zb]], [r_fin[fi]])
                            if s_ == 0:
                                self.tt(o1n[:, :T], self.ps[:, ob, :T], rz[fi][:, :T], ALU.mult, [self.r_ps[ob], r_fin[fi]], [r_o1n])
                            else:
                                self.tt(osb[fi][:, :T], self.ps[:, ob, :T], rz[fi][:, :T], ALU.mult, [self.r_ps[ob], r_fin[fi]], [r_fin[fi]])
                                self.stt(comb[:, :T], osb[fi][:, :T], neglam, o1n[:, :T], ALU.mult, ALU.add,
                                         [r_fin[fi], r_o1n, self.r_small], [r_comb])
                                sq = W["sq"][0]
                                self.tt(sq[:, :T], comb[:, :T], comb[:, :T], ALU.mult, [r_comb], [W["r_sq"][0]])
                                self.mm(self.ps[:, zb, :T], self.ones_bf[:], sq[:, :T], True, True, [W["r_sq"][0], self.r_small], [self.r_ps[zb]])
                                rstd = W["rstd"]
                                self.act(rstd[:, :T], self.ps[:, zb, :T], AF.Ln, [self.r_ps[zb], self.r_small], [W["r_rstd"]], scale=1.0 / 128, bias=self.eps_t[:])
                                self.act(rstd[:, :T], rstd[:, :T], AF.Exp, [W["r_rstd"]], [W["r_rstd"]], scale=-0.5)
                                self.stt(attnT[:, h, t0:t0 + T], comb[:, :T], gsub8, rstd[:, :T], ALU.mult, ALU.mult,
                                         [r_comb, W["r_rstd"], self.r_small], [r_attn])
                        steps[-1]["fin"] = fin
            else:
                po, pz = (0, 64) if h % 2 == 0 else (64, 0)
                vsl = (64, 192) if h % 2 == 0 else (0, 128)
                for (t0, T, v) in OWN_CHUNKS:
                    kbs = list(range(NKB)) if v == 0 else [32, 33]
                    ob, _ = accs[state["acc"] % 2]
                    state["acc"] += 1
                    for n_, kb in enumerate(kbs):
                        step = dict(kT=Kb[b][0:96, kb * 128:(kb + 1) * 128], qT=Qz[b][0][0:96, t0:t0 + T], T=T, scale=96 ** -0.5,
                                    v=Vb[b][:, kb, vsl[0]:vsl[1]], M=128, ob=ob, zb=None, first=(n_ == 0), last=(n_ == len(kbs) - 1),
                                    reads=[r_set[b]], fin=None)
                        steps.append(step)

                    def fin(ob=ob, t0=t0, T=T, h=h, po=po, pz=pz):
                        fi = state["fin"] % 2
                        state["fin"] += 1
                        self.copy(osb[fi][po:po + 64, :T], self.ps[po:po + 64, ob, :T], [self.r_ps[ob]], [r_fin[fi]])
                        S.op("dve", lambda e: e.reciprocal(out=self.ps[pz:pz + 64, ob, :T], in_=self.ps[pz:pz + 64, ob, :T]),
                             [self.r_ps[ob]], [self.r_ps[ob]])
                        self.tt(attnT[po:po + 64, 4 + h // 2, t0:t0 + T], self.ps[pz:pz + 64, ob, :T], osb[fi][po:po + 64, :T], ALU.mult,
                                [self.r_ps[ob], r_fin[fi]], [r_attn])
                    steps[-1]["fin"] = fin
            self.attn_stream(steps, Pt, r_P, state)

    def layer1(self):
        nc, S, I = self.nc, self.S, self.I
        ada, A1, A2 = self.ada[1], self.A1[1], self.A2[1]
        fused = self.mode == "fused"
        LAT = OWN_CHUNKS[:4]
        S.barrier()
        self.areset()
        with ExitStack() as st:
            W = self.alloc_work(st)
            bd = self.sb(st, "bd_ones", [128, 128], BF16)
            r_bd = S.res("bd")
            self.memset(bd[:], 0.0, [r_bd])
            self.memset(bd[0:64, 0:64], 1.0, [r_bd])
            self.memset(bd[64:128, 64:128], 1.0, [r_bd])
            wq = self.sb(st, "wq1", [128, 8, 1536], BF16)
            wk = self.sb(st, "wk1", [128, 8, 768], BF16)
            wv = self.sb(st, "wv1", [128, 8, 640], BF16)
            r_w = S.res("wA1")
            for k in range(8):
                S.dma("pool", wq[:, k, :], I["wq1"][k * 128:(k + 1) * 128, :], writes=[r_w], semkey="w1a")
                S.dma("pool", wk[:, k, :], I["wk1"][k * 128:(k + 1) * 128, :], writes=[r_w], semkey="w1a")
                S.dma("pool", wv[:, k, :], I["wv1"][k * 128:(k + 1) * 128, :], writes=[r_w], semkey="w1a")
            xs = None if fused else self.sb(st, "xstage", [128, 8, 512], F32)
            r_xs = S.res("xs")
            hT = [self.sb(st, "hT%d" % i, [128, 8, 512], BF16) for i in range(2)]
            r_hT = [S.res("hT%d" % i) for i in range(2)]
            c64 = self.sb(st, "c64", [128, 512], F32)
            s64 = self.sb(st, "s64", [128, 512], F32)
            r_tab = S.res("tab")
            t1 = [self.sb(st, "t1_%d" % i, [128, 512], F32) for i in range(2)]
            t2 = [self.sb(st, "t2_%d" % i, [128, 512], F32) for i in range(2)]
            r_t1 = [S.res("t1_%d" % i) for i in range(2)]
            r_t2 = [S.res("t2_%d" % i) for i in range(2)]
            ost = [self.sb(st, "ost%d" % i, [128, 512], BF16) for i in range(4)]
            r_ost = [S.res("ost%d" % i) for i in range(4)]
            vst = [self.sb(st, "vst%d" % i, [128, 640], BF16) for i in range(2)]
            r_vst = [S.res("vst%d" % i) for i in range(2)]
            cnt = {"o": 0, "t": 0, "v": 0, "h": 0}
            allw = [r_w]

            def load_tabs(T, which, t0):
                S.dma("sp", c64[:, :T], I["c64_" + which][:, t0:t0 + T], writes=[r_tab], semkey="tab")
                S.dma("sp", s64[:, :T], I["s64_" + which][:, t0:t0 + T], writes=[r_tab], semkey="tab")

            def normrope(pb1, pb2, T, gname, dsts):
                i = cnt["t"] % 2
                cnt["t"] += 1
                o = cnt["o"] % 4
                cnt["o"] += 1
                pbs = self.bank()
                sq = W["sq"][0]
                self.act(sq[:, :T], self.ps[:, pb1, :T], AF.Square, [self.r_ps[pb1]], [W["r_sq"][0]])
                self.mm(self.ps[:, pbs, :T], bd[:], sq[:, :T], True, True, [W["r_sq"][0], r_bd], [self.r_ps[pbs]])
                rstd = W["rstd"]
                self.act(rstd[:, :T], self.ps[:, pbs, :T], AF.Ln, [self.r_ps[pbs], self.r_small], [W["r_rstd"]], scale=1.0 / 64, bias=self.eps_t[:])
                self.act(rstd[:, :T], rstd[:, :T], AF.Exp, [W["r_rstd"]], [W["r_rstd"]], scale=-0.5)
                self.stt(t1[i][:, :T], self.ps[:, pb1, :T], self.cs(gname), rstd[:, :T], ALU.mult, ALU.mult,
                         [self.r_ps[pb1], self.r_cst, W["r_rstd"]], [r_t1[i]])
                self.stt(t2[i][:, :T], self.ps[:, pb2, :T], self.cs(gname + "s"), rstd[:, :T], ALU.mult, ALU.mult,
                         [self.r_ps[pb2], self.r_cst, W["r_rstd"]], [r_t2[i]])
                self.tt(t1[i][:, :T], t1[i][:, :T], c64[:, :T], ALU.mult, [r_t1[i], r_tab], [r_t1[i]])
                self.tt(t2[i][:, :T], t2[i][:, :T], s64[:, :T], ALU.mult, [r_t2[i], r_tab], [r_t2[i]], eng="pool")
                self.tt(ost[o][:, :T], t1[i][:, :T], t2[i][:, :T], ALU.add, [r_t1[i], r_t2[i]], [r_ost[o]])
                for dap in dsts:
                    S.dma("sp", dap, ost[o][:, :T], reads=[r_ost[o]], semkey="ost%d" % o)

            def plain_evac(pb, T, dsts, c0=0):
                o = cnt["o"] % 4
                cnt["o"] += 1
                self.copy(ost[o][:, :T], self.ps[:, pb, c0:c0 + T], [self.r_ps[pb]], [r_ost[o]], eng="act")
                for (dap, a, b_) in dsts:
                    S.dma("sp", dap, ost[o][:, a:b_], reads=[r_ost[o]], semkey="ost%d" % o)

            def kd_vd(h, c0, T, L0, halo=None):
                xv = None
                if halo is not None:
                    xv = self.xsend[384:512, :].rearrange("p (h b d) -> p h b d", h=8, b=4)
                for j in range(4):
                    pb = self.bank()
                    self.proj(pb, 128, T, lambda k: wk[:, k, 256 + j * 128:256 + (j + 1) * 128], lambda k: h[:, k, c0:c0 + T], 8, allw + r_hT)
                    dsts = [(self.kT_d[j, :, L0 * 128:L0 * 128 + T], 0, T)]
                    if halo is not None:
                        dsts.append((self.xsend[256:384, j * 512 + halo[1] * 128:j * 512 + halo[1] * 128 + 256], halo[0], halo[0] + 256))
                    plain_evac(pb, T, dsts)
                for tb in range(T // 128):
                    pb = self.bank()
                    for k in range(8):
                        self.mm(self.ps[:, pb, :], h[:, k, c0 + tb * 128:c0 + (tb + 1) * 128], wv[:, k, 128:640], k == 0, k == 7, allw + r_hT, [self.r_ps[pb]])
                    vi = cnt["v"] % 2
                    cnt["v"] += 1
                    self.copy(vst[vi][:, 0:512], self.ps[:, pb, :], [self.r_ps[pb]], [r_vst[vi]], eng="act")
                    S.dma("sp", self.v_d[:, :, L0 + tb, :].rearrange("h p d -> p h d"), vst[vi][:, 0:512].rearrange("p (h d) -> p h d", h=8),
                          reads=[r_vst[vi]], semkey="vst%d" % vi)
                    if halo is not None and halo[0] // 128 <= tb < halo[0] // 128 + 2:
                        S.dma("sp", xv[:, :, halo[1] + tb - halo[0] // 128, :], vst[vi][:, 0:512].rearrange("p (h d) -> p h d", h=8),
                              reads=[r_vst[vi]], semkey="vst%d" % vi)

            def kc_vc(h, T, k0, own_t0=None):
                hk = lambda k: h[:, k, :T]
                if own_t0 is None:
                    load_tabs(T, "all", k0)
                    kdst = self.kT_c[:, k0:k0 + T]
                else:
                    load_tabs(T, "own", own_t0)
                    kdst = self.xsend[0:128, own_t0:own_t0 + T]
                    xvc = self.xsend[128:256, :].rearrange("p (g kb d) -> p g kb d", g=2, kb=16)
                pb1, pb2 = self.bank(), self.bank()
                self.proj(pb1, 128, T, lambda k: wk[:, k, 0:128], hk, 8, allw + r_hT)
                self.proj(pb2, 128, T, lambda k: wk[:, k, 128:256], hk, 8, allw + r_hT)
                normrope(pb1, pb2, T, "gkc", [kdst])
                for tb in range(T // 128):
                    pb = self.bank()
                    for k in range(8):
                        self.mm(self.ps[:, pb, 0:128], h[:, k, tb * 128:(tb + 1) * 128], wv[:, k, 0:128], k == 0, k == 7, allw + r_hT, [self.r_ps[pb]])
                    vi = cnt["v"] % 2
                    cnt["v"] += 1
                    self.copy(vst[vi][:, 512:640], self.ps[:, pb, 0:128], [self.r_ps[pb]], [r_vst[vi]], eng="act")
                    if own_t0 is None:
                        vdst = self.v_c[:, :, k0 // 128 + tb, :].rearrange("g p d -> p g d")
                    else:
                        vdst = xvc[:, :, own_t0 // 128 + tb, :]
                    S.dma("sp", vdst, vst[vi][:, 512:640].rearrange("p (g d) -> p g d", g=2),
                          reads=[r_vst[vi]], semkey="vst%d" % vi)

            def q_side(h, T, t0):
                hk = lambda k: h[:, k, :T]
                load_tabs(T, "own", t0)
                for j in range(4):
                    pb1, pb2 = self.bank(), self.bank()
                    self.proj(pb1, 128, T, lambda k: wq[:, k, j * 128:(j + 1) * 128], hk, 8, allw + r_hT)
                    self.proj(pb2, 128, T, lambda k: wq[:, k, 512 + j * 128:512 + (j + 1) * 128], hk, 8, allw + r_hT)
                    normrope(pb1, pb2, T, "gqc", [self.qT_c[j, :, t0:t0 + T]])
                for j in range(4):
                    pb = self.bank()
                    self.proj(pb, 128, T, lambda k: wq[:, k, 1024 + j * 128:1024 + (j + 1) * 128], hk, 8, allw + r_hT)
                    plain_evac(pb, T, [(self.qT_d[j, :, t0:t0 + T], 0, T)])

            for ci, (t0, T, v) in enumerate(OWN_CHUNKS):
                hi = cnt["h"] % 2
                cnt["h"] += 1
                self.norm_chunk(lambda k: self.xT[:, k, t0:t0 + T], self.r_x[ci], T, A1, (ada, 0), v,
                                lambda k: hT[hi][:, k, :T], r_hT[hi], W)
                if v == 0:
                    q_side(hT[hi], T, t0)
                    halo = None
                    if fused and ci == 0:
                        halo = (0, 0)
                    if fused and ci == 3:
                        halo = (256, 2)
                    kd_vd(hT[hi], 0, T, 2 + t0 // 128, halo=halo)
                    if fused:
                        kc_vc(hT[hi], T, None, own_t0=t0)
                else:
                    kc_vc(hT[hi], T, SEQ)
                    kd_vd(hT[hi], 0, T, 20)
            xall = self.xT_all.rearrange("(k p) t -> p k t", p=128)
            import os as _os
            if fused and _os.environ.get("MK_SKIPX") != "1":
                S.barrier()
                xs_, xg_ = self.xsend, self.xgath
                r_g = S.res("xgath")
                r_stg = S.res("xstg")
                if _os.environ.get("MK_X", "") != "nocc":
                    S.op("pool", lambda e: e.collective_compute("AllGather", ALU.bypass, replica_groups=[list(range(8))],
                                                                ins=[xs_], outs=[xg_]), [], [r_g])
                r_st = [S.res("xstg%d" % i) for i in range(2)]
                n_ = 0
                for hp in range(2):
                    for rg in range(4):
                        sg = self.xstg[n_ % 2]
                        rs_ = r_st[n_ % 2]
                        sk = "xcp%d" % (n_ % 2)
                        n_ += 1

                        def ind(e, hp=hp, rg=rg, sg=sg):
                            return e.indirect_dma_start(out=sg[:], out_offset=None, in_=xg_,
                                                        in_offset=bass.IndirectOffsetOnAxis(ap=self.gidx[:, hp:hp + 1], axis=0),
                                                        element_offset=rg * 128 * 2048)
                        rec = Rec("pool", ind, dma=True, semkey="xind%d" % (n_ % 2))
                        rec.inc = True
                        S._add(rec, [r_g, self.r_gidx], [rs_])
                        if rg == 0:
                            S.dma("sp", self.kT_c[:, hp * 2048:(hp + 1) * 2048], sg[:], reads=[rs_], semkey=sk)
                        elif rg == 1:
                            for g_ in range(2):
                                S.dma("sp", self.v_c[g_, :, hp * 16:(hp + 1) * 16, :],
                                      sg[:, g_ * 1024:(g_ + 1) * 1024].rearrange("p (kb d) -> p kb d", kb=16), reads=[rs_], semkey=sk)
                        elif rg == 2:
                            for j in range(4):
                                if hp == 0:
                                    S.dma("sp", self.kT_d[j, :, 0:256], sg[:, j * 512 + 256:j * 512 + 512], reads=[rs_], semkey=sk)
                                else:
                                    S.dma("sp", self.kT_d[j, :, 18 * 128:20 * 128], sg[:, j * 512:j * 512 + 256], reads=[rs_], semkey=sk)
                        else:
                            for h_ in range(8):
                                if hp == 0:
                                    S.dma("sp", self.v_d[h_, :, 0:2, :], sg[:, h_ * 256 + 128:h_ * 256 + 256].rearrange("p (b d) -> p b d", b=2),
                                          reads=[rs_], semkey=sk)
                                else:
                                    S.dma("sp", self.v_d[h_, :, 18:20, :], sg[:, h_ * 256:h_ * 256 + 128].rearrange("p (b d) -> p b d", b=2),
                                          reads=[rs_], semkey=sk)
            for c in range(0 if fused else 8):
                hi = cnt["h"] % 2
                cnt["h"] += 1
                S.dma("sp", xs[:], xall[:, :, c * 512:(c + 1) * 512], writes=[r_xs], semkey="xs")
                self.norm_chunk(lambda k: xs[:, k, :], r_xs, 512, A1, (ada, 0), 0, lambda k: hT[hi][:, k, :], r_hT[hi], W)
                kc_vc(hT[hi], 512, c * 512)
                if c == 3:
                    kd_vd(hT[hi], 256, 256, 0)
                if c == 4:
                    kd_vd(hT[hi], 0, 256, 18)

        S.barrier()
        self.areset()
        attnT = self.aalloc("attnT1", [128, 8, TOWN], BF16)
        a_after_attn = self.a_lo
        r_attn = S.res("attnT1")
        self.attention_gqa(attnT, r_attn)
        S.barrier()
        self.areset(lo=a_after_attn)
        self.attention_na(attnT, r_attn)
        S.barrier()
        self.areset(lo=a_after_attn)
        hF = self.aalloc("hF1", [128, 8, TOWN], BF16, top=True)
        self.hF_off = self.a_hi
        r_hF = [S.res("hF%d" % i) for i in range(4)]
        W = self.alloc_work(None)
        wo = self.aalloc("wout1", [128, 8, D], BF16)
        r_wo = S.res("wout1")
        for k in range(8):
            S.dma("pool", wo[:, k, :], I["wout1"][k * 128:(k + 1) * 128, :], writes=[r_wo], semkey="wout")
        for ci, (t0, T, v) in enumerate(LAT):
            for m in range(8):
                pb = self.bank()
                self.proj(pb, 128, T, lambda k: wo[:, k, m * 128:(m + 1) * 128], lambda k: attnT[:, k, t0:t0 + T], 8, [r_wo, r_attn])
                self.stt(self.xT[:, m, t0:t0 + T], self.ps[:, pb, :T], ada[:, 16 + m, v:v + 1], self.xT[:, m, t0:t0 + T],
                         ALU.mult, ALU.add, [self.r_ps[pb], self.r_ada, self.r_x[ci]], [self.r_x[ci]])
            self.norm_chunk(lambda k: self.xT[:, k, t0:t0 + T], self.r_x[ci], T, A2, (ada, 24), v,
                            lambda k: hF[:, k, t0:t0 + T], r_hF[ci], W)
        self.ffn(None, hF, r_hF, I["w1_1"], I["w3_1"], I["w2_1"], ada, LAT)

    def attention_gqa(self, attnT, r_attn):
        S = self.S
        Qc = [[self.aalloc("Qc%d_%d" % (j, g_), [128, TOWN], BF16) for g_ in range(2)] for j in range(4)]
        Kc = self.aalloc("Kc", [128, NKEY], BF16)
        Vc = [self.aalloc("Vc%d" % g, [128, NKB, 192], BF16) for g in range(2)]
        r_in = S.res("gqa_in")
        Pt = [self.aalloc("Pt%d" % i, [128, 2, 512], BF16) for i in range(3)]
        r_P = [S.res("Pt%d" % i) for i in range(3)]
        osb = [self.aalloc("osb%d" % i, [128, 512], F32) for i in range(2)]
        r_fin = [S.res("fin%d" % i) for i in range(2)]
        for j in range(4):
            for g in range(2):
                self.memset(Qc[j][g][:], 0.0, [r_in])
        for g in range(2):
            self.memset(Vc[g][:, :, 0:64], 1.0, [r_in])
            self.memset(Vc[g][:, :, 128:192], 1.0, [r_in])
        S.dma("sp", Kc[:], self.kT_c, writes=[r_in], semkey="gin")
        for j in range(4):
            for g in range(2):
                S.dma("sp", Qc[j][g][g * 64:(g + 1) * 64, :], self.qT_c[j, g * 64:(g + 1) * 64, :], writes=[r_in], semkey="gin")
        for g in range(2):
            S.dma("sp", Vc[g][:, :, 64:128], self.v_c[g], writes=[r_in], semkey="gin")
        accs = [4, 5]
        state = {"acc": 0, "fin": 0}
        steps = []
        for g in range(2):
            for j in range(4):
                h = g * 4 + j
                po, pz = (0, 64) if h % 2 == 0 else (64, 0)
                vsl = (64, 192) if h % 2 == 0 else (0, 128)
                for (t0, T, v) in OWN_CHUNKS[:4]:
                    ob = accs[state["acc"] % 2]
                    state["acc"] += 1
                    for kb in range(NKB):
                        steps.append(dict(kT=Kc[:, kb * 128:(kb + 1) * 128], qT=Qc[j][g][:, t0:t0 + T],
                                          T=T, scale=0.125, v=Vc[g][:, kb, vsl[0]:vsl[1]], M=128, ob=ob, zb=None, first=(kb == 0),
                                          last=(kb == NKB - 1), reads=[r_in], fin=None))

                    def fin(ob=ob, t0=t0, T=T, h=h, po=po, pz=pz):
                        fi = state["fin"] % 2
                        state["fin"] += 1
                        self.copy(osb[fi][po:po + 64, :T], self.ps[po:po + 64, ob, :T], [self.r_ps[ob]], [r_fin[fi]])
                        S.op("dve", lambda e: e.reciprocal(out=self.ps[pz:pz + 64, ob, :T], in_=self.ps[pz:pz + 64, ob, :T]),
                             [self.r_ps[ob]], [self.r_ps[ob]])
                        self.tt(attnT[po:po + 64, h // 2, t0:t0 + T], self.ps[pz:pz + 64, ob, :T], osb[fi][po:po + 64, :T], ALU.mult,
                                [self.r_ps[ob], r_fin[fi]], [r_attn])
                    steps[-1]["fin"] = fin
        self.attn_stream(steps, Pt, r_P, state)

    def attention_na(self, attnT, r_attn):
        S = self.S
        r_pair = [S.res("napair%d" % i) for i in range(2)]
        r_head = [S.res("nahead%d" % i) for i in range(2)]
        Qd = [[self.aalloc("Qd%d_%d" % (i, e_), [128, TOWN], BF16) for e_ in range(2)] for i in range(2)]
        for i in range(2):
            for e_ in range(2):
                self.memset(Qd[i][e_][:], 0.0, [r_pair[i]])
        Kd = [self.aalloc("Kd%d" % i, [128, 22 * 128], BF16) for i in range(2)]
        Vd = [self.aalloc("Vd%d" % i, [128, 22, 192], BF16) for i in range(2)]
        tab = [self.aalloc("natab%d" % i, [128, 5, 768], F32) for i in range(2)]
        tmp = [self.aalloc("natmp%d" % i, [128, 768], F32) for i in range(2)]
        r_tmp = [S.res("natmp%d" % i) for i in range(2)]
        P = [self.aalloc("naP%d" % i, [128, 1024], BF16) for i in range(2)]
        r_P = [S.res("naP%d" % i) for i in range(2)]
        osb = [self.aalloc("osb%d" % i, [128, 512], F32) for i in range(2)]
        r_fin = [S.res("fin%d" % i) for i in range(2)]
        for b in range(2):
            self.memset(Vd[b][:, :, 0:64], 1.0, [r_head[b]])
            self.memset(Vd[b][:, :, 128:192], 1.0, [r_head[b]])
        nab = self.I["nab"]

        def load_pair(j):
            b = j % 2
            for e_ in range(2):
                S.dma("sp", Qd[b][e_][e_ * 64:(e_ + 1) * 64, :], self.qT_d[j, e_ * 64:(e_ + 1) * 64, :], writes=[r_pair[b]], semkey="napair%d" % b)
            S.dma("sp", Kd[b][:], self.kT_d[j], writes=[r_pair[b]], semkey="napair%d" % b)

        def load_head(h):
            b = h % 2
            S.dma("sp", Vd[b][:, :, 64:128], self.v_d[h], writes=[r_head[b]], semkey="nahead%d" % b)
            S.dma("sp", tab[b][:], nab[:, :, h * 768:(h + 1) * 768].rearrange("c p f -> p c f"), writes=[r_head[b]], semkey="nahead%d" % b)

        units = [(h, i) for h in range(8) for i in range(16)]
        CLS = {0: 1, 1: 2, 14: 3, 15: 4}
        sbanks = [(0, 1), (2, 3)]
        obanks = [4, 5]
        state = {"fin": 0}

        def emit_s(u):
            h, i = units[u]
            j, e = h // 2, h % 2
            bp = j % 2
            A, B = sbanks[u % 2]
            L0 = min(i, 14)
            q = Qd[bp][e][:, i * 128:(i + 1) * 128]
            for s_ in range(8):
                L = L0 + s_ if s_ < 6 else 20 + (s_ - 6)
                bk = A if s_ < 4 else B
                cc = (s_ % 4) * 128
                self.mm(self.ps[:, bk, cc:cc + 128], Kd[bp][:, L * 128:(L + 1) * 128], q, True, True,
                        [r_pair[bp]], [self.r_ps[bk]])

        load_pair(0)
        load_head(0)
        emit_s(0)
        for u, (h, i) in enumerate(units):
            j, e = h // 2, h % 2
            bp, bh = j % 2, h % 2
            if i == 0:
                if h + 1 < 8:
                    load_head(h + 1)
                    if e == 1:
                        load_pair(j + 1)
            A, B = sbanks[u % 2]
            cls = CLS.get(i, 0)
            L0 = min(i, 14)
            t = tmp[u % 2]
            p = P[u % 2]
            self.stt(t[:, 0:512], self.ps[:, A, :], 0.125, tab[bh][:, cls, 0:512], ALU.mult, ALU.add,
                     [self.r_ps[A], r_head[bh]], [r_tmp[u % 2]])
            self.stt(t[:, 512:768], self.ps[:, B, 0:256], 0.125, tab[bh][:, cls, 512:768], ALU.mult, ALU.add,
                     [self.r_ps[B], r_head[bh]], [r_tmp[u % 2]])
            self.act(p[:, 0:768], t[:, 0:768], AF.Exp, [r_tmp[u % 2]], [r_P[u % 2]])
            self.act(p[:, 768:1024], self.ps[:, B, 256:512], AF.Exp, [self.r_ps[B]], [r_P[u % 2]], scale=0.125)
            if u + 1 < len(units):
                emit_s(u + 1)
            ob = obanks[(u // 4) % 2]
            oc = (i % 4) * 128
            vsl = (64, 192) if e == 0 else (0, 128)
            for s_ in range(8):
                L = L0 + s_ if s_ < 6 else 20 + (s_ - 6)
                self.mm(self.ps[:, ob, oc:oc + 128], Vd[bh][:, L, vsl[0]:vsl[1]], p[:, s_ * 128:(s_ + 1) * 128], s_ == 0, s_ == 7,
                        [r_head[bh], r_P[u % 2]], [self.r_ps[ob]])
            if i % 4 == 3:
                po, pz = (0, 64) if e == 0 else (64, 0)
                t0 = (i - 3) * 128
                fi = state["fin"] % 2
                state["fin"] += 1
                self.copy(osb[fi][po:po + 64, :], self.ps[po:po + 64, ob, :], [self.r_ps[ob]], [r_fin[fi]])
                S.op("dve", lambda e_, ob=ob, pz=pz: e_.reciprocal(out=self.ps[pz:pz + 64, ob, :], in_=self.ps[pz:pz + 64, ob, :]),
                     [self.r_ps[ob]], [self.r_ps[ob]])
                self.tt(attnT[po:po + 64, 4 + h // 2, t0:t0 + 512], self.ps[pz:pz + 64, ob, :], osb[fi][po:po + 64, :], ALU.mult,
                        [self.r_ps[ob], r_fin[fi]], [r_attn])

    def final_norm_out(self):
        S = self.S
        S.barrier()
        self.areset()
        W = self.alloc_work(None)
        stage = [self.aalloc("ostage%d" % i, [128, 8, 512], F32) for i in range(2)]
        r_stage = [S.res("ostage%d" % i) for i in range(2)]
        fins = []
        gf = CL["gfinal"][0]
        for ci, (t0, T, v) in enumerate(OWN_CHUNKS[:4]):
            pb = self.bank()
            for k in range(8):
                sq = W["sq"][k % 2]
                self.act(sq[:, :T], self.xT[:, k, t0:t0 + T], AF.Square, [self.r_x[ci]], [W["r_sq"][k % 2]])
                self.mm(self.ps[:, pb, :T], self.ones_bf[:], sq[:, :T], k == 0, k == 7, [W["r_sq"][k % 2], self.r_small], [self.r_ps[pb]])
            rstd = W["rstd"]
            self.act(rstd[:, :T], self.ps[:, pb, :T], AF.Ln, [self.r_ps[pb], self.r_small], [W["r_rstd"]], scale=1.0 / D, bias=self.eps_t[:])
            self.act(rstd[:, :T], rstd[:, :T], AF.Exp, [W["r_rstd"]], [W["r_rstd"]], scale=-0.5)
            sg = stage[ci % 2]
            for k in range(8):
                self.stt(sg[:, k, :T], self.xT[:, k, t0:t0 + T], self.cst[:, gf + k:gf + k + 1], rstd[:, :T], ALU.mult, ALU.mult,
                         [self.r_x[ci], self.r_cst, W["r_rstd"]], [r_stage[ci % 2]])
            for k in range(8):
                fins.append(S.dma("sp", self.y_x[k * 128:(k + 1) * 128, t0:t0 + T], sg[:, k, :T], reads=[r_stage[ci % 2]],
                                  semkey="ystage%d" % (ci % 2)))
        return fins


def _rope_tables(rot_dim, pos):
    rows = (pos // GRID_W).astype(np.float32)
    cols = (pos % GRID_W).astype(np.float32)
    axis_dim = rot_dim // 2
    inv = (10000.0 ** (-np.arange(0, axis_dim, 2, dtype=np.float32) / axis_dim)).astype(np.float32)
    ang = np.concatenate([rows[:, None] * inv, cols[:, None] * inv], axis=-1).astype(np.float32)
    return np.cos(ang).astype(np.float32), np.sin(ang).astype(np.float32)


def _tables64(pos, nctx):
    c, s = _rope_tables(64, pos)
    C = np.concatenate([c, c, c, c], axis=1).T
    Sg = np.concatenate([-s, s, -s, s], axis=1).T
    C = np.concatenate([C, np.ones((128, nctx), np.float32)], axis=1)
    Sg = np.concatenate([Sg, np.zeros((128, nctx), np.float32)], axis=1)
    return np.ascontiguousarray(C, np.float32), np.ascontiguousarray(Sg, np.float32)


def _tables96(pos, nctx):
    c, s = _rope_tables(32, pos)
    T = len(pos)
    C = np.concatenate([np.ones((T, 64), np.float32), c, c], axis=1).T
    Sg = np.concatenate([np.zeros((T, 64), np.float32), -s, s], axis=1).T
    C = np.concatenate([C, np.ones((96, nctx), np.float32)], axis=1)
    Sg = np.concatenate([Sg, np.zeros((96, nctx), np.float32)], axis=1)
    return np.ascontiguousarray(C, np.float32), np.ascontiguousarray(Sg, np.float32)


def _fm(vec, k):
    return np.ascontiguousarray(np.asarray(vec, np.float32).reshape(k, 128).T)


def _consts(inp, b):
    cst = np.zeros((128, NCONST), np.float32)

    def put(name, arr):
        o, w = CL[name]
        cst[:, o:o + w] = np.asarray(arr, np.float32).reshape(128, w)
    cfm = np.stack([_fm(inp["c"][b], 8), _fm(inp["c_ctx"], 8)], axis=-1)
    put("cfm", cfm.reshape(128, 16))
    put("bmod0", _fm(inp["l0_b_mod"], 48))
    put("bmod1", _fm(inp["l1_b_mod"], 48))
    put("gattn0", _fm(inp["l0_g_attn"], 8))
    put("gffn0", _fm(inp["l0_g_ffn"], 8))
    put("gattn1", _fm(inp["l1_g_attn"], 8))
    put("gffn1", _fm(inp["l1_g_ffn"], 8))
    put("gfinal", _fm(inp["g_final"], 8))
    put("gsub", _fm(inp["l0_g_subln"], 1))
    put("gcq", _fm(inp["l0_g_cq"], 2))
    put("gckv", _fm(inp["l0_g_ckv"], 1))
    lam = np.concatenate([inp["l0_lam_q1"], inp["l0_lam_k1"], inp["l0_lam_q2"], inp["l0_lam_k2"]]).astype(np.float32)
    put("lamv", np.broadcast_to(lam[None, :], (128, 256)))
    gq = np.asarray(inp["l1_g_qc"], np.float32)
    gk = np.asarray(inp["l1_g_kc"], np.float32)
    sw = _swap_halves(np.arange(64), 64)
    put("gqc", np.tile(gq, 2)[:, None])
    put("gqcs", np.tile(gq[sw], 2)[:, None])
    put("gkc", np.tile(gk, 2)[:, None])
    put("gkcs", np.tile(gk[sw], 2)[:, None])
    return cst


def _l0_weights(inp):
    w_in = np.asarray(inp["l0_w_in"], np.float32)
    qa = np.arange(0, 512)
    ka = np.arange(512, 1024)
    va = np.arange(1024, 1536)
    cq = np.arange(1536, 1792)
    ckv = np.arange(1792, 1920)
    kr = np.arange(1920, 1952)
    wq_cols = np.concatenate([qa, _swap_halves(qa, 64), cq])
    wk_cols = np.concatenate([ka, _swap_halves(ka, 64), ckv, ckv[:64], kr, ckv[:64], _swap_halves(kr, 32)])
    w_uq = np.asarray(inp["l0_w_uq"], np.float32)
    uq = np.arange(768).reshape(8, 96)
    uqs = uq.copy()
    for h in range(8):
        uqs[h, 64:96] = _swap_halves(uq[h, 64:96], 32)
    w_ukv = np.asarray(inp["l0_w_ukv"], np.float32)
    kv = np.arange(1024).reshape(8, 128)
    return {
        "wq0": np.ascontiguousarray(w_in[:, wq_cols]),
        "wk0": np.ascontiguousarray(w_in[:, wk_cols]),
        "wv0": np.ascontiguousarray(w_in[:, va]),
        "wuq": np.ascontiguousarray(w_uq[:, np.concatenate([uq.reshape(-1), uqs.reshape(-1)])]),
        "wkn": np.ascontiguousarray(w_ukv[:, kv[:, :64].reshape(-1)]),
        "wvb": np.ascontiguousarray(w_ukv[:, kv[:, 64:].reshape(-1)]),
    }


def _core_inputs_l0(inp, core, shared):
    b, half = core // 2, core % 2
    x = np.asarray(inp["x"], np.float32)
    m = {}
    xb = x[b]
    m["xT_own"] = np.ascontiguousarray(xb[half * TOWN:(half + 1) * TOWN].T)
    m["xT_all"] = shared.setdefault(("xT_all", b), np.ascontiguousarray(xb.T))
    m["ctxT"] = shared.setdefault(("ctxT", b), np.ascontiguousarray(np.asarray(inp["ctx"], np.float32)[b].T))
    m["consts"] = _consts(inp, b)
    return m


_CACHE = {}


def _get_prog(mode):
    if mode not in _CACHE:
        p = Prog(mode)
        p.build()
        _CACHE[mode] = p
    return _CACHE[mode]


def run_l0(inp):
    prog = _get_prog("l0")
    shared = {}
    W = _l0_weights(inp)
    pos_all = np.arange(SEQ)
    c64a, s64a = _tables64(pos_all, NCTX)
    c96a, s96a = _tables96(pos_all, NCTX)
    maps = []
    for core in range(8):
        half = core % 2
        m = _core_inputs_l0(inp, core, shared)
        m.update(W)
        m["wmod0"] = np.asarray(inp["l0_w_mod"], np.float32)
        m["wout0"] = np.asarray(inp["l0_w_out"], np.float32)
        m["w1_0"] = np.asarray(inp["l0_w1"], np.float32)
        m["w3_0"] = np.asarray(inp["l0_w3"], np.float32)
        m["w2_0"] = np.asarray(inp["l0_w2"], np.float32)
        pos_own = np.arange(half * TOWN, (half + 1) * TOWN)
        m["c64_own"], m["s64_own"] = shared.setdefault(("t64", half), _tables64(pos_own, NCTX))
        m["c96_own"], m["s96_own"] = shared.setdefault(("t96", half), _tables96(pos_own, NCTX))
        m["c64_all"], m["s64_all"], m["c96_all"], m["s96_all"] = c64a, s64a, c96a, s96a
        maps.append(m)
    res = run_bass_kernel_spmd(prog.nc, maps, core_ids=list(range(8)))
    x1 = np.zeros((4, SEQ, D), np.float32)
    xc1 = np.zeros((4, NCTX, D), np.float32)
    for core in range(8):
        b, half = core // 2, core % 2
        x1[b, half * TOWN:(half + 1) * TOWN] = res.results[core]["y_x"].T
        xc1[b] = res.results[core]["y_c"].T
    return x1, xc1


def _l1_weights(inp):
    w_in = np.asarray(inp["l1_w_in"], np.float32)
    qc = np.arange(0, 512).reshape(8, 64)
    kc = np.arange(512, 640)
    vc = np.arange(640, 768)
    qd = np.arange(768, 1280)
    kd = np.arange(1280, 1792)
    vd = np.arange(1792, 2304)
    qct = np.concatenate([np.concatenate([qc[j], qc[j + 4]]) for j in range(4)])
    wq_cols = np.concatenate([qct, _swap_halves(qct, 64), qd])
    wk_cols = np.concatenate([kc, _swap_halves(kc, 64), kd])
    wv_cols = np.concatenate([vc, vd])
    return {"wq1": np.ascontiguousarray(w_in[:, wq_cols]), "wk1": np.ascontiguousarray(w_in[:, wk_cols]),
            "wv1": np.ascontiguousarray(w_in[:, wv_cols])}


def _na_tables(rpb, half):
    rpb = np.asarray(rpb, np.float32)
    NEG = -30000.0
    out = np.full((5, 128, 8, 6, 128), NEG, np.float32)
    for cls, i in [(0, 5), (1, 0), (2, 1), (3, 14), (4, 15)]:
        gi = 16 * half + i
        qpos = gi * 128 + np.arange(128)
        r, c = qpos // GRID_W, qpos % GRID_W
        rs = np.clip(r - 4, 0, 56)
        cs = np.clip(c - 8, 0, 48)
        L0 = min(i, 14)
        for s_ in range(6):
            L = L0 + s_
            if L < 2:
                gb = 14 + L
            elif L < 18:
                gb = 16 * half + L - 2
            else:
                gb = 16 + L - 18
            kpos = gb * 128 + np.arange(128)
            kr, kc = kpos // GRID_W, kpos % GRID_W
            inwin = ((kr[:, None] >= rs[None, :]) & (kr[:, None] < rs[None, :] + 8)
                     & (kc[:, None] >= cs[None, :]) & (kc[:, None] < cs[None, :] + 16))
            rel_r = np.clip(kr[:, None] - r[None, :] + 7, 0, 14)
            rel_c = np.clip(kc[:, None] - c[None, :] + 15, 0, 30)
            for h in range(8):
                out[cls, :, h, s_, :] = np.where(inwin, rpb[h][rel_r, rel_c], NEG)
    return np.ascontiguousarray(out.reshape(5, 128, 8 * 6 * 128))


def run_l1(inp, x1, xc1):
    prog = _get_prog("l1")
    shared = {}
    W = _l1_weights(inp)
    c64a, s64a = _tables64(np.arange(SEQ), NCTX)
    maps = []
    inp2 = dict(inp)
    inp2["x"] = x1
    inp2["ctx"] = xc1
    for core in range(8):
        half = core % 2
        m = _core_inputs_l0(inp2, core, shared)
        m.update(W)
        m["wmod1"] = np.asarray(inp["l1_w_mod"], np.float32)
        m["wout1"] = np.asarray(inp["l1_w_out"], np.float32)
        m["w1_1"] = np.asarray(inp["l1_w1"], np.float32)
        m["w3_1"] = np.asarray(inp["l1_w3"], np.float32)
        m["w2_1"] = np.asarray(inp["l1_w2"], np.float32)
        pos_own = np.arange(half * TOWN, (half + 1) * TOWN)
        m["c64_own"], m["s64_own"] = shared.setdefault(("t64", half), _tables64(pos_own, NCTX))
        m["c64_all"], m["s64_all"] = c64a, s64a
        m["nab"] = shared.setdefault(("nab", half), _na_tables(inp["l1_rpb"], half))
        maps.append(m)
    res = run_bass_kernel_spmd(prog.nc, maps, core_ids=list(range(8)))
    out = np.zeros((4, SEQ, D), np.float32)
    for core in range(8):
        b, half = core // 2, core % 2
        out[b, half * TOWN:(half + 1) * TOWN] = res.results[core]["y_x"].T
    return out


def run_fused(inp):
    prog = _get_prog("fused")
    shared = {}
    W0 = _l0_weights(inp)
    W1 = _l1_weights(inp)
    pos_all = np.arange(SEQ)
    c64a, s64a = _tables64(pos_all, NCTX)
    c96a, s96a = _tables96(pos_all, NCTX)
    maps = []
    for core in range(8):
        half = core % 2
        m = _core_inputs_l0(inp, core, shared)
        m.update(W0)
        m.update(W1)
        for l in range(2):
            m["wmod%d" % l] = np.asarray(inp["l%d_w_mod" % l], np.float32)
            m["wout%d" % l] = np.asarray(inp["l%d_w_out" % l], np.float32)
            m["w1_%d" % l] = np.asarray(inp["l%d_w1" % l], np.float32)
            m["w3_%d" % l] = np.asarray(inp["l%d_w3" % l], np.float32)
            m["w2_%d" % l] = np.asarray(inp["l%d_w2" % l], np.float32)
        pos_own = np.arange(half * TOWN, (half + 1) * TOWN)
        m["c64_own"], m["s64_own"] = shared.setdefault(("t64", half), _tables64(pos_own, NCTX))
        m["c96_own"], m["s96_own"] = shared.setdefault(("t96", half), _tables96(pos_own, NCTX))
        m["c64_all"], m["s64_all"], m["c96_all"], m["s96_all"] = c64a, s64a, c96a, s96a
        m["nab"] = shared.setdefault(("nab", half), _na_tables(inp["l1_rpb"], half))
        P = core // 2
        m["gidx"] = np.stack([(2 * P + hp) * 512 + np.arange(128) for hp in range(2)], axis=1).astype(np.uint32)
        maps.append(m)
    res = run_bass_kernel_spmd(prog.nc, maps, core_ids=list(range(8)))
    out = np.zeros((4, SEQ, D), np.float32)
    for core in range(8):
        b, half = core // 2, core % 2
        out[b, half * TOWN:(half + 1) * TOWN] = res.results[core]["y_x"].T
    return out


def kernel(**inputs):
    inp = {k: np.asarray(v) for k, v in inputs.items()}
    return run_fused(inp)
```

```python
import math
from contextlib import ExitStack
import numpy as np
import concourse.bass as bass
import concourse.mybir as mybir
from concourse.bass_utils import run_bass_kernel_spmd

F32 = mybir.dt.float32
BF16 = mybir.dt.bfloat16
AF = mybir.ActivationFunctionType
ALU = mybir.AluOpType

ENGS = ("pe", "act", "dve", "pool", "sp")


class Res:
    __slots__ = ("name", "w", "r")

    def __init__(self, name):
        self.name = name
        self.w = None
        self.r = {}


class Rec:
    __slots__ = ("eng", "fn", "deps", "inc", "val", "dma", "semkey", "idx")

    def __init__(self, eng, fn, dma=False, semkey=None):
        self.eng = eng
        self.fn = fn
        self.deps = []
        self.inc = False
        self.val = None
        self.dma = dma
        self.semkey = semkey
        self.idx = None


class Sched:
    def __init__(self, nc):
        self.nc = nc
        self.q = {e: [] for e in ENGS}
        self.n = 0
        self.pending = {e: [] for e in ENGS}
        self.last = {e: None for e in ENGS}
        self.open_dmas = []

    def res(self, name):
        return Res(name)

    def barrier(self):
        toks = [r for r in self.last.values() if r is not None] + list(self.open_dmas)
        self.open_dmas = []
        for e in ENGS:
            self.pending[e] = list(toks)

    def _add(self, rec, reads, writes):
        eng = rec.eng
        deps = []
        for r in reads:
            if r.w is not None:
                deps.append(r.w)
        for w in writes:
            if w.w is not None:
                deps.append(w.w)
            for e2, rr in w.r.items():
                if e2 == eng and not rr.dma:
                    continue
                deps.append(rr)
        if self.pending[eng]:
            deps = deps + [d for d in self.pending[eng] if d.dma or d.eng != eng]
            self.pending[eng] = []
        seen = set()
        for d in deps:
            if d is rec or id(d) in seen:
                continue
            if d.eng == eng == "pe" and not d.dma:
                continue
            if rec.dma and d.dma and d.semkey == rec.semkey:
                continue
            seen.add(id(d))
            rec.deps.append(d)
            d.inc = True
        for r in reads:
            r.r[("dma" + str(self.n)) if rec.dma else eng] = rec
        for w in writes:
            w.w = rec
            w.r = {}
        rec.idx = self.n
        self.n += 1
        self.q[eng].append(rec)
        if rec.dma:
            self.open_dmas.append(rec)
        else:
            self.last[eng] = rec
        return rec

    def op(self, eng, fn, reads=(), writes=()):
        return self._add(Rec(eng, fn), reads, writes)

    def dma(self, eng, out, in_, reads=(), writes=(), semkey=None, **kw):
        assert semkey is not None

        def fn(e, out=out, in_=in_, kw=kw):
            return e.dma_start(out=out, in_=in_, **kw)
        rec = Rec(eng, fn, dma=True, semkey=semkey)
        rec.inc = True
        return self._add(rec, reads, writes)

    def finalize(self, final_waits=()):
        import bisect
        nc = self.nc
        dmakeys = []
        for e in ENGS:
            c = 0
            for rec in self.q[e]:
                if rec.dma:
                    if rec.semkey not in dmakeys:
                        dmakeys.append(rec.semkey)
                elif rec.inc:
                    c += 1
                    rec.val = c
        dcount = {k: 0 for k in dmakeys}
        keyq = {}
        allrecs = sorted([r for e in ENGS for r in self.q[e] if r.dma], key=lambda r: r.idx)
        klist = {}
        for rec in allrecs:
            assert keyq.setdefault(rec.semkey, rec.eng) == rec.eng
            dcount[rec.semkey] += 16
            rec.val = dcount[rec.semkey]
            klist.setdefault(rec.semkey, []).append(rec)
        kidx = {k: [r.idx for r in v] for k, v in klist.items()}
        self.nsem = len(dmakeys) + len(ENGS)
        with ExitStack() as st:
            esem = {e: st.enter_context(nc.semaphore("s_" + e)) for e in ENGS}
            dsem = {k: st.enter_context(nc.semaphore("d_%d" % i)) for i, k in enumerate(dmakeys)}
            block = st.enter_context(nc.Block())

            def semof(rec):
                return dsem[rec.semkey] if rec.dma else esem[rec.eng]

            def valof(d, consumer_idx):
                if not d.dma:
                    return d.val
                i = bisect.bisect_left(kidx[d.semkey], consumer_idx) - 1
                return max(d.val, klist[d.semkey][i].val if i >= 0 else 0)

            def replay(eng_name, e):
                seen = {}
                for rec in self.q[eng_name]:
                    need = {}
                    for d in rec.deps:
                        s = semof(d)
                        k = id(s)
                        v = valof(d, rec.idx)
                        if seen.get(k, 0) >= v:
                            continue
                        if k not in need or need[k][1] < v:
                            need[k] = (s, v)
                    for k, (s, v) in need.items():
                        e.wait_ge(s, v)
                        seen[k] = v
                    ins = rec.fn(e)
                    if rec.dma:
                        ins.then_inc(dsem[rec.semkey], 16)
                    elif rec.inc:
                        ins.then_inc(esem[eng_name], 1)
                if eng_name == "sp":
                    for d in final_waits:
                        e.wait_ge(semof(d), valof(d, 1 << 60))

            @block.tensor
            def _(e):
                replay("pe", e)

            @block.scalar
            def _(e):
                replay("act", e)

            @block.vector
            def _(e):
                replay("dve", e)

            @block.gpsimd
            def _(e):
                replay("pool", e)

            @block.sync
            def _(e):
                replay("sp", e)


D = 1024
SEQ = 4096
NCTX = 256
TOWN = 2048
NOWN = TOWN + NCTX
NKEY = SEQ + NCTX
NKB = NKEY // 128
GRID_W = 64
EPS = 1e-6
FFN = 2816
NJ = FFN // 128
LAMBDA_INIT0 = 0.8 - 0.6 * math.exp(-0.3 * 0)

CL = {}
_off = 0
for _n, _w in [("cfm", 16), ("bmod0", 48), ("bmod1", 48), ("gattn0", 8), ("gffn0", 8), ("gattn1", 8),
               ("gffn1", 8), ("gfinal", 8), ("gsub", 1), ("gcq", 2), ("gckv", 1), ("lamv", 256),
               ("gqc", 1), ("gqcs", 1), ("gkc", 1), ("gkcs", 1)]:
    CL[_n] = (_off, _w)
    _off += _w
NCONST = _off

OWN_CHUNKS = [(0, 512, 0), (512, 512, 0), (1024, 512, 0), (1536, 512, 0), (2048, 256, 1)]


def _swap_halves(cols, group):
    c = np.asarray(cols).reshape(-1, 2, group // 2)
    return c[:, ::-1, :].reshape(-1)


class Prog:
    def __init__(self, mode):
        self.mode = mode
        self.nc = bass.Bass("TRN2", target_bir_lowering=False)
        self.S = Sched(self.nc)
        self.bank_rr = 0
        self.tmp_rr = 0

    def din(self, name, shape, dt=F32):
        return self.nc.dram_tensor(name, list(shape), dt, kind="ExternalInput").ap()

    def dout(self, name, shape, dt=F32):
        return self.nc.dram_tensor(name, list(shape), dt, kind="ExternalOutput").ap()

    def dscr(self, name, shape, dt=BF16):
        return self.nc.dram_tensor(name, list(shape), dt).ap()

    def sb(self, st, name, shape, dt):
        if st is not None and st is self.g:
            return st.enter_context(self.nc.sbuf_tensor(name, list(shape), dt))
        return self.aalloc(name, shape, dt)

    def areset(self, lo=0, hi=None):
        self.a_lo = lo
        self.a_hi = self.a_words if hi is None else hi

    def aalloc(self, name, shape, dt, top=False):
        n = 1
        for d_ in shape[1:]:
            n *= d_
        w = n if dt == F32 else (n + 1) // 2
        w = (w + 7) // 8 * 8
        if top:
            self.a_hi -= w
            off = self.a_hi
        else:
            off = self.a_lo
            self.a_lo += w
        assert self.a_lo <= self.a_hi, "arena overflow %s: lo=%d hi=%d" % (name, self.a_lo, self.a_hi)
        v = self.arena[:, off:off + w]
        if dt != F32:
            v = v.bitcast(dt)
        v = v[:, 0:n]
        if len(shape) == 3:
            v = v.rearrange("p (a b) -> p a b", a=shape[1])
        if shape[0] != 128:
            v = v[0:shape[0]]
        return v

    def bank(self):
        b = self.bank_rr
        self.bank_rr = (self.bank_rr + 1) % 5
        return b

    def mm(self, out, lhsT, rhs, start, stop, reads, writes):
        self.S.op("pe", lambda e: e.matmul(out, lhsT=lhsT, rhs=rhs, start=start, stop=stop), reads, writes)

    def act(self, out, in_, func, reads, writes, scale=None, bias=None):
        kw = {}
        if scale is not None:
            kw["scale"] = scale
        if bias is not None:
            kw["bias"] = bias
        self.S.op("act", lambda e: e.activation(out=out, in_=in_, func=func, **kw), reads, writes)

    def tt(self, out, in0, in1, op, reads, writes, eng="dve"):
        self.S.op(eng, lambda e: e.tensor_tensor(out=out, in0=in0, in1=in1, op=op), reads, writes)

    def stt(self, out, in0, scalar, in1, op0, op1, reads, writes):
        self.S.op("dve", lambda e: e.scalar_tensor_tensor(out=out, in0=in0, scalar=scalar, in1=in1, op0=op0, op1=op1),
                  reads, writes)

    def ts(self, out, in0, s1, s2, op0, op1, reads, writes, eng="dve"):
        if op1 is None:
            self.S.op(eng, lambda e: e.tensor_scalar(out=out, in0=in0, scalar1=s1, scalar2=None, op0=op0), reads, writes)
        else:
            self.S.op(eng, lambda e: e.tensor_scalar(out=out, in0=in0, scalar1=s1, scalar2=s2, op0=op0, op1=op1),
                      reads, writes)

    def copy(self, out, in_, reads, writes, eng="dve"):
        if eng == "act":
            self.S.op("act", lambda e: e.copy(out=out, in_=in_), reads, writes)
        else:
            self.S.op(eng, lambda e: e.tensor_copy(out=out, in_=in_), reads, writes)

    def memset(self, ap, val, writes, eng="dve"):
        self.S.op(eng, lambda e: e.memset(ap, val), (), writes)

    def build(self):
        nc, S, mode = self.nc, self.S, self.mode
        do0 = mode in ("l0", "fused")
        do1 = mode in ("l1", "fused")
        self.xT_own = self.din("xT_own", [D, TOWN])
        self.xT_all = self.din("xT_all", [D, SEQ])
        self.ctxT = self.din("ctxT", [D, NCTX])
        self.consts_d = self.din("consts", [128, NCONST])
        I = {}
        if do0:
            for n, shp in [("wmod0", [D, 6 * D]), ("wq0", [D, 1280]), ("wk0", [D, 1344]), ("wv0", [D, 512]),
                           ("wuq", [256, 1536]), ("wkn", [128, 512]), ("wvb", [128, 512]), ("wout0", [D, D]),
                           ("w1_0", [D, FFN]), ("w3_0", [D, FFN]), ("w2_0", [FFN, D]),
                           ("c64_own", [128, NOWN]), ("s64_own", [128, NOWN]), ("c64_all", [128, NKEY]),
                           ("s64_all", [128, NKEY]), ("c96_own", [96, NOWN]), ("s96_own", [96, NOWN]),
                           ("c96_all", [96, NKEY]), ("s96_all", [96, NKEY])]:
                I[n] = self.din(n, shp)
        if do1:
            for n, shp in [("wmod1", [D, 6 * D]), ("wq1", [D, 1536]), ("wk1", [D, 768]), ("wv1", [D, 640]),
                           ("wout1", [D, D]), ("w1_1", [D, FFN]), ("w3_1", [D, FFN]), ("w2_1", [FFN, D]),
                           ("nab", [5, 128, 8 * 6 * 128]), ("c64_own", [128, NOWN]), ("s64_own", [128, NOWN]),
                           ("c64_all", [128, NKEY]), ("s64_all", [128, NKEY])]:
                if n not in I:
                    I[n] = self.din(n, shp)
        self.I = I
        if mode == "l0":
            self.y_x = self.dout("y_x", [D, TOWN])
            self.y_c = self.dout("y_c", [D, NCTX])
        else:
            self.y_x = self.dout("y_x", [D, TOWN])
        self.qT_a = self.dscr("qT_a", [4, 128, NOWN])
        self.kT_a = self.dscr("kT_a", [4, 128, NKEY])
        self.v_a = self.dscr("v_a", [4, 128, NKB, 128])
        self.qT_b = self.dscr("qT_b", [8, 128, NOWN])
        self.kT_b = self.dscr("kT_b", [8, 128, NKEY])
        self.v_b = self.dscr("v_b", [8, 128, NKB, 64])
        self.qT_c = self.dscr("qT_c", [4, 128, TOWN])
        self.kT_c = self.dscr("kT_c", [128, NKEY])
        self.v_c = self.dscr("v_c", [2, 128, NKB, 64])
        self.qT_d = self.dscr("qT_d", [4, 128, TOWN])
        self.kT_d = self.dscr("kT_d", [4, 128, 22 * 128])
        self.v_d = self.dscr("v_d", [8, 128, 22, 64])
        if mode == "fused":
            self.gidx_d = self.nc.dram_tensor("gidx", [128, 2], mybir.dt.uint32, kind="ExternalInput").ap()
            self.xsend = self.nc.dram_tensor("xsend", [512, 2048], BF16, kind="Internal", addr_space="Local").ap()
            self.xgath = self.nc.dram_tensor("xgath", [8 * 512, 2048], BF16, kind="Internal", addr_space="Local").ap()

        with ExitStack() as g:
            self.g = g
            self.a_words = 31936
            self.arena = g.enter_context(nc.sbuf_tensor("arena", [128, self.a_words], F32))
            self.areset()
            self.xT = self.sb(g, "xT", [128, 8, NOWN], F32)
            self.r_x = [S.res("x%d" % i) for i in range(5)]
            self.cst = self.sb(g, "cst", [128, NCONST], F32)
            self.r_cst = S.res("cst")
            self.ones_bf = self.sb(g, "ones_bf", [128, 128], BF16)
            self.eps_t = self.sb(g, "eps_t", [128, 1], F32)
            self.csil = self.sb(g, "csil", [128, 8, 2], BF16)
            self.ada = [self.sb(g, "ada%d" % l, [128, 48, 2], F32) for l in range(2)]
            self.A1 = [self.sb(g, "A1_%d" % l, [128, 8, 2], F32) for l in range(2)]
            self.A2 = [self.sb(g, "A2_%d" % l, [128, 8, 2], F32) for l in range(2)]
            self.small = self.sb(g, "small", [128, 16], F32)
            self.r_small = S.res("small")
            self.r_ada = S.res("ada")
            self.ps = g.enter_context(nc.psum_tensor("ps", [128, 8, 512], F32))
            self.r_ps = [S.res("ps%d" % i) for i in range(8)]
            if mode == "fused":
                self.xstg = [self.sb(g, "xstg%d" % i, [128, 2048], BF16) for i in range(2)]
                self.gidx = self.sb(g, "gidx_sb", [128, 2], mybir.dt.uint32)
                self.r_gidx = S.res("gidx")
                S.dma("sp", self.gidx[:], self.gidx_d, writes=[self.r_gidx], semkey="gidx")

            for k in range(8):
                S.dma("sp", self.xT[:, k, 0:TOWN], self.xT_own[k * 128:(k + 1) * 128, :], writes=self.r_x[0:4], semkey="xin")
                S.dma("sp", self.xT[:, k, TOWN:NOWN], self.ctxT[k * 128:(k + 1) * 128, :], writes=[self.r_x[4]], semkey="xin")
            S.dma("sp", self.cst[:], self.consts_d, writes=[self.r_cst], semkey="cst")
            self.memset(self.ones_bf[:], 1.0, [self.r_small])
            self.memset(self.eps_t[:], EPS, [self.r_small])
            c0 = CL["cfm"][0]
            self.act(self.csil[:].rearrange("p k v -> p (k v)"), self.cst[:, c0:c0 + 16], AF.Silu, [self.r_cst], [self.r_small])

            if do0:
                self.ada_params(0)
                self.layer0()
            if do1:
                self.ada_params(1)
                self.layer1()
            fins = []
            if mode == "l0":
                S.barrier()
                for k in range(8):
                    fins.append(S.dma("sp", self.y_x[k * 128:(k + 1) * 128, :], self.xT[:, k, 0:TOWN], reads=self.r_x, semkey="yout"))
                    fins.append(S.dma("sp", self.y_c[k * 128:(k + 1) * 128, :], self.xT[:, k, TOWN:NOWN], reads=self.r_x, semkey="yout"))
            else:
                fins = self.final_norm_out()
            S.finalize(final_waits=fins)
        return nc

    def cs(self, name, j=None):
        o, w = CL[name]
        if j is None:
            return self.cst[:, o:o + w]
        return self.cst[:, o + j:o + j + 1]

    def ada_params(self, l):
        nc, S = self.nc, self.S
        wmod = self.I["wmod%d" % l].rearrange("(k p) c -> p k c", p=128)
        S.barrier()
        self.areset()
        with ExitStack() as st:
            wt = [self.sb(st, "wm%d" % i, [128, 8, 512], BF16) for i in range(2)]
            r_wt = [S.res("wm%d" % i) for i in range(2)]
            pb = 7
            for cc in range(12):
                b = cc % 2
                S.dma("pool", wt[b][:], wmod[:, :, cc * 512:(cc + 1) * 512], writes=[r_wt[b]], semkey="wm%d" % b)
                for jj in range(4):
                    j = cc * 4 + jj
                    for k in range(8):
                        self.mm(self.ps[:, pb, j * 2:j * 2 + 2], wt[b][:, k, jj * 128:(jj + 1) * 128], self.csil[:, k, :],
                                k == 0, k == 7, [r_wt[b], self.r_small], [self.r_ps[pb]])
            bm = self.cs("bmod%d" % l)
            pv = self.ps[:, pb, 0:96].rearrange("p (j v) -> p j v", v=2)
            for v in range(2):
                self.tt(self.ada[l][:, :, v], pv[:, :, v], bm, ALU.add, [self.r_ps[pb], self.r_cst], [self.r_ada])
            for v in range(2):
                self.stt(self.A1[l][:, :, v], self.ada[l][:, 8:16, v], 1.0, self.cs("gattn%d" % l), ALU.add, ALU.mult,
                         [self.r_ada, self.r_cst], [self.r_ada])
                self.stt(self.A2[l][:, :, v], self.ada[l][:, 32:40, v], 1.0, self.cs("gffn%d" % l), ALU.add, ALU.mult,
                         [self.r_ada, self.r_cst], [self.r_ada])

    def norm_chunk(self, src, r_src, T, A, sh, v, hT, r_hT, W):
        S = self.S
        pb = self.bank()
        for k in range(8):
            sq = W["sq"][k % 2]
            self.act(sq[:, :T], src(k), AF.Square, [r_src], [W["r_sq"][k % 2]])
            self.mm(self.ps[:, pb, :T], self.ones_bf[:], sq[:, :T], k == 0, k == 7, [W["r_sq"][k % 2], self.r_small], [self.r_ps[pb]])
        rstd = W["rstd"]
        self.act(rstd[:, :T], self.ps[:, pb, :T], AF.Ln, [self.r_ps[pb], self.r_small], [W["r_rstd"]], scale=1.0 / D, bias=self.eps_t[:])
        self.act(rstd[:, :T], rstd[:, :T], AF.Exp, [W["r_rstd"]], [W["r_rstd"]], scale=-0.5)
        sh_t, sh_base = sh
        for k in range(8):
            t = W["nt"][k % 2]
            self.stt(t[:, :T], src(k), A[:, k, v:v + 1], rstd[:, :T], ALU.mult, ALU.mult, [r_src, self.r_ada, W["r_rstd"]], [W["r_nt"][k % 2]])
            self.act(hT(k), t[:, :T], AF.Identity, [W["r_nt"][k % 2], self.r_ada], [r_hT], bias=sh_t[:, sh_base + k, v:v + 1])

    def alloc_work(self, st):
        S = self.S
        W = {}
        W["sq"] = [self.sb(st, "sq%d" % i, [128, 512], BF16) for i in range(2)]
        W["r_sq"] = [S.res("sq%d" % i) for i in range(2)]
        W["rstd"] = self.sb(st, "rstd", [128, 512], F32)
        W["r_rstd"] = S.res("rstd")
        W["nt"] = [self.sb(st, "nt%d" % i, [128, 512], F32) for i in range(2)]
        W["r_nt"] = [S.res("nt%d" % i) for i in range(2)]
        return W

    def proj(self, pb, M, T, w_of_k, h_of_k, nk, reads):
        for k in range(nk):
            self.mm(self.ps[0:M, pb, :T], w_of_k(k), h_of_k(k), k == 0, k == nk - 1, reads, [self.r_ps[pb]])

    def layer0(self):
        nc, S, I = self.nc, self.S, self.I
        l = 0
        ada, A1, A2 = self.ada[0], self.A1[0], self.A2[0]
        sm = self.small
        lo = CL["lamv"][0]
        self.areset()
        with ExitStack() as st:
            tmp = self.sb(st, "lamtmp", [128, 64], F32)
            r_t = S.res("lamtmp")
            for i in range(2):
                self.tt(tmp[:], self.cst[:, lo + i * 128:lo + i * 128 + 64], self.cst[:, lo + i * 128 + 64:lo + i * 128 + 128],
                        ALU.mult, [self.r_cst], [r_t])
                S.op("dve", lambda e, i=i: e.tensor_reduce(out=sm[:, i:i + 1], in_=tmp[:], axis=mybir.AxisListType.X, op=ALU.add),
                     [r_t], [self.r_small])
            self.act(sm[:, 0:2], sm[:, 0:2], AF.Exp, [self.r_small], [self.r_small])
            self.tt(sm[:, 2:3], sm[:, 1:2], sm[:, 0:1], ALU.subtract, [self.r_small], [self.r_small])
            self.ts(sm[:, 2:3], sm[:, 2:3], -LAMBDA_INIT0, None, ALU.add, None, [self.r_small], [self.r_small])
            self.ts(sm[:, 3:4], self.cs("gsub"), 1.0 - LAMBDA_INIT0, None, ALU.mult, None, [self.r_cst], [self.r_small])
            S.barrier()
        neglam = sm[:, 2:3]
        gsub8 = sm[:, 3:4]

        S.barrier()
        self.areset()
        with ExitStack() as st:
            W = self.alloc_work(st)
            wq = self.sb(st, "wq0", [128, 8, 1280], BF16)
            wk = self.sb(st, "wk0", [128, 8, 1344], BF16)
            wv = self.sb(st, "wv0", [128, 8, 512], BF16)
            wuq = self.sb(st, "wuq", [128, 2, 1536], BF16)
            wkn = self.sb(st, "wkn", [128, 512], BF16)
            wvb = self.sb(st, "wvb", [128, 512], BF16)
            r_w = S.res("wA")
            r_wq = [S.res("wq%d" % k) for k in range(8)]
            r_wk = [S.res("wk%d" % k) for k in range(8)]
            for k in range(8):
                S.dma("pool", wq[:, k, :], I["wq0"][k * 128:(k + 1) * 128, :], writes=[r_wq[k]], semkey="wq%d" % k)
            for k in range(2):
                S.dma("pool", wuq[:, k, :], I["wuq"][k * 128:(k + 1) * 128, :], writes=[r_w], semkey="wsm")
            for k in range(8):
                S.dma("pool", wk[:, k, :], I["wk0"][k * 128:(k + 1) * 128, :], writes=[r_wk[k]], semkey="wk%d" % k)
                S.dma("pool", wv[:, k, :], I["wv0"][k * 128:(k + 1) * 128, :], writes=[r_wk[k]], semkey="wk%d" % k)
            S.dma("pool", wkn[:], I["wkn"], writes=[r_w], semkey="wsm")
            S.dma("pool", wvb[:], I["wvb"], writes=[r_w], semkey="wsm")
            xs = self.sb(st, "xstage", [128, 8, 512], F32)
            r_xs = S.res("xs")
            hT = [self.sb(st, "hT%d" % i, [128, 8, 512], BF16) for i in range(2)]
            r_hT = [S.res("hT%d" % i) for i in range(2)]
            c64 = self.sb(st, "c64", [128, 512], F32)
            s64 = self.sb(st, "s64", [128, 512], F32)
            c96 = self.sb(st, "c96", [128, 512], F32)
            s96 = self.sb(st, "s96", [128, 512], F32)
            r_tab = S.res("tab")
            t1 = [self.sb(st, "t1_%d" % i, [128, 512], F32) for i in range(2)]
            t2 = [self.sb(st, "t2_%d" % i, [128, 512], F32) for i in range(2)]
            r_t1 = [S.res("t1_%d" % i) for i in range(2)]
            r_t2 = [S.res("t2_%d" % i) for i in range(2)]
            ost = [self.sb(st, "ost%d" % i, [128, 512], BF16) for i in range(3)]
            r_ost = [S.res("ost%d" % i) for i in range(3)]
            cqf = self.sb(st, "cqf", [128, 2, 512], F32)
            cqn = self.sb(st, "cqn", [128, 2, 512], BF16)
            r_cq = S.res("cq")
            vst = [self.sb(st, "vst%d" % i, [128, 512], BF16) for i in range(2)]
            r_vst = [S.res("vst%d" % i) for i in range(2)]
            cnt = {"o": 0, "t": 0, "v": 0, "h": 0}
            allw = r_wq + r_wk + [r_w]

            def rope_evac(pb1, pb2, M, T, ctab, stab, dst_aps, p0=0):
                i = cnt["t"] % 2
                cnt["t"] += 1
                o = cnt["o"] % 3
                cnt["o"] += 1
                self.tt(t1[i][p0:M, :T], self.ps[p0:M, pb1, :T], ctab[p0:M, :T], ALU.mult, [self.r_ps[pb1], r_tab], [r_t1[i]])
                self.tt(t2[i][p0:M, :T], self.ps[p0:M, pb2, :T], stab[p0:M, :T], ALU.mult, [self.r_ps[pb2], r_tab], [r_t2[i]])
                self.tt(ost[o][p0:M, :T], t1[i][p0:M, :T], t2[i][p0:M, :T], ALU.add, [r_t1[i], r_t2[i]], [r_ost[o]], eng="pool")
                for (dap, a, b) in dst_aps:
                    S.dma("sp", dap, ost[o][a:b, :T], reads=[r_ost[o]], semkey="ost%d" % o)

            def small_norm(pbs, nk, T, dim, gname, outf, outn):
                pb = self.bank()
                for j in range(nk):
                    self.copy(outf[:, j, :T], self.ps[:, pbs[j], :T], [self.r_ps[pbs[j]]], [r_cq], eng="act")
                    sq = W["sq"][j % 2]
                    self.act(sq[:, :T], self.ps[:, pbs[j], :T], AF.Square, [self.r_ps[pbs[j]]], [W["r_sq"][j % 2]])
                    self.mm(self.ps[:, pb, :T], self.ones_bf[:], sq[:, :T], j == 0, j == nk - 1, [W["r_sq"][j % 2], self.r_small], [self.r_ps[pb]])
                rstd = W["rstd"]
                self.act(rstd[:, :T], self.ps[:, pb, :T], AF.Ln, [self.r_ps[pb], self.r_small], [W["r_rstd"]], scale=1.0 / dim, bias=self.eps_t[:])
                self.act(rstd[:, :T], rstd[:, :T], AF.Exp, [W["r_rstd"]], [W["r_rstd"]], scale=-0.5)
                for j in range(nk):
                    self.stt(outn[:, j, :T], outf[:, j, :T], self.cs(gname, j), rstd[:, :T], ALU.mult, ALU.mult,
                             [r_cq, self.r_cst, W["r_rstd"]], [r_cq])

            def q_side(h, T, t0):
                hk = lambda k: h[:, k, :T]
                S.dma("sp", c64[:, :T], I["c64_own"][:, t0:t0 + T], writes=[r_tab], semkey="tab")
                S.dma("sp", s64[:, :T], I["s64_own"][:, t0:t0 + T], writes=[r_tab], semkey="tab")
                S.dma("sp", c96[0:96, :T], I["c96_own"][:, t0:t0 + T], writes=[r_tab], semkey="tab")
                S.dma("sp", s96[0:96, :T], I["s96_own"][:, t0:t0 + T], writes=[r_tab], semkey="tab")
                for hd in range(4):
                    pb1, pb2 = self.bank(), self.bank()
                    self.proj(pb1, 128, T, lambda k: wq[:, k, hd * 128:(hd + 1) * 128], hk, 8, allw + r_hT)
                    self.proj(pb2, 128, T, lambda k: wq[:, k, 512 + hd * 128:512 + (hd + 1) * 128], hk, 8, allw + r_hT)
                    rope_evac(pb1, pb2, 128, T, c64, s64, [(self.qT_a[hd, :, t0:t0 + T], 0, 128)])
                pbs = [self.bank(), self.bank()]
                for j in range(2):
                    self.proj(pbs[j], 128, T, lambda k: wq[:, k, 1024 + j * 128:1024 + (j + 1) * 128], hk, 8, allw + r_hT)
                small_norm(pbs, 2, T, 256, "gcq", cqf, cqn)
                for hd in range(8):
                    pb1, pb2 = self.bank(), self.bank()
                    self.proj(pb1, 96, T, lambda k: wuq[:, k, hd * 96:(hd + 1) * 96], lambda k: cqn[:, k, :T], 2, allw + [r_cq])
                    self.proj(pb2, 96, T, lambda k: wuq[:, k, 768 + hd * 96:768 + (hd + 1) * 96], lambda k: cqn[:, k, :T], 2, allw + [r_cq])
                    rope_evac(pb1, pb2, 96, T, c96, s96, [(self.qT_b[hd, 0:96, t0:t0 + T], 0, 96)])

            def k_side(h, T, k0):
                hk = lambda k: h[:, k, :T]
                S.dma("sp", c64[:, :T], I["c64_all"][:, k0:k0 + T], writes=[r_tab], semkey="tab")
                S.dma("sp", s64[:, :T], I["s64_all"][:, k0:k0 + T], writes=[r_tab], semkey="tab")
                S.dma("sp", c96[0:96, :T], I["c96_all"][:, k0:k0 + T], writes=[r_tab], semkey="tab")
                S.dma("sp", s96[0:96, :T], I["s96_all"][:, k0:k0 + T], writes=[r_tab], semkey="tab")
                for hd in range(4):
                    pb1, pb2 = self.bank(), self.bank()
                    self.proj(pb1, 128, T, lambda k: wk[:, k, hd * 128:(hd + 1) * 128], hk, 8, allw + r_hT)
                    self.proj(pb2, 128, T, lambda k: wk[:, k, 512 + hd * 128:512 + (hd + 1) * 128], hk, 8, allw + r_hT)
                    rope_evac(pb1, pb2, 128, T, c64, s64, [(self.kT_a[hd, :, k0:k0 + T], 0, 128)])
                pb1, pb2 = self.bank(), self.bank()
                self.proj(pb1, 96, T, lambda k: wk[:, k, 1152:1248], hk, 8, allw + r_hT)
                self.proj(pb2, 96, T, lambda k: wk[:, k, 1248:1344], hk, 8, allw + r_hT)
                rope_evac(pb1, pb2, 96, T, c96, s96, [(self.kT_b[hd, 64:96, k0:k0 + T], 64, 96) for hd in range(8)], p0=64)
                pbc = self.bank()
                self.proj(pbc, 128, T, lambda k: wk[:, k, 1024:1152], hk, 8, allw + r_hT)
                small_norm([pbc], 1, T, 128, "gckv", cqf, cqn)
                for hp in range(4):
                    pb = self.bank()
                    self.proj(pb, 128, T, lambda k: wkn[:, hp * 128:(hp + 1) * 128], lambda k: cqn[:, 0, :T], 1, allw + [r_cq])
                    o = cnt["o"] % 3
                    cnt["o"] += 1
                    self.copy(ost[o][:, :T], self.ps[:, pb, :T], [self.r_ps[pb]], [r_ost[o]])
                    S.dma("sp", self.kT_b[2 * hp, 0:64, k0:k0 + T], ost[o][0:64, :T], reads=[r_ost[o]], semkey="ost%d" % o)
                    S.dma("sp", self.kT_b[2 * hp + 1, 0:64, k0:k0 + T], ost[o][64:128, :T], reads=[r_ost[o]], semkey="ost%d" % o)
                for tb in range(T // 128):
                    kb = k0 // 128 + tb
                    pb = self.bank()
                    for k in range(8):
                        self.mm(self.ps[:, pb, :], h[:, k, tb * 128:(tb + 1) * 128], wv[:, k, :], k == 0, k == 7, allw + r_hT, [self.r_ps[pb]])
                    vi = cnt["v"] % 2
                    cnt["v"] += 1
                    self.copy(vst[vi][:], self.ps[:, pb, :], [self.r_ps[pb]], [r_vst[vi]], eng="act")
                    S.dma("sp", self.v_a[:, :, kb, :].rearrange("h p d -> p h d"), vst[vi][:].rearrange("p (h d) -> p h d", h=4),
                          reads=[r_vst[vi]], semkey="vst%d" % vi)
                    pb = self.bank()
                    self.mm(self.ps[:, pb, :], cqn[:, 0, tb * 128:(tb + 1) * 128], wvb[:], True, True, allw + [r_cq], [self.r_ps[pb]])
                    vi = cnt["v"] % 2
                    cnt["v"] += 1
                    self.copy(vst[vi][:], self.ps[:, pb, :], [self.r_ps[pb]], [r_vst[vi]], eng="act")
                    S.dma("sp", self.v_b[:, :, kb, :].rearrange("h p d -> p h d"), vst[vi][:].rearrange("p (h d) -> p h d", h=8),
                          reads=[r_vst[vi]], semkey="vst%d" % vi)

            for ci, (t0, T, v) in enumerate(OWN_CHUNKS):
                hi = cnt["h"] % 2
                cnt["h"] += 1
                self.norm_chunk(lambda k: self.xT[:, k, t0:t0 + T], self.r_x[ci], T, A1, (ada, 0), v,
                                lambda k: hT[hi][:, k, :T], r_hT[hi], W)
                q_side(hT[hi], T, t0)
                if v == 1:
                    k_side(hT[hi], T, SEQ)
            xall = self.xT_all.rearrange("(k p) t -> p k t", p=128)
            for c in range(8):
                S.dma("sp", xs[:], xall[:, :, c * 512:(c + 1) * 512], writes=[r_xs], semkey="xs")
                hi = cnt["h"] % 2
                cnt["h"] += 1
                self.norm_chunk(lambda k: xs[:, k, :], r_xs, 512, A1, (ada, 0), 0, lambda k: hT[hi][:, k, :], r_hT[hi], W)
                k_side(hT[hi], 512, c * 512)

        S.barrier()
        self.areset()
        with ExitStack() as st:
            attnT = self.sb(st, "attnT", [128, 8, NOWN], BF16)
            a_after_attn = self.a_lo
            r_attn = S.res("attnT")
            self.attention_l0(st, attnT, r_attn, neglam, gsub8)
            S.barrier()
            self.areset(lo=a_after_attn)
            hF = self.aalloc("hF", [128, 8, NOWN], BF16, top=True)
            self.hF_off = self.a_hi
            r_hF = [S.res("hF%d" % i) for i in range(5)]
            with ExitStack() as st2:
                W = self.alloc_work(st2)
                wo = self.sb(st2, "wout", [128, 8, D], BF16)
                r_wo = S.res("wout")
                for k in range(8):
                    S.dma("pool", wo[:, k, :], I["wout0"][k * 128:(k + 1) * 128, :], writes=[r_wo], semkey="wout")
                for ci, (t0, T, v) in enumerate(OWN_CHUNKS):
                    for m in range(8):
                        pb = self.bank()
                        self.proj(pb, 128, T, lambda k: wo[:, k, m * 128:(m + 1) * 128], lambda k: attnT[:, k, t0:t0 + T], 8, [r_wo, r_attn])
                        self.stt(self.xT[:, m, t0:t0 + T], self.ps[:, pb, :T], ada[:, 16 + m, v:v + 1], self.xT[:, m, t0:t0 + T],
                                 ALU.mult, ALU.add, [self.r_ps[pb], self.r_ada, self.r_x[ci]], [self.r_x[ci]])
                    self.norm_chunk(lambda k: self.xT[:, k, t0:t0 + T], self.r_x[ci], T, A2, (ada, 24), v,
                                    lambda k: hF[:, k, t0:t0 + T], r_hF[ci], W)
            self.ffn(st, hF, r_hF, I["w1_0"], I["w3_0"], I["w2_0"], ada, OWN_CHUNKS)

    def ffn(self, st_outer, hF, r_hF, w1d, w3d, w2d, ada, chunks):
        S = self.S
        S.barrier()
        self.areset(hi=self.hF_off)
        passes = [(0, 6), (6, 6), (12, 5), (17, 5)]
        w1v = w1d.rearrange("(k p) c -> p k c", p=128)
        w3v = w3d.rearrange("(k p) c -> p k c", p=128)
        w2v = w2d.rearrange("(j p) c -> p j c", p=128)
        with ExitStack() as st:
            w1 = [self.sb(st, "w1_%d" % i, [128, 8, 768], BF16) for i in range(2)]
            w3 = [self.sb(st, "w3_%d" % i, [128, 8, 768], BF16) for i in range(2)]
            w2 = [self.sb(st, "w2_%d" % i, [128, 6, D], BF16) for i in range(2)]
            r_wf = [S.res("wf%d" % i) for i in range(2)]
            gT = [self.sb(st, "gT%d" % i, [128, 6, 512], BF16) for i in range(2)]
            r_g = [S.res("gT%d" % i) for i in range(2)]
            sil = [self.sb(st, "sil%d" % i, [128, 512], BF16) for i in range(2)]
            r_sil = [S.res("sil%d" % i) for i in range(2)]
            gi = 0
            si = 0
            for pi, (j0, nj) in enumerate(passes):
                b = pi % 2
                for k in range(8):
                    S.dma("pool", w1[b][:, k, 0:nj * 128], w1v[:, k, j0 * 128:(j0 + nj) * 128], writes=[r_wf[b]], semkey="wf%d" % b)
                    S.dma("pool", w3[b][:, k, 0:nj * 128], w3v[:, k, j0 * 128:(j0 + nj) * 128], writes=[r_wf[b]], semkey="wf%d" % b)
                for jj in range(nj):
                    S.dma("pool", w2[b][:, jj, :], w2v[:, j0 + jj, :], writes=[r_wf[b]], semkey="wf%d" % b)
                for ci, (t0, T, v) in enumerate(chunks):
                    g = gT[gi % 2]
                    rg = r_g[gi % 2]
                    gi += 1
                    for jj in range(nj):
                        pb1, pb3 = self.bank(), self.bank()
                        self.proj(pb1, 128, T, lambda k: w1[b][:, k, jj * 128:(jj + 1) * 128], lambda k: hF[:, k, t0:t0 + T], 8, [r_wf[b], r_hF[ci]])
                        self.proj(pb3, 128, T, lambda k: w3[b][:, k, jj * 128:(jj + 1) * 128], lambda k: hF[:, k, t0:t0 + T], 8, [r_wf[b], r_hF[ci]])
                        s_ = sil[si % 2]
                        rs = r_sil[si % 2]
                        si += 1
                        self.act(s_[:, :T], self.ps[:, pb1, :T], AF.Silu, [self.r_ps[pb1]], [rs])
                        self.tt(g[:, jj, :T], self.ps[:, pb3, :T], s_[:, :T], ALU.mult, [self.r_ps[pb3], rs], [rg])
                    for m in range(8):
                        pb = self.bank()
                        self.proj(pb, 128, T, lambda jj: w2[b][:, jj, m * 128:(m + 1) * 128], lambda jj: g[:, jj, :T], nj, [r_wf[b], rg])
                        self.stt(self.xT[:, m, t0:t0 + T], self.ps[:, pb, :T], ada[:, 40 + m, v:v + 1], self.xT[:, m, t0:t0 + T],
                                 ALU.mult, ALU.add, [self.r_ps[pb], self.r_ada, self.r_x[ci]], [self.r_x[ci]])

    def attn_stream(self, steps, Pt, r_P, st_state):
        S = self.S
        groups = []
        i = 0
        while i < len(steps):
            if i + 1 < len(steps) and not steps[i]["last"]:
                groups.append([steps[i], steps[i + 1]])
                i += 2
            else:
                groups.append([steps[i]])
                i += 1
        n = len(groups)
        sbanks = [(0, 1), (2, 3)]

        def emit_s(gi):
            bk = sbanks[gi % 2]
            for j, s in enumerate(groups[gi]):
                s["sb"] = bk[j]
                self.mm(self.ps[:, bk[j], :s["T"]], s["kT"], s["qT"], True, True, s["reads"], [self.r_ps[bk[j]]])
        emit_s(0)
        for gi in range(n):
            g = groups[gi]
            T = g[0]["T"]
            ng = len(g)
            p = Pt[gi % len(Pt)]
            rp = r_P[gi % len(Pt)]
            b0 = g[0]["sb"]
            for j in range(ng):
                self.act(p[:, j, :T], self.ps[:, b0 + j, :T], AF.Exp, [self.r_ps[b0 + j]], [rp], scale=g[0]["scale"])
            if gi + 1 < n:
                emit_s(gi + 1)
            for j, s in enumerate(g):
                self.mm(self.ps[0:s["M"], s["ob"], :T], s["v"], p[:, j, :T], s["first"], s["last"], s["reads"] + [rp], [self.r_ps[s["ob"]]])
            for j, s in enumerate(g):
                if s["zb"] is not None:
                    self.mm(self.ps[:, s["zb"], :T], self.ones_bf[:], p[:, j, :T], s["first"], s["last"], [rp, self.r_small], [self.r_ps[s["zb"]]])
            if g[-1]["last"] and g[-1]["fin"] is not None:
                g[-1]["fin"]()

    def attention_l0(self, st, attnT, r_attn, neglam, gsub8):
        S = self.S
        Qz = [[self.sb(st, "Qz%d_%d" % (i, s_), [128, NOWN], BF16) for s_ in range(2)] for i in range(2)]
        Kb = [self.sb(st, "Kb%d" % i, [128, NKEY], BF16) for i in range(2)]
        Vb = [self.sb(st, "Vb%d" % i, [128, NKB, 192], BF16) for i in range(2)]
        r_set = [S.res("set%d" % i) for i in range(2)]
        Pt = [self.sb(st, "Pt%d" % i, [128, 2, 512], BF16) for i in range(3)]
        r_P = [S.res("Pt%d" % i) for i in range(3)]
        for i in range(2):
            for s_ in range(2):
                self.memset(Qz[i][s_][:], 0.0, [r_set[i]])
        W = self.alloc_work(st)
        o1n = self.sb(st, "o1n", [128, 512], F32)
        comb = self.sb(st, "comb", [128, 512], F32)
        rz = [self.sb(st, "rz%d" % i, [128, 512], F32) for i in range(2)]
        osb = [self.sb(st, "osb%d" % i, [128, 512], F32) for i in range(2)]
        r_fin = [S.res("fin%d" % i) for i in range(2)]
        r_o1n = S.res("o1n")
        r_comb = S.res("comb")
        heads = [("a", h) for h in range(4)] + [("b", h) for h in range(8)]

        def load(idx):
            kind, h = heads[idx]
            b = idx % 2
            if kind == "a":
                S.dma("sp", Qz[b][0][0:64, :], self.qT_a[h, 0:64, :], writes=[r_set[b]], semkey="set%d" % b)
                S.dma("sp", Qz[b][1][64:128, :], self.qT_a[h, 64:128, :], writes=[r_set[b]], semkey="set%d" % b)
                S.dma("sp", Kb[b][:], self.kT_a[h], writes=[r_set[b]], semkey="set%d" % b)
                S.dma("sp", Vb[b][:, :, 0:128], self.v_a[h], writes=[r_set[b]], semkey="set%d" % b)
            else:
                if h < 2:
                    self.memset(Vb[b][:, :, 0:64], 1.0, [r_set[b]])
                    self.memset(Vb[b][:, :, 128:192], 1.0, [r_set[b]])
                S.dma("sp", Qz[b][0][0:96, :], self.qT_b[h, 0:96, :], writes=[r_set[b]], semkey="set%d" % b)
                S.dma("sp", Kb[b][0:96, :], self.kT_b[h, 0:96, :], writes=[r_set[b]], semkey="set%d" % b)
                S.dma("sp", Vb[b][:, :, 64:128], self.v_b[h], writes=[r_set[b]], semkey="set%d" % b)

        accs = [(4, 6), (5, 7)]
        state = {"acc": 0, "fin": 0}
        load(0)
        for idx, (kind, h) in enumerate(heads):
            b = idx % 2
            if idx + 1 < len(heads):
                load(idx + 1)
            steps = []
            if kind == "a":
                for (t0, T, v) in OWN_CHUNKS:
                    kbs = list(range(NKB)) if v == 0 else [32, 33]
                    for s_ in range(2):
                        ob, zb = accs[state["acc"] % 2]
                        state["acc"] += 1
                        for n_, kb in enumerate(kbs):
                            step = dict(kT=Kb[b][:, kb * 128:(kb + 1) * 128], qT=Qz[b][s_][:, t0:t0 + T],
                                        T=T, scale=0.125, v=Vb[b][:, kb, 0:128], M=128, ob=ob, zb=zb, first=(n_ == 0),
                                        last=(n_ == len(kbs) - 1), reads=[r_set[b]], fin=None)
                            steps.append(step)

                        def fin(ob=ob, zb=zb, s_=s_, t0=t0, T=T, h=h):
                            fi = state["fin"] % 2
                            state["fin"] += 1
                            S.op("dve", lambda e: e.reciprocal(out=rz[fi][:, :T], in_=self.ps[:, zb, :T]), [self.r_ps[zb]], [r_fin[fi]])
                            if s_ == 0:
                                self.tt(o1n[:, :T], self.ps[:, ob, :T], rz[fi][:, :T], ALU.mult, [self.r_ps[ob], r_fin[fi]], [r_o1n])
                            else:
                                self.tt(osb[fi][:, :T], self.ps[:, ob, :T], rz[fi][:, :T], ALU.mult, [self.r_ps[ob], r_fin[fi]], [r_fin[fi]])
                                self.stt(comb[:, :T], osb[fi][:, :T], neglam, o1n[:, :T], ALU.mult, ALU.add,
                                         [r_fin[fi], r_o1n, self.r_small], [r_comb])
                                sq = W["sq"][0]
                                self.tt(sq[:, :T], comb[:, :T], comb[:, :T], ALU.mult, [r_comb], [W["r_sq"][0]])
                                self.mm(self.ps[:, zb, :T], self.ones_bf[:], sq[:, :T], True, True, [W["r_sq"][0], self.r_small], [self.r_ps[zb]])
                                rstd = W["rstd"]
                                self.act(rstd[:, :T], self.ps[:, zb, :T], AF.Ln, [self.r_ps[zb], self.r_small], [W["r_rstd"]], scale=1.0 / 128, bias=self.eps_t[:])
                                self.act(rstd[:, :T], rstd[:, :T], AF.Exp, [W["r_rstd"]], [W["r_rstd"]], scale=-0.5)
                                self.stt(attnT[:, h, t0:t0 + T], comb[:, :T], gsub8, rstd[:, :T], ALU.mult, ALU.mult,
                                         [r_comb, W["r_rstd"], self.r_small], [r_attn])
                        steps[-1]["fin"] = fin
            else:
                po, pz = (0, 64) if h % 2 == 0 else (64, 0)
                vsl = (64, 192) if h % 2 == 0 else (0, 128)
                for (t0, T, v) in OWN_CHUNKS:
                    kbs = list(range(NKB)) if v == 0 else [32, 33]
                    ob, _ = accs[state["acc"] % 2]
                    state["acc"] += 1
                    for n_, kb in enumerate(kbs):
                        step = dict(kT=Kb[b][0:96, kb * 128:(kb + 1) * 128], qT=Qz[b][0][0:96, t0:t0 + T], T=T, scale=96 ** -0.5,
                                    v=Vb[b][:, kb, vsl[0]:vsl[1]], M=128, ob=ob, zb=None, first=(n_ == 0), last=(n_ == len(kbs) - 1),
                                    reads=[r_set[b]], fin=None)
                        steps.append(step)

                    def fin(ob=ob, t0=t0, T=T, h=h, po=po, pz=pz):
                        fi = state["fin"] % 2
                        state["fin"] += 1
                        self.copy(osb[fi][po:po + 64, :T], self.ps[po:po + 64, ob, :T], [self.r_ps[ob]], [r_fin[fi]])
                        S.op("dve", lambda e: e.reciprocal(out=self.ps[pz:pz + 64, ob, :T], in_=self.ps[pz:pz + 64, ob, :T]),
                             [self.r_ps[ob]], [self.r_ps[ob]])
                        self.tt(attnT[po:po + 64, 4 + h // 2, t0:t0 + T], self.ps[pz:pz + 64, ob, :T], osb[fi][po:po + 64, :T], ALU.mult,
                                [self.r_ps[ob], r_fin[fi]], [r_attn])
                    steps[-1]["fin"] = fin
            self.attn_stream(steps, Pt, r_P, state)

    def layer1(self):
        nc, S, I = self.nc, self.S, self.I
        ada, A1, A2 = self.ada[1], self.A1[1], self.A2[1]
        fused = self.mode == "fused"
        LAT = OWN_CHUNKS[:4]
        S.barrier()
        self.areset()
        with ExitStack() as st:
            W = self.alloc_work(st)
            bd = self.sb(st, "bd_ones", [128, 128], BF16)
            r_bd = S.res("bd")
            self.memset(bd[:], 0.0, [r_bd])
            self.memset(bd[0:64, 0:64], 1.0, [r_bd])
            self.memset(bd[64:128, 64:128], 1.0, [r_bd])
            wq = self.sb(st, "wq1", [128, 8, 1536], BF16)
            wk = self.sb(st, "wk1", [128, 8, 768], BF16)
            wv = self.sb(st, "wv1", [128, 8, 640], BF16)
            r_w = S.res("wA1")
            for k in range(8):
                S.dma("pool", wq[:, k, :], I["wq1"][k * 128:(k + 1) * 128, :], writes=[r_w], semkey="w1a")
                S.dma("pool", wk[:, k, :], I["wk1"][k * 128:(k + 1) * 128, :], writes=[r_w], semkey="w1a")
                S.dma("pool", wv[:, k, :], I["wv1"][k * 128:(k + 1) * 128, :], writes=[r_w], semkey="w1a")
            xs = None if fused else self.sb(st, "xstage", [128, 8, 512], F32)
            r_xs = S.res("xs")
            hT = [self.sb(st, "hT%d" % i, [128, 8, 512], BF16) for i in range(2)]
            r_hT = [S.res("hT%d" % i) for i in range(2)]
            c64 = self.sb(st, "c64", [128, 512], F32)
            s64 = self.sb(st, "s64", [128, 512], F32)
            r_tab = S.res("tab")
            t1 = [self.sb(st, "t1_%d" % i, [128, 512], F32) for i in range(2)]
            t2 = [self.sb(st, "t2_%d" % i, [128, 512], F32) for i in range(2)]
            r_t1 = [S.res("t1_%d" % i) for i in range(2)]
            r_t2 = [S.res("t2_%d" % i) for i in range(2)]
            ost = [self.sb(st, "ost%d" % i, [128, 512], BF16) for i in range(4)]
            r_ost = [S.res("ost%d" % i) for i in range(4)]
            vst = [self.sb(st, "vst%d" % i, [128, 640], BF16) for i in range(2)]
            r_vst = [S.res("vst%d" % i) for i in range(2)]
            cnt = {"o": 0, "t": 0, "v": 0, "h": 0}
            allw = [r_w]

            def load_tabs(T, which, t0):
                S.dma("sp", c64[:, :T], I["c64_" + which][:, t0:t0 + T], writes=[r_tab], semkey="tab")
                S.dma("sp", s64[:, :T], I["s64_" + which][:, t0:t0 + T], writes=[r_tab], semkey="tab")

            def normrope(pb1, pb2, T, gname, dsts):
                i = cnt["t"] % 2
                cnt["t"] += 1
                o = cnt["o"] % 4
                cnt["o"] += 1
                pbs = self.bank()
                sq = W["sq"][0]
                self.act(sq[:, :T], self.ps[:, pb1, :T], AF.Square, [self.r_ps[pb1]], [W["r_sq"][0]])
                self.mm(self.ps[:, pbs, :T], bd[:], sq[:, :T], True, True, [W["r_sq"][0], r_bd], [self.r_ps[pbs]])
                rstd = W["rstd"]
                self.act(rstd[:, :T], self.ps[:, pbs, :T], AF.Ln, [self.r_ps[pbs], self.r_small], [W["r_rstd"]], scale=1.0 / 64, bias=self.eps_t[:])
                self.act(rstd[:, :T], rstd[:, :T], AF.Exp, [W["r_rstd"]], [W["r_rstd"]], scale=-0.5)
                self.stt(t1[i][:, :T], self.ps[:, pb1, :T], self.cs(gname), rstd[:, :T], ALU.mult, ALU.mult,
                         [self.r_ps[pb1], self.r_cst, W["r_rstd"]], [r_t1[i]])
                self.stt(t2[i][:, :T], self.ps[:, pb2, :T], self.cs(gname + "s"), rstd[:, :T], ALU.mult, ALU.mult,
                         [self.r_ps[pb2], self.r_cst, W["r_rstd"]], [r_t2[i]])
                self.tt(t1[i][:, :T], t1[i][:, :T], c64[:, :T], ALU.mult, [r_t1[i], r_tab], [r_t1[i]])
                self.tt(t2[i][:, :T], t2[i][:, :T], s64[:, :T], ALU.mult, [r_t2[i], r_tab], [r_t2[i]], eng="pool")
                self.tt(ost[o][:, :T], t1[i][:, :T], t2[i][:, :T], ALU.add, [r_t1[i], r_t2[i]], [r_ost[o]])
                for dap in dsts:
                    S.dma("sp", dap, ost[o][:, :T], reads=[r_ost[o]], semkey="ost%d" % o)

            def plain_evac(pb, T, dsts, c0=0):
                o = cnt["o"] % 4
                cnt["o"] += 1
                self.copy(ost[o][:, :T], self.ps[:, pb, c0:c0 + T], [self.r_ps[pb]], [r_ost[o]], eng="act")
                for (dap, a, b_) in dsts:
                    S.dma("sp", dap, ost[o][:, a:b_], reads=[r_ost[o]], semkey="ost%d" % o)

            def kd_vd(h, c0, T, L0, halo=None):
                xv = None
                if halo is not None:
                    xv = self.xsend[384:512, :].rearrange("p (h b d) -> p h b d", h=8, b=4)
                for j in range(4):
                    pb = self.bank()
                    self.proj(pb, 128, T, lambda k: wk[:, k, 256 + j * 128:256 + (j + 1) * 128], lambda k: h[:, k, c0:c0 + T], 8, allw + r_hT)
                    dsts = [(self.kT_d[j, :, L0 * 128:L0 * 128 + T], 0, T)]
                    if halo is not None:
                        dsts.append((self.xsend[256:384, j * 512 + halo[1] * 128:j * 512 + halo[1] * 128 + 256], halo[0], halo[0] + 256))
                    plain_evac(pb, T, dsts)
                for tb in range(T // 128):
                    pb = self.bank()
                    for k in range(8):
                        self.mm(self.ps[:, pb, :], h[:, k, c0 + tb * 128:c0 + (tb + 1) * 128], wv[:, k, 128:640], k == 0, k == 7, allw + r_hT, [self.r_ps[pb]])
                    vi = cnt["v"] % 2
                    cnt["v"] += 1
                    self.copy(vst[vi][:, 0:512], self.ps[:, pb, :], [self.r_ps[pb]], [r_vst[vi]], eng="act")
                    S.dma("sp", self.v_d[:, :, L0 + tb, :].rearrange("h p d -> p h d"), vst[vi][:, 0:512].rearrange("p (h d) -> p h d", h=8),
                          reads=[r_vst[vi]], semkey="vst%d" % vi)
                    if halo is not None and halo[0] // 128 <= tb < halo[0] // 128 + 2:
                        S.dma("sp", xv[:, :, halo[1] + tb - halo[0] // 128, :], vst[vi][:, 0:512].rearrange("p (h d) -> p h d", h=8),
                              reads=[r_vst[vi]], semkey="vst%d" % vi)

            def kc_vc(h, T, k0, own_t0=None):
                hk = lambda k: h[:, k, :T]
                if own_t0 is None:
                    load_tabs(T, "all", k0)
                    kdst = self.kT_c[:, k0:k0 + T]
                else:
                    load_tabs(T, "own", own_t0)
                    kdst = self.xsend[0:128, own_t0:own_t0 + T]
                    xvc = self.xsend[128:256, :].rearrange("p (g kb d) -> p g kb d", g=2, kb=16)
                pb1, pb2 = self.bank(), self.bank()
                self.proj(pb1, 128, T, lambda k: wk[:, k, 0:128], hk, 8, allw + r_hT)
                self.proj(pb2, 128, T, lambda k: wk[:, k, 128:256], hk, 8, allw + r_hT)
                normrope(pb1, pb2, T, "gkc", [kdst])
                for tb in range(T // 128):
                    pb = self.bank()
                    for k in range(8):
                        self.mm(self.ps[:, pb, 0:128], h[:, k, tb * 128:(tb + 1) * 128], wv[:, k, 0:128], k == 0, k == 7, allw + r_hT, [self.r_ps[pb]])
                    vi = cnt["v"] % 2
                    cnt["v"] += 1
                    self.copy(vst[vi][:, 512:640], self.ps[:, pb, 0:128], [self.r_ps[pb]], [r_vst[vi]], eng="act")
                    if own_t0 is None:
                        vdst = self.v_c[:, :, k0 // 128 + tb, :].rearrange("g p d -> p g d")
                    else:
                        vdst = xvc[:, :, own_t0 // 128 + tb, :]
                    S.dma("sp", vdst, vst[vi][:, 512:640].rearrange("p (g d) -> p g d", g=2),
                          reads=[r_vst[vi]], semkey="vst%d" % vi)

            def q_side(h, T, t0):
                hk = lambda k: h[:, k, :T]
                load_tabs(T, "own", t0)
                for j in range(4):
                    pb1, pb2 = self.bank(), self.bank()
                    self.proj(pb1, 128, T, lambda k: wq[:, k, j * 128:(j + 1) * 128], hk, 8, allw + r_hT)
                    self.proj(pb2, 128, T, lambda k: wq[:, k, 512 + j * 128:512 + (j + 1) * 128], hk, 8, allw + r_hT)
                    normrope(pb1, pb2, T, "gqc", [self.qT_c[j, :, t0:t0 + T]])
                for j in range(4):
                    pb = self.bank()
                    self.proj(pb, 128, T, lambda k: wq[:, k, 1024 + j * 128:1024 + (j + 1) * 128], hk, 8, allw + r_hT)
                    plain_evac(pb, T, [(self.qT_d[j, :, t0:t0 + T], 0, T)])

            for ci, (t0, T, v) in enumerate(OWN_CHUNKS):
                hi = cnt["h"] % 2
                cnt["h"] += 1
                self.norm_chunk(lambda k: self.xT[:, k, t0:t0 + T], self.r_x[ci], T, A1, (ada, 0), v,
                                lambda k: hT[hi][:, k, :T], r_hT[hi], W)
                if v == 0:
                    q_side(hT[hi], T, t0)
                    halo = None
                    if fused and ci == 0:
                        halo = (0, 0)
                    if fused and ci == 3:
                        halo = (256, 2)
                    kd_vd(hT[hi], 0, T, 2 + t0 // 128, halo=halo)
                    if fused:
                        kc_vc(hT[hi], T, None, own_t0=t0)
                else:
                    kc_vc(hT[hi], T, SEQ)
                    kd_vd(hT[hi], 0, T, 20)
            xall = self.xT_all.rearrange("(k p) t -> p k t", p=128)
            import os as _os
            if fused and _os.environ.get("MK_SKIPX") != "1":
                S.barrier()
                xs_, xg_ = self.xsend, self.xgath
                r_g = S.res("xgath")
                r_stg = S.res("xstg")
                if _os.environ.get("MK_X", "") != "nocc":
                    S.op("pool", lambda e: e.collective_compute("AllGather", ALU.bypass, replica_groups=[list(range(8))],
                                                                ins=[xs_], outs=[xg_]), [], [r_g])
                r_st = [S.res("xstg%d" % i) for i in range(2)]
                n_ = 0
                for hp in range(2):
                    for rg in range(4):
                        sg = self.xstg[n_ % 2]
                        rs_ = r_st[n_ % 2]
                        sk = "xcp%d" % (n_ % 2)
                        n_ += 1

                        def ind(e, hp=hp, rg=rg, sg=sg):
                            return e.indirect_dma_start(out=sg[:], out_offset=None, in_=xg_,
                                                        in_offset=bass.IndirectOffsetOnAxis(ap=self.gidx[:, hp:hp + 1], axis=0),
                                                        element_offset=rg * 128 * 2048)
                        rec = Rec("pool", ind, dma=True, semkey="xind%d" % (n_ % 2))
                        rec.inc = True
                        S._add(rec, [r_g, self.r_gidx], [rs_])
                        if rg == 0:
                            S.dma("sp", self.kT_c[:, hp * 2048:(hp + 1) * 2048], sg[:], reads=[rs_], semkey=sk)
                        elif rg == 1:
                            for g_ in range(2):
                                S.dma("sp", self.v_c[g_, :, hp * 16:(hp + 1) * 16, :],
                                      sg[:, g_ * 1024:(g_ + 1) * 1024].rearrange("p (kb d) -> p kb d", kb=16), reads=[rs_], semkey=sk)
                        elif rg == 2:
                            for j in range(4):
                                if hp == 0:
                                    S.dma("sp", self.kT_d[j, :, 0:256], sg[:, j * 512 + 256:j * 512 + 512], reads=[rs_], semkey=sk)
                                else:
                                    S.dma("sp", self.kT_d[j, :, 18 * 128:20 * 128], sg[:, j * 512:j * 512 + 256], reads=[rs_], semkey=sk)
                        else:
                            for h_ in range(8):
                                if hp == 0:
                                    S.dma("sp", self.v_d[h_, :, 0:2, :], sg[:, h_ * 256 + 128:h_ * 256 + 256].rearrange("p (b d) -> p b d", b=2),
                                          reads=[rs_], semkey=sk)
                                else:
                                    S.dma("sp", self.v_d[h_, :, 18:20, :], sg[:, h_ * 256:h_ * 256 + 128].rearrange("p (b d) -> p b d", b=2),
                                          reads=[rs_], semkey=sk)
            for c in range(0 if fused else 8):
                hi = cnt["h"] % 2
                cnt["h"] += 1
                S.dma("sp", xs[:], xall[:, :, c * 512:(c + 1) * 512], writes=[r_xs], semkey="xs")
                self.norm_chunk(lambda k: xs[:, k, :], r_xs, 512, A1, (ada, 0), 0, lambda k: hT[hi][:, k, :], r_hT[hi], W)
                kc_vc(hT[hi], 512, c * 512)
                if c == 3:
                    kd_vd(hT[hi], 256, 256, 0)
                if c == 4:
                    kd_vd(hT[hi], 0, 256, 18)

        S.barrier()
        self.areset()
        attnT = self.aalloc("attnT1", [128, 8, TOWN], BF16)
        a_after_attn = self.a_lo
        r_attn = S.res("attnT1")
        self.attention_gqa(attnT, r_attn)
        S.barrier()
        self.areset(lo=a_after_attn)
        self.attention_na(attnT, r_attn)
        S.barrier()
        self.areset(lo=a_after_attn)
        hF = self.aalloc("hF1", [128, 8, TOWN], BF16, top=True)
        self.hF_off = self.a_hi
        r_hF = [S.res("hF%d" % i) for i in range(4)]
        W = self.alloc_work(None)
        wo = self.aalloc("wout1", [128, 8, D], BF16)
        r_wo = S.res("wout1")
        for k in range(8):
            S.dma("pool", wo[:, k, :], I["wout1"][k * 128:(k + 1) * 128, :], writes=[r_wo], semkey="wout")
        for ci, (t0, T, v) in enumerate(LAT):
            for m in range(8):
                pb = self.bank()
                self.proj(pb, 128, T, lambda k: wo[:, k, m * 128:(m + 1) * 128], lambda k: attnT[:, k, t0:t0 + T], 8, [r_wo, r_attn])
                self.stt(self.xT[:, m, t0:t0 + T], self.ps[:, pb, :T], ada[:, 16 + m, v:v + 1], self.xT[:, m, t0:t0 + T],
                         ALU.mult, ALU.add, [self.r_ps[pb], self.r_ada, self.r_x[ci]], [self.r_x[ci]])
            self.norm_chunk(lambda k: self.xT[:, k, t0:t0 + T], self.r_x[ci], T, A2, (ada, 24), v,
                            lambda k: hF[:, k, t0:t0 + T], r_hF[ci], W)
        self.ffn(None, hF, r_hF, I["w1_1"], I["w3_1"], I["w2_1"], ada, LAT)

    def attention_gqa(self, attnT, r_attn):
        S = self.S
        Qc = [[self.aalloc("Qc%d_%d" % (j, g_), [128, TOWN], BF16) for g_ in range(2)] for j in range(4)]
        Kc = self.aalloc("Kc", [128, NKEY], BF16)
        Vc = [self.aalloc("Vc%d" % g, [128, NKB, 192], BF16) for g in range(2)]
        r_in = S.res("gqa_in")
        Pt = [self.aalloc("Pt%d" % i, [128, 2, 512], BF16) for i in range(3)]
        r_P = [S.res("Pt%d" % i) for i in range(3)]
        osb = [self.aalloc("osb%d" % i, [128, 512], F32) for i in range(2)]
        r_fin = [S.res("fin%d" % i) for i in range(2)]
        for j in range(4):
            for g in range(2):
                self.memset(Qc[j][g][:], 0.0, [r_in])
        for g in range(2):
            self.memset(Vc[g][:, :, 0:64], 1.0, [r_in])
            self.memset(Vc[g][:, :, 128:192], 1.0, [r_in])
        S.dma("sp", Kc[:], self.kT_c, writes=[r_in], semkey="gin")
        for j in range(4):
            for g in range(2):
                S.dma("sp", Qc[j][g][g * 64:(g + 1) * 64, :], self.qT_c[j, g * 64:(g + 1) * 64, :], writes=[r_in], semkey="gin")
        for g in range(2):
            S.dma("sp", Vc[g][:, :, 64:128], self.v_c[g], writes=[r_in], semkey="gin")
        accs = [4, 5]
        state = {"acc": 0, "fin": 0}
        steps = []
        for g in range(2):
            for j in range(4):
                h = g * 4 + j
                po, pz = (0, 64) if h % 2 == 0 else (64, 0)
                vsl = (64, 192) if h % 2 == 0 else (0, 128)
                for (t0, T, v) in OWN_CHUNKS[:4]:
                    ob = accs[state["acc"] % 2]
                    state["acc"] += 1
                    for kb in range(NKB):
                        steps.append(dict(kT=Kc[:, kb * 128:(kb + 1) * 128], qT=Qc[j][g][:, t0:t0 + T],
                                          T=T, scale=0.125, v=Vc[g][:, kb, vsl[0]:vsl[1]], M=128, ob=ob, zb=None, first=(kb == 0),
                                          last=(kb == NKB - 1), reads=[r_in], fin=None))

                    def fin(ob=ob, t0=t0, T=T, h=h, po=po, pz=pz):
                        fi = state["fin"] % 2
                        state["fin"] += 1
                        self.copy(osb[fi][po:po + 64, :T], self.ps[po:po + 64, ob, :T], [self.r_ps[ob]], [r_fin[fi]])
                        S.op("dve", lambda e: e.reciprocal(out=self.ps[pz:pz + 64, ob, :T], in_=self.ps[pz:pz + 64, ob, :T]),
                             [self.r_ps[ob]], [self.r_ps[ob]])
                        self.tt(attnT[po:po + 64, h // 2, t0:t0 + T], self.ps[pz:pz + 64, ob, :T], osb[fi][po:po + 64, :T], ALU.mult,
                                [self.r_ps[ob], r_fin[fi]], [r_attn])
                    steps[-1]["fin"] = fin
        self.attn_stream(steps, Pt, r_P, state)

    def attention_na(self, attnT, r_attn):
        S = self.S
        r_pair = [S.res("napair%d" % i) for i in range(2)]
        r_head = [S.res("nahead%d" % i) for i in range(2)]
        Qd = [[self.aalloc("Qd%d_%d" % (i, e_), [128, TOWN], BF16) for e_ in range(2)] for i in range(2)]
        for i in range(2):
            for e_ in range(2):
                self.memset(Qd[i][e_][:], 0.0, [r_pair[i]])
        Kd = [self.aalloc("Kd%d" % i, [128, 22 * 128], BF16) for i in range(2)]
        Vd = [self.aalloc("Vd%d" % i, [128, 22, 192], BF16) for i in range(2)]
        tab = [self.aalloc("natab%d" % i, [128, 5, 768], F32) for i in range(2)]
        tmp = [self.aalloc("natmp%d" % i, [128, 768], F32) for i in range(2)]
        r_tmp = [S.res("natmp%d" % i) for i in range(2)]
        P = [self.aalloc("naP%d" % i, [128, 1024], BF16) for i in range(2)]
        r_P = [S.res("naP%d" % i) for i in range(2)]
        osb = [self.aalloc("osb%d" % i, [128, 512], F32) for i in range(2)]
        r_fin = [S.res("fin%d" % i) for i in range(2)]
        for b in range(2):
            self.memset(Vd[b][:, :, 0:64], 1.0, [r_head[b]])
            self.memset(Vd[b][:, :, 128:192], 1.0, [r_head[b]])
        nab = self.I["nab"]

        def load_pair(j):
            b = j % 2
            for e_ in range(2):
                S.dma("sp", Qd[b][e_][e_ * 64:(e_ + 1) * 64, :], self.qT_d[j, e_ * 64:(e_ + 1) * 64, :], writes=[r_pair[b]], semkey="napair%d" % b)
            S.dma("sp", Kd[b][:], self.kT_d[j], writes=[r_pair[b]], semkey="napair%d" % b)

        def load_head(h):
            b = h % 2
            S.dma("sp", Vd[b][:, :, 64:128], self.v_d[h], writes=[r_head[b]], semkey="nahead%d" % b)
            S.dma("sp", tab[b][:], nab[:, :, h * 768:(h + 1) * 768].rearrange("c p f -> p c f"), writes=[r_head[b]], semkey="nahead%d" % b)

        units = [(h, i) for h in range(8) for i in range(16)]
        CLS = {0: 1, 1: 2, 14: 3, 15: 4}
        sbanks = [(0, 1), (2, 3)]
        obanks = [4, 5]
        state = {"fin": 0}

        def emit_s(u):
            h, i = units[u]
            j, e = h // 2, h % 2
            bp = j % 2
            A, B = sbanks[u % 2]
            L0 = min(i, 14)
            q = Qd[bp][e][:, i * 128:(i + 1) * 128]
            for s_ in range(8):
                L = L0 + s_ if s_ < 6 else 20 + (s_ - 6)
                bk = A if s_ < 4 else B
                cc = (s_ % 4) * 128
                self.mm(self.ps[:, bk, cc:cc + 128], Kd[bp][:, L * 128:(L + 1) * 128], q, True, True,
                        [r_pair[bp]], [self.r_ps[bk]])

        load_pair(0)
        load_head(0)
        emit_s(0)
        for u, (h, i) in enumerate(units):
            j, e = h // 2, h % 2
            bp, bh = j % 2, h % 2
            if i == 0:
                if h + 1 < 8:
                    load_head(h + 1)
                    if e == 1:
                        load_pair(j + 1)
            A, B = sbanks[u % 2]
            cls = CLS.get(i, 0)
            L0 = min(i, 14)
            t = tmp[u % 2]
            p = P[u % 2]
            self.stt(t[:, 0:512], self.ps[:, A, :], 0.125, tab[bh][:, cls, 0:512], ALU.mult, ALU.add,
                     [self.r_ps[A], r_head[bh]], [r_tmp[u % 2]])
            self.stt(t[:, 512:768], self.ps[:, B, 0:256], 0.125, tab[bh][:, cls, 512:768], ALU.mult, ALU.add,
                     [self.r_ps[B], r_head[bh]], [r_tmp[u % 2]])
            self.act(p[:, 0:768], t[:, 0:768], AF.Exp, [r_tmp[u % 2]], [r_P[u % 2]])
            self.act(p[:, 768:1024], self.ps[:, B, 256:512], AF.Exp, [self.r_ps[B]], [r_P[u % 2]], scale=0.125)
            if u + 1 < len(units):
                emit_s(u + 1)
            ob = obanks[(u // 4) % 2]
            oc = (i % 4) * 128
            vsl = (64, 192) if e == 0 else (0, 128)
            for s_ in range(8):
                L = L0 + s_ if s_ < 6 else 20 + (s_ - 6)
                self.mm(self.ps[:, ob, oc:oc + 128], Vd[bh][:, L, vsl[0]:vsl[1]], p[:, s_ * 128:(s_ + 1) * 128], s_ == 0, s_ == 7,
                        [r_head[bh], r_P[u % 2]], [self.r_ps[ob]])
            if i % 4 == 3:
                po, pz = (0, 64) if e == 0 else (64, 0)
                t0 = (i - 3) * 128
                fi = state["fin"] % 2
                state["fin"] += 1
                self.copy(osb[fi][po:po + 64, :], self.ps[po:po + 64, ob, :], [self.r_ps[ob]], [r_fin[fi]])
                S.op("dve", lambda e_, ob=ob, pz=pz: e_.reciprocal(out=self.ps[pz:pz + 64, ob, :], in_=self.ps[pz:pz + 64, ob, :]),
                     [self.r_ps[ob]], [self.r_ps[ob]])
                self.tt(attnT[po:po + 64, 4 + h // 2, t0:t0 + 512], self.ps[pz:pz + 64, ob, :], osb[fi][po:po + 64, :], ALU.mult,
                        [self.r_ps[ob], r_fin[fi]], [r_attn])

    def final_norm_out(self):
        S = self.S
        S.barrier()
        self.areset()
        W = self.alloc_work(None)
        stage = [self.aalloc("ostage%d" % i, [128, 8, 512], F32) for i in range(2)]
        r_stage = [S.res("ostage%d" % i) for i in range(2)]
        fins = []
        gf = CL["gfinal"][0]
        for ci, (t0, T, v) in enumerate(OWN_CHUNKS[:4]):
            pb = self.bank()
            for k in range(8):
                sq = W["sq"][k % 2]
                self.act(sq[:, :T], self.xT[:, k, t0:t0 + T], AF.Square, [self.r_x[ci]], [W["r_sq"][k % 2]])
                self.mm(self.ps[:, pb, :T], self.ones_bf[:], sq[:, :T], k == 0, k == 7, [W["r_sq"][k % 2], self.r_small], [self.r_ps[pb]])
            rstd = W["rstd"]
            self.act(rstd[:, :T], self.ps[:, pb, :T], AF.Ln, [self.r_ps[pb], self.r_small], [W["r_rstd"]], scale=1.0 / D, bias=self.eps_t[:])
            self.act(rstd[:, :T], rstd[:, :T], AF.Exp, [W["r_rstd"]], [W["r_rstd"]], scale=-0.5)
            sg = stage[ci % 2]
            for k in range(8):
                self.stt(sg[:, k, :T], self.xT[:, k, t0:t0 + T], self.cst[:, gf + k:gf + k + 1], rstd[:, :T], ALU.mult, ALU.mult,
                         [self.r_x[ci], self.r_cst, W["r_rstd"]], [r_stage[ci % 2]])
            for k in range(8):
                fins.append(S.dma("sp", self.y_x[k * 128:(k + 1) * 128, t0:t0 + T], sg[:, k, :T], reads=[r_stage[ci % 2]],
                                  semkey="ystage%d" % (ci % 2)))
        return fins


def _rope_tables(rot_dim, pos):
    rows = (pos // GRID_W).astype(np.float32)
    cols = (pos % GRID_W).astype(np.float32)
    axis_dim = rot_dim // 2
    inv = (10000.0 ** (-np.arange(0, axis_dim, 2, dtype=np.float32) / axis_dim)).astype(np.float32)
    ang = np.concatenate([rows[:, None] * inv, cols[:, None] * inv], axis=-1).astype(np.float32)
    return np.cos(ang).astype(np.float32), np.sin(ang).astype(np.float32)


def _tables64(pos, nctx):
    c, s = _rope_tables(64, pos)
    C = np.concatenate([c, c, c, c], axis=1).T
    Sg = np.concatenate([-s, s, -s, s], axis=1).T
    C = np.concatenate([C, np.ones((128, nctx), np.float32)], axis=1)
    Sg = np.concatenate([Sg, np.zeros((128, nctx), np.float32)], axis=1)
    return np.ascontiguousarray(C, np.float32), np.ascontiguousarray(Sg, np.float32)


def _tables96(pos, nctx):
    c, s = _rope_tables(32, pos)
    T = len(pos)
    C = np.concatenate([np.ones((T, 64), np.float32), c, c], axis=1).T
    Sg = np.concatenate([np.zeros((T, 64), np.float32), -s, s], axis=1).T
    C = np.concatenate([C, np.ones((96, nctx), np.float32)], axis=1)
    Sg = np.concatenate([Sg, np.zeros((96, nctx), np.float32)], axis=1)
    return np.ascontiguousarray(C, np.float32), np.ascontiguousarray(Sg, np.float32)


def _fm(vec, k):
    return np.ascontiguousarray(np.asarray(vec, np.float32).reshape(k, 128).T)


def _consts(inp, b):
    cst = np.zeros((128, NCONST), np.float32)

    def put(name, arr):
        o, w = CL[name]
        cst[:, o:o + w] = np.asarray(arr, np.float32).reshape(128, w)
    cfm = np.stack([_fm(inp["c"][b], 8), _fm(inp["c_ctx"], 8)], axis=-1)
    put("cfm", cfm.reshape(128, 16))
    put("bmod0", _fm(inp["l0_b_mod"], 48))
    put("bmod1", _fm(inp["l1_b_mod"], 48))
    put("gattn0", _fm(inp["l0_g_attn"], 8))
    put("gffn0", _fm(inp["l0_g_ffn"], 8))
    put("gattn1", _fm(inp["l1_g_attn"], 8))
    put("gffn1", _fm(inp["l1_g_ffn"], 8))
    put("gfinal", _fm(inp["g_final"], 8))
    put("gsub", _fm(inp["l0_g_subln"], 1))
    put("gcq", _fm(inp["l0_g_cq"], 2))
    put("gckv", _fm(inp["l0_g_ckv"], 1))
    lam = np.concatenate([inp["l0_lam_q1"], inp["l0_lam_k1"], inp["l0_lam_q2"], inp["l0_lam_k2"]]).astype(np.float32)
    put("lamv", np.broadcast_to(lam[None, :], (128, 256)))
    gq = np.asarray(inp["l1_g_qc"], np.float32)
    gk = np.asarray(inp["l1_g_kc"], np.float32)
    sw = _swap_halves(np.arange(64), 64)
    put("gqc", np.tile(gq, 2)[:, None])
    put("gqcs", np.tile(gq[sw], 2)[:, None])
    put("gkc", np.tile(gk, 2)[:, None])
    put("gkcs", np.tile(gk[sw], 2)[:, None])
    return cst


def _l0_weights(inp):
    w_in = np.asarray(inp["l0_w_in"], np.float32)
    qa = np.arange(0, 512)
    ka = np.arange(512, 1024)
    va = np.arange(1024, 1536)
    cq = np.arange(1536, 1792)
    ckv = np.arange(1792, 1920)
    kr = np.arange(1920, 1952)
    wq_cols = np.concatenate([qa, _swap_halves(qa, 64), cq])
    wk_cols = np.concatenate([ka, _swap_halves(ka, 64), ckv, ckv[:64], kr, ckv[:64], _swap_halves(kr, 32)])
    w_uq = np.asarray(inp["l0_w_uq"], np.float32)
    uq = np.arange(768).reshape(8, 96)
    uqs = uq.copy()
    for h in range(8):
        uqs[h, 64:96] = _swap_halves(uq[h, 64:96], 32)
    w_ukv = np.asarray(inp["l0_w_ukv"], np.float32)
    kv = np.arange(1024).reshape(8, 128)
    return {
        "wq0": np.ascontiguousarray(w_in[:, wq_cols]),
        "wk0": np.ascontiguousarray(w_in[:, wk_cols]),
        "wv0": np.ascontiguousarray(w_in[:, va]),
        "wuq": np.ascontiguousarray(w_uq[:, np.concatenate([uq.reshape(-1), uqs.reshape(-1)])]),
        "wkn": np.ascontiguousarray(w_ukv[:, kv[:, :64].reshape(-1)]),
        "wvb": np.ascontiguousarray(w_ukv[:, kv[:, 64:].reshape(-1)]),
    }


def _core_inputs_l0(inp, core, shared):
    b, half = core // 2, core % 2
    x = np.asarray(inp["x"], np.float32)
    m = {}
    xb = x[b]
    m["xT_own"] = np.ascontiguousarray(xb[half * TOWN:(half + 1) * TOWN].T)
    m["xT_all"] = shared.setdefault(("xT_all", b), np.ascontiguousarray(xb.T))
    m["ctxT"] = shared.setdefault(("ctxT", b), np.ascontiguousarray(np.asarray(inp["ctx"], np.float32)[b].T))
    m["consts"] = _consts(inp, b)
    return m


_CACHE = {}


def _get_prog(mode):
    if mode not in _CACHE:
        p = Prog(mode)
        p.build()
        _CACHE[mode] = p
    return _CACHE[mode]


def run_l0(inp):
    prog = _get_prog("l0")
    shared = {}
    W = _l0_weights(inp)
    pos_all = np.arange(SEQ)
    c64a, s64a = _tables64(pos_all, NCTX)
    c96a, s96a = _tables96(pos_all, NCTX)
    maps = []
    for core in range(8):
        half = core % 2
        m = _core_inputs_l0(inp, core, shared)
        m.update(W)
        m["wmod0"] = np.asarray(inp["l0_w_mod"], np.float32)
        m["wout0"] = np.asarray(inp["l0_w_out"], np.float32)
        m["w1_0"] = np.asarray(inp["l0_w1"], np.float32)
        m["w3_0"] = np.asarray(inp["l0_w3"], np.float32)
        m["w2_0"] = np.asarray(inp["l0_w2"], np.float32)
        pos_own = np.arange(half * TOWN, (half + 1) * TOWN)
        m["c64_own"], m["s64_own"] = shared.setdefault(("t64", half), _tables64(pos_own, NCTX))
        m["c96_own"], m["s96_own"] = shared.setdefault(("t96", half), _tables96(pos_own, NCTX))
        m["c64_all"], m["s64_all"], m["c96_all"], m["s96_all"] = c64a, s64a, c96a, s96a
        maps.append(m)
    res = run_bass_kernel_spmd(prog.nc, maps, core_ids=list(range(8)))
    x1 = np.zeros((4, SEQ, D), np.float32)
    xc1 = np.zeros((4, NCTX, D), np.float32)
    for core in range(8):
        b, half = core // 2, core % 2
        x1[b, half * TOWN:(half + 1) * TOWN] = res.results[core]["y_x"].T
        xc1[b] = res.results[core]["y_c"].T
    return x1, xc1


def _l1_weights(inp):
    w_in = np.asarray(inp["l1_w_in"], np.float32)
    qc = np.arange(0, 512).reshape(8, 64)
    kc = np.arange(512, 640)
    vc = np.arange(640, 768)
    qd = np.arange(768, 1280)
    kd = np.arange(1280, 1792)
    vd = np.arange(1792, 2304)
    qct = np.concatenate([np.concatenate([qc[j], qc[j + 4]]) for j in range(4)])
    wq_cols = np.concatenate([qct, _swap_halves(qct, 64), qd])
    wk_cols = np.concatenate([kc, _swap_halves(kc, 64), kd])
    wv_cols = np.concatenate([vc, vd])
    return {"wq1": np.ascontiguousarray(w_in[:, wq_cols]), "wk1": np.ascontiguousarray(w_in[:, wk_cols]),
            "wv1": np.ascontiguousarray(w_in[:, wv_cols])}


def _na_tables(rpb, half):
    rpb = np.asarray(rpb, np.float32)
    NEG = -30000.0
    out = np.full((5, 128, 8, 6, 128), NEG, np.float32)
    for cls, i in [(0, 5), (1, 0), (2, 1), (3, 14), (4, 15)]:
        gi = 16 * half + i
        qpos = gi * 128 + np.arange(128)
        r, c = qpos // GRID_W, qpos % GRID_W
        rs = np.clip(r - 4, 0, 56)
        cs = np.clip(c - 8, 0, 48)
        L0 = min(i, 14)
        for s_ in range(6):
            L = L0 + s_
            if L < 2:
                gb = 14 + L
            elif L < 18:
                gb = 16 * half + L - 2
            else:
                gb = 16 + L - 18
            kpos = gb * 128 + np.arange(128)
            kr, kc = kpos // GRID_W, kpos % GRID_W
            inwin = ((kr[:, None] >= rs[None, :]) & (kr[:, None] < rs[None, :] + 8)
                     & (kc[:, None] >= cs[None, :]) & (kc[:, None] < cs[None, :] + 16))
            rel_r = np.clip(kr[:, None] - r[None, :] + 7, 0, 14)
            rel_c = np.clip(kc[:, None] - c[None, :] + 15, 0, 30)
            for h in range(8):
                out[cls, :, h, s_, :] = np.where(inwin, rpb[h][rel_r, rel_c], NEG)
    return np.ascontiguousarray(out.reshape(5, 128, 8 * 6 * 128))


def run_l1(inp, x1, xc1):
    prog = _get_prog("l1")
    shared = {}
    W = _l1_weights(inp)
    c64a, s64a = _tables64(np.arange(SEQ), NCTX)
    maps = []
    inp2 = dict(inp)
    inp2["x"] = x1
    inp2["ctx"] = xc1
    for core in range(8):
        half = core % 2
        m = _core_inputs_l0(inp2, core, shared)
        m.update(W)
        m["wmod1"] = np.asarray(inp["l1_w_mod"], np.float32)
        m["wout1"] = np.asarray(inp["l1_w_out"], np.float32)
        m["w1_1"] = np.asarray(inp["l1_w1"], np.float32)
        m["w3_1"] = np.asarray(inp["l1_w3"], np.float32)
        m["w2_1"] = np.asarray(inp["l1_w2"], np.float32)
        pos_own = np.arange(half * TOWN, (half + 1) * TOWN)
        m["c64_own"], m["s64_own"] = shared.setdefault(("t64", half), _tables64(pos_own, NCTX))
        m["c64_all"], m["s64_all"] = c64a, s64a
        m["nab"] = shared.setdefault(("nab", half), _na_tables(inp["l1_rpb"], half))
        maps.append(m)
    res = run_bass_kernel_spmd(prog.nc, maps, core_ids=list(range(8)))
    out = np.zeros((4, SEQ, D), np.float32)
    for core in range(8):
        b, half = core // 2, core % 2
        out[b, half * TOWN:(half + 1) * TOWN] = res.results[core]["y_x"].T
    return out


def run_fused(inp):
    prog = _get_prog("fused")
    shared = {}
    W0 = _l0_weights(inp)
    W1 = _l1_weights(inp)
    pos_all = np.arange(SEQ)
    c64a, s64a = _tables64(pos_all, NCTX)
    c96a, s96a = _tables96(pos_all, NCTX)
    maps = []
    for core in range(8):
        half = core % 2
        m = _core_inputs_l0(inp, core, shared)
        m.update(W0)
        m.update(W1)
        for l in range(2):
            m["wmod%d" % l] = np.asarray(inp["l%d_w_mod" % l], np.float32)
            m["wout%d" % l] = np.asarray(inp["l%d_w_out" % l], np.float32)
            m["w1_%d" % l] = np.asarray(inp["l%d_w1" % l], np.float32)
            m["w3_%d" % l] = np.asarray(inp["l%d_w3" % l], np.float32)
            m["w2_%d" % l] = np.asarray(inp["l%d_w2" % l], np.float32)
        pos_own = np.arange(half * TOWN, (half + 1) * TOWN)
        m["c64_own"], m["s64_own"] = shared.setdefault(("t64", half), _tables64(pos_own, NCTX))
        m["c96_own"], m["s96_own"] = shared.setdefault(("t96", half), _tables96(pos_own, NCTX))
        m["c64_all"], m["s64_all"], m["c96_all"], m["s96_all"] = c64a, s64a, c96a, s96a
        m["nab"] = shared.setdefault(("nab", half), _na_tables(inp["l1_rpb"], half))
        P = core // 2
        m["gidx"] = np.stack([(2 * P + hp) * 512 + np.arange(128) for hp in range(2)], axis=1).astype(np.uint32)
        maps.append(m)
    res = run_bass_kernel_spmd(prog.nc, maps, core_ids=list(range(8)))
    out = np.zeros((4, SEQ, D), np.float32)
    for core in range(8):
        b, half = core // 2, core % 2
        out[b, half * TOWN:(half + 1) * TOWN] = res.results[core]["y_x"].T
    return out


FUSED = False


def kernel(**inputs):
    inp = {k: np.asarray(v) for k, v in inputs.items()}
    if FUSED:
        return run_fused(inp)
    x1, xc1 = run_l0(inp)
    return run_l1(inp, x1, xc1)
```
